# Optimizing a Trainium2 kernel written in Bass

```python
import math
import jax, jax.numpy as jnp
from jax import lax
import numpy as np

D_MODEL = 1024
BATCH = 16
SEQ = 2048
DEPTH = 4
DEC_BATCH = 8
DEC_SEQ = 4096
PAST_LEN = 128

GRID_W = 64
GROUP_W = D_MODEL // 4
FN_GROUPS = 4
FN_CH = GROUP_W // FN_GROUPS
DA_HEADS = 4
DA_VDIM = GROUP_W // DA_HEADS
DA_QDIM = DA_VDIM // 2
ROPE_THETA = 10000.0
Q_BLOCK = 128
NA_HEADS = 4
NA_HDIM = GROUP_W // NA_HEADS
NA_WIN_R = 8
NA_WIN_C = 16
NA_QCOLS = 16
NA_KCOLS = 32
SSM_HEADS = 4
SSM_HDIM = GROUP_W // SSM_HEADS
SSM_GROUPS = 2
SSM_HPG = SSM_HEADS // SSM_GROUPS
SSM_STATE = 128
SSM_CONV = 5
SSM_CHUNK = 128
SSM_XBC = GROUP_W + 2 * SSM_GROUPS * SSM_STATE
SSM_COLS = GROUP_W + SSM_XBC + 2 * SSM_HEADS
OFF_FN = 0
OFF_DA = OFF_FN + GROUP_W
OFF_NA = OFF_DA + 3 * GROUP_W
OFF_SSM = OFF_NA + 3 * GROUP_W
D_IN = OFF_SSM + SSM_COLS
D_FF = ((8 * D_MODEL + 3 * 256 - 1) // (3 * 256)) * 256
EPS = 1e-6

kernel_name = 'hybrid_bidir_encoder_fourier_diffattn_natten_ssd'


def _rms(x):
    xf = x.astype(jnp.float32)
    return xf * lax.rsqrt(jnp.mean(xf * xf, axis=-1, keepdims=True) + EPS)


def rmsnorm(x, g):
    return (_rms(x) * g.astype(jnp.float32)).astype(x.dtype)


def rope(x):
    l, d = x.shape[1], x.shape[-1]
    half = d // 2
    pos = jnp.arange(l, dtype=jnp.float32)
    inv = jnp.power(ROPE_THETA, -jnp.arange(half, dtype=jnp.float32) * (2.0 / d))
    ang = pos[:, None] * inv[None, :]
    shape = (1, l) + (1,) * (x.ndim - 3) + (half,)
    cos = jnp.cos(ang).reshape(shape)
    sin = jnp.sin(ang).reshape(shape)
    xf = x.astype(jnp.float32)
    x1, x2 = xf[..., :half], xf[..., half:]
    return jnp.concatenate([x1 * cos - x2 * sin, x2 * cos + x1 * sin], axis=-1).astype(x.dtype)


def fourier_mixer(u, w):
    b, l, _ = u.shape
    uf = u.astype(jnp.float32).reshape(b, l, FN_GROUPS, FN_CH)
    f = jnp.fft.fft2(uf, axes=(1, 3), norm='ortho').real
    return f.reshape(b, l, GROUP_W).astype(u.dtype) @ w


def diff_attention(u, lam_vecs, subln_g, lambda_init):
    b, l, _ = u.shape
    q, k, v = jnp.split(u, 3, axis=-1)
    q = rope(q.reshape(b, l, DA_HEADS, 2, DA_QDIM)) * (DA_QDIM ** -0.5)
    k = rope(k.reshape(b, l, DA_HEADS, 2, DA_QDIM))
    v = v.reshape(b, l, DA_HEADS, DA_VDIM)
    lv = lam_vecs.astype(jnp.float32)
    lam = jnp.exp(jnp.sum(lv[0] * lv[1])) - jnp.exp(jnp.sum(lv[2] * lv[3])) + lambda_init
    nb = l // Q_BLOCK
    qb = jnp.moveaxis(q.reshape(b, nb, Q_BLOCK, DA_HEADS, 2, DA_QDIM), 1, 0)

    def block(qi):
        s = jnp.einsum('bqhtd,bshtd->bhtqs', qi, k).astype(jnp.float32)
        p = jax.nn.softmax(s, axis=-1)
        a = p[:, :, 0] - lam * p[:, :, 1]
        return jnp.einsum('bhqs,bshd->bqhd', a.astype(v.dtype), v)

    o = lax.map(block, qb)
    o = jnp.moveaxis(o, 0, 1).reshape(b, l, DA_HEADS, DA_VDIM)
    o = rmsnorm(o, subln_g) * (1.0 - lambda_init)
    return o.reshape(b, l, GROUP_W)


def neighbourhood_attention(u, rpb):
    b, l, _ = u.shape
    rows = l // GRID_W
    wr = min(NA_WIN_R, rows)
    ncb = GRID_W // NA_QCOLS
    q, k, v = jnp.split(u, 3, axis=-1)
    q = q.reshape(b, rows, ncb, NA_QCOLS, NA_HEADS, NA_HDIM)
    k = k.reshape(b, rows, GRID_W, NA_HEADS, NA_HDIM)
    v = v.reshape(b, rows, GRID_W, NA_HEADS, NA_HDIM)
    r = jnp.arange(rows)
    row_idx = jnp.clip(r - wr // 2, 0, rows - wr)[:, None] + jnp.arange(wr)[None, :]
    c0 = jnp.arange(ncb) * NA_QCOLS
    col_idx = jnp.clip(c0 - NA_WIN_C // 2, 0, GRID_W - NA_KCOLS)[:, None] + jnp.arange(NA_KCOLS)[None, :]
    ri = row_idx[:, None, :, None]
    ci = col_idx[None, :, None, :]
    kg = k[:, ri, ci]
    vg = v[:, ri, ci]
    s = jnp.einsum('brcqhd,brcijhd->brchqij', q, kg).astype(jnp.float32) * (NA_HDIM ** -0.5)
    qc = c0[:, None] + jnp.arange(NA_QCOLS)[None, :]
    sc = jnp.clip(qc - NA_WIN_C // 2, 0, GRID_W - NA_WIN_C)
    kc = col_idx[:, None, :]
    valid = (kc >= sc[:, :, None]) & (kc < sc[:, :, None] + NA_WIN_C)
    dr = row_idx - r[:, None] + (NA_WIN_R - 1)
    dc = jnp.clip(kc - qc[:, :, None] + (NA_WIN_C - 1), 0, 2 * NA_WIN_C - 2)
    bias = rpb[:, dr[:, None, None, :, None], dc[None, :, :, None, :]]
    bias = jnp.moveaxis(bias, 0, 2).astype(jnp.float32)
    s = jnp.where(valid[:, None, :, None, :], s + bias, -jnp.inf)
    p = jax.nn.softmax(s, axis=(-2, -1))
    o = jnp.einsum('brchqij,brcijhd->brcqhd', p.astype(vg.dtype), vg)
    return o.reshape(b, l, GROUP_W)


def dwconv(x, w, bias):
    c = x.shape[-1]
    y = lax.conv_general_dilated(x, w[:, None, :].astype(x.dtype), (1,),
                                 [(SSM_CONV // 2, SSM_CONV // 2)],
                                 dimension_numbers=('NWC', 'WIO', 'NWC'),
                                 feature_group_count=c)
    return y + bias.astype(x.dtype)


def ssd_scan(x, dt, a, bm, cm):
    b, l = x.shape[:2]
    nc = l // SSM_CHUNK
    f32 = jnp.float32
    x = x.astype(f32).reshape(b, nc, SSM_CHUNK, SSM_GROUPS, SSM_HPG, SSM_HDIM)
    dt = dt.reshape(b, nc, SSM_CHUNK, SSM_GROUPS, SSM_HPG)
    bm = bm.astype(f32).reshape(b, nc, SSM_CHUNK, SSM_GROUPS, SSM_STATE)
    cm = cm.astype(f32).reshape(b, nc, SSM_CHUNK, SSM_GROUPS, SSM_STATE)
    acs = jnp.cumsum(dt * a.reshape(SSM_GROUPS, SSM_HPG), axis=2)
    xdt = x * dt[..., None]
    causal = jnp.tril(jnp.ones((SSM_CHUNK, SSM_CHUNK), dtype=bool))[:, :, None, None]
    seg = acs[:, :, :, None] - acs[:, :, None, :]
    decay_ls = jnp.exp(jnp.where(causal, seg, -jnp.inf))
    cb = jnp.einsum('bclgn,bcsgn->bclsg', cm, bm)
    y_diag = jnp.einsum('bclsgk,bcsgkp->bclgkp', cb[..., None] * decay_ls, xdt)
    decay_s = jnp.exp(acs[:, :, -1:] - acs)
    states = jnp.einsum('bcsgn,bcsgk,bcsgkp->bcgkpn', bm, decay_s, xdt)
    chunk_decay = jnp.exp(acs[:, :, -1])

    def step(h, inp):
        s_c, d_c = inp
        return h * d_c[..., None, None] + s_c, h

    h0 = jnp.zeros((b, SSM_GROUPS, SSM_HPG, SSM_HDIM, SSM_STATE), f32)
    _, prev = lax.scan(step, h0, (jnp.moveaxis(states, 1, 0), jnp.moveaxis(chunk_decay, 1, 0)))
    prev = jnp.moveaxis(prev, 0, 1)
    y_off = jnp.einsum('bclgn,bcgkpn->bclgkp', cm, prev) * jnp.exp(acs)[..., None]
    return (y_diag + y_off).reshape(b, l, SSM_GROUPS, SSM_HPG, SSM_HDIM)


def ssd_mixer(u, conv_w, conv_b, dt_bias, a_log, d_skip, norm_g):
    b, l, _ = u.shape
    z = u[..., :GROUP_W]
    xbc = u[..., GROUP_W:GROUP_W + SSM_XBC]
    dt_raw = u[..., GROUP_W + SSM_XBC:].reshape(b, l, 2, SSM_HEADS)
    xbc = jax.nn.silu(dwconv(xbc, conv_w, conv_b))
    x = xbc[..., :GROUP_W].reshape(b, l, SSM_GROUPS, SSM_HPG, SSM_HDIM)
    nbc = SSM_GROUPS * SSM_STATE
    bm = xbc[..., GROUP_W:GROUP_W + nbc].reshape(b, l, SSM_GROUPS, SSM_STATE)
    cm = xbc[..., GROUP_W + nbc:].reshape(b, l, SSM_GROUPS, SSM_STATE)
    dt = jax.nn.softplus(dt_raw.astype(jnp.float32) + dt_bias.astype(jnp.float32))
    a = -jnp.exp(a_log.astype(jnp.float32))
    flip = lambda t: jnp.flip(t, axis=1)
    y_f = ssd_scan(x, dt[:, :, 0], a[0], bm, cm)
    y_b = flip(ssd_scan(flip(x), flip(dt[:, :, 1]), a[1], flip(bm), flip(cm)))
    y = y_f + y_b + x.astype(jnp.float32) * d_skip.astype(jnp.float32).reshape(SSM_GROUPS, SSM_HPG, 1)
    y = y.reshape(b, l, GROUP_W) * jax.nn.silu(z.astype(jnp.float32))
    y = _rms(y.reshape(b, l, SSM_GROUPS, GROUP_W // SSM_GROUPS)).reshape(b, l, GROUP_W)
    return (y * norm_g.astype(jnp.float32)).astype(u.dtype)


def trunk(x, attn_norm_g, w_in, w_fourier, diff_lambda, diff_subln_g, na_rpb,
          ssm_conv_w, ssm_conv_b, ssm_dt_bias, ssm_A_log, ssm_D, ssm_norm_g,
          w_out, ffn_norm_g, w_gate, w_up, w_down, final_norm_g):
    for i in range(DEPTH):
        lambda_init = 0.8 - 0.6 * math.exp(-0.3 * i)
        h = rmsnorm(x, attn_norm_g[i])
        proj = h @ w_in[i]
        o_fn = fourier_mixer(proj[..., OFF_FN:OFF_DA], w_fourier[i])
        o_da = diff_attention(proj[..., OFF_DA:OFF_NA], diff_lambda[i], diff_subln_g[i], lambda_init)
        o_na = neighbourhood_attention(proj[..., OFF_NA:OFF_SSM], na_rpb[i])
        o_ssm = ssd_mixer(proj[..., OFF_SSM:], ssm_conv_w[i], ssm_conv_b[i], ssm_dt_bias[i],
                          ssm_A_log[i], ssm_D[i], ssm_norm_g[i])
        x = x + jnp.concatenate([o_fn, o_da, o_na, o_ssm], axis=-1) @ w_out[i]
        h = rmsnorm(x, ffn_norm_g[i])
        x = x + (jax.nn.silu(h @ w_gate[i]) * (h @ w_up[i])) @ w_down[i]
    return rmsnorm(x, final_norm_g)


def setup_inputs(seed: int = 0) -> dict:
    key = jax.random.key(seed)
    ks = jax.random.split(key, 24)
    f32 = jnp.float32
    nrm = lambda k, shape, scale: jax.random.normal(k, shape, f32) * scale
    dt0 = jnp.exp(jax.random.uniform(ks[10], (DEPTH, 2, SSM_HEADS), f32,
                                     math.log(1e-3), math.log(1e-1)))
    return {
        'x_prompt': nrm(ks[0], (BATCH, SEQ, D_MODEL), 1.0),
        'x_sample': nrm(ks[1], (DEC_BATCH, DEC_SEQ, D_MODEL), 1.0),
        'attn_norm_g': 1.0 + nrm(ks[2], (DEPTH, D_MODEL), 0.05),
        'w_in': nrm(ks[3], (DEPTH, D_MODEL, D_IN), D_MODEL ** -0.5),
        'w_fourier': nrm(ks[4], (DEPTH, GROUP_W, GROUP_W), GROUP_W ** -0.5),
        'diff_lambda': nrm(ks[5], (DEPTH, 4, DA_QDIM), 0.1),
        'diff_subln_g': 1.0 + nrm(ks[6], (DEPTH, DA_VDIM), 0.05),
        'na_rpb': nrm(ks[7], (DEPTH, NA_HEADS, 2 * NA_WIN_R - 1, 2 * NA_WIN_C - 1), 0.02),
        'ssm_conv_w': nrm(ks[8], (DEPTH, SSM_CONV, SSM_XBC), SSM_CONV ** -0.5),
        'ssm_conv_b': nrm(ks[9], (DEPTH, SSM_XBC), 0.01),
        'ssm_dt_bias': dt0 + jnp.log(-jnp.expm1(-dt0)),
        'ssm_A_log': jnp.log(jax.random.uniform(ks[11], (DEPTH, 2, SSM_HEADS), f32, 1.0, 16.0)),
        'ssm_D': 1.0 + nrm(ks[12], (DEPTH, SSM_HEADS), 0.05),
        'ssm_norm_g': 1.0 + nrm(ks[13], (DEPTH, GROUP_W), 0.05),
        'w_out': nrm(ks[14], (DEPTH, D_MODEL, D_MODEL), D_MODEL ** -0.5),
        'ffn_norm_g': 1.0 + nrm(ks[15], (DEPTH, D_MODEL), 0.05),
        'w_gate': nrm(ks[16], (DEPTH, D_MODEL, D_FF), D_MODEL ** -0.5),
        'w_up': nrm(ks[17], (DEPTH, D_MODEL, D_FF), D_MODEL ** -0.5),
        'w_down': nrm(ks[18], (DEPTH, D_FF, D_MODEL), D_FF ** -0.5),
        'final_norm_g': 1.0 + nrm(ks[19], (D_MODEL,), 0.05),
    }


def reference(x_prompt, x_sample, attn_norm_g, w_in, w_fourier, diff_lambda, diff_subln_g, na_rpb,
              ssm_conv_w, ssm_conv_b, ssm_dt_bias, ssm_A_log, ssm_D, ssm_norm_g,
              w_out, ffn_norm_g, w_gate, w_up, w_down, final_norm_g):
    y_prompt = trunk(x_prompt, attn_norm_g, w_in, w_fourier, diff_lambda, diff_subln_g, na_rpb,
                     ssm_conv_w, ssm_conv_b, ssm_dt_bias, ssm_A_log, ssm_D, ssm_norm_g,
                     w_out, ffn_norm_g, w_gate, w_up, w_down, final_norm_g)
    y_sample = trunk(x_sample, attn_norm_g, w_in, w_fourier, diff_lambda, diff_subln_g, na_rpb,
                     ssm_conv_w, ssm_conv_b, ssm_dt_bias, ssm_A_log, ssm_D, ssm_norm_g,
                     w_out, ffn_norm_g, w_gate, w_up, w_down, final_norm_g)
    return (y_prompt, y_sample)
```

```python
import math
import numpy as np
import ml_dtypes
import concourse.bass as bass
import concourse.mybir as mybir
from concourse.bass_utils import run_bass_kernel_spmd

F32 = mybir.dt.float32
BF16 = mybir.dt.bfloat16
AF = mybir.ActivationFunctionType
ALU = mybir.AluOpType
AX = mybir.AxisListType

D = 1024
DEPTH = 4
GW = 256
D_IN = 2824
D_FF = 2816
NFF = D_FF // 128
OFF_FN, OFF_DA, OFF_NA, OFF_SSM = 0, 256, 1024, 1792
EPS = 1e-6
NEG = -240000.0
SBUF_BASE = 16640
SBUF_LIMIT = 229376
NRING = 12
DA_NFILL = 0


class Res:
    __slots__ = ("w", "r")

    def __init__(self):
        self.w = {}
        self.r = {}


def _merge(dst, src):
    for k, v in src.items():
        if dst.get(k, 0) < v:
            dst[k] = v


class Prog:
    ENG = ("pe", "act", "dve", "pool", "sp")

    def __init__(self, nc, esems, rings):
        self.nc = nc
        self.h = {"pe": nc.tensor, "act": nc.scalar, "dve": nc.vector, "pool": nc.gpsimd, "sp": nc.sync}
        self.esem = esems
        self.cnt = {e: 0 for e in self.ENG}
        self.seen = {e: {} for e in self.ENG}
        self.lists = {e: [] for e in self.ENG}
        self.ring = rings
        self.ringcnt = {q: [0] * len(rings[q]) for q in rings}
        self.ringpos = {q: 0 for q in rings}
        self.ninst = 0
        self._cap = None

    def capture(self, fn):
        old = self._cap
        self._cap = []
        fn()
        ops, self._cap = self._cap, old
        return ops

    def replay(self, lists, weights=None):
        idx = [0] * len(lists)
        weights = weights or [1] * len(lists)
        while any(idx[k] < len(lists[k]) for k in range(len(lists))):
            for k in range(len(lists)):
                for _ in range(weights[k]):
                    if idx[k] < len(lists[k]):
                        kind, args, kw = lists[k][idx[k]]
                        idx[k] += 1
                        (self.op if kind == "op" else self.dma)(*args, **kw)

    def _waits(self, e, toks):
        out = []
        seen = self.seen[e]
        for k, v in toks.items():
            if e == "pe" and k == self.esem["pe"]:
                continue
            if seen.get(k, 0) < v:
                seen[k] = v
                out.append((k, v))
        return out

    def op(self, e, fn, reads=(), writes=(), inc=True):
        if self._cap is not None:
            self._cap.append(("op", (e, fn), dict(reads=reads, writes=writes, inc=inc)))
            return
        toks = {}
        for r in reads:
            _merge(toks, r.w)
        for w in writes:
            _merge(toks, w.w)
            _merge(toks, w.r)
        waits = self._waits(e, toks)
        sem = self.esem[e]
        if inc:
            self.cnt[e] += 1
            val = self.cnt[e]
        else:
            val = self.cnt[e] + 1
        for r in reads:
            if r.r.get(sem, 0) < val:
                r.r[sem] = val
        for w in writes:
            if w.w.get(sem, 0) < val:
                w.w[sem] = val
        self.lists[e].append((waits, fn, (sem, 1) if inc else None))
        self.ninst += 1

    def dma(self, q, out, in_, reads=(), writes=(), slow=False):
        if self._cap is not None:
            self._cap.append(("dma", (q, out, in_), dict(reads=reads, writes=writes, slow=slow)))
            return
        k = self.ringpos[q]
        self.ringpos[q] = (k + 1) % len(self.ring[q])
        sem = self.ring[q][k]
        toks = {}
        for r in reads:
            _merge(toks, r.w)
        for w in writes:
            _merge(toks, w.w)
            _merge(toks, w.r)
        prev = self.ringcnt[q][k]
        if prev > 0:
            toks[sem] = max(toks.get(sem, 0), 16 * prev)
        waits = self._waits(q, toks)
        self.ringcnt[q][k] += 1
        val = 16 * self.ringcnt[q][k]
        for r in reads:
            if r.r.get(sem, 0) < val:
                r.r[sem] = val
        for w in writes:
            if w.w.get(sem, 0) < val:
                w.w[sem] = val
        self.lists[q].append((waits, lambda eng: eng.dma_start(out=out, in_=in_, allow_slow_non_contiguous=slow), (sem, 16)))
        self.ninst += 1

    def barrier(self):
        toks = {}
        for e in self.ENG:
            if self.cnt[e] > 0:
                toks[self.esem[e]] = self.cnt[e]
        for q in self.ring:
            for k, s in enumerate(self.ring[q]):
                if self.ringcnt[q][k] > 0:
                    toks[s] = 16 * self.ringcnt[q][k]
        for e in self.ENG:
            waits = self._waits(e, dict(toks))
            if waits:
                self.lists[e].append((waits, None, None))

    def emit(self, e, eng):
        for waits, fn, inc in self.lists[e]:
            for s, v in waits:
                eng.wait_ge(s, v)
            if fn is not None:
                ins = fn(eng)
                if inc is not None:
                    ins.then_inc(inc[0], inc[1])


class SB:
    def __init__(self, nc):
        self.nc = nc
        self.off = SBUF_BASE
        self.n = 0

    def alloc(self, shape, dt, name="t"):
        esz = 2 if dt == BF16 else 4
        nb = esz
        for s in shape[1:]:
            nb *= s
        nb = (nb + 31) // 32 * 32
        assert self.off + nb <= SBUF_LIMIT, f"SBUF overflow {name}: {self.off}+{nb}"
        self.n += 1
        t = self.nc.alloc_sbuf_tensor_at(f"{name}{self.n}", list(shape), dt, offset=self.off)
        self.off += nb
        return t


def _rr(n):
    return [Res() for _ in range(n)]


def pipeline(n, stages, skew=1):
    ns = len(stages)
    for step in range(n + (ns - 1) * skew):
        for s_ in range(ns - 1, -1, -1):
            i = step - s_ * skew
            if 0 <= i < n:
                stages[s_](i)


WEIGHT_SHAPES = {
    "attn_norm_g": (DEPTH, D), "w_in": (DEPTH, D, D_IN), "w_fourier": (DEPTH, GW, GW),
    "diff_lambda": (DEPTH, 4, 32), "diff_subln_g": (DEPTH, 64), "na_rpb": (DEPTH, 4, 15, 31),
    "ssm_conv_w": (DEPTH, 5, 768), "ssm_conv_b": (DEPTH, 768), "ssm_dt_bias": (DEPTH, 2, 4),
    "ssm_A_log": (DEPTH, 2, 4), "ssm_D": (DEPTH, 4), "ssm_norm_g": (DEPTH, GW),
    "w_out": (DEPTH, D, D), "ffn_norm_g": (DEPTH, D), "w_gate": (DEPTH, D, D_FF),
    "w_up": (DEPTH, D, D_FF), "w_down": (DEPTH, D_FF, D), "final_norm_g": (D,),
}


class Ctx:
    pass


def build(seq_lens, depth=DEPTH, mix=("fn", "da", "na", "ssm"), do_ffn=True):
    from contextlib import ExitStack
    T = sum(seq_lens)
    nc = bass.Bass("TRN2", target_bir_lowering=False)
    c = Ctx()
    c.nc, c.T, c.seq_lens, c.depth, c.mix = nc, T, seq_lens, depth, mix

    def din(name, shape, dt=F32):
        return nc.dram_tensor(name, list(shape), dt, kind="ExternalInput").ap()

    c.xin = din("xin", [T, D])
    c.W = {name: din(name, shape) for name, shape in WEIGHT_SHAPES.items()}
    c.C = {name: din(name, shape, dt) for name, (shape, dt) in const_specs(seq_lens).items()}
    c.y = nc.dram_tensor("y", [T, D], F32, kind="ExternalOutput").ap()
    c.rpad = nc.dram_tensor("rpad_scr", [128 + 1860 + 128], F32)
    import os as _os
    c.Xs = nc.dram_tensor("xs_scr", [T, D], F32, kind=("ExternalOutput" if _os.environ.get("KDEBUG") else "Internal")).ap()
    import os as _os
    c.Os = nc.dram_tensor("os_scr", [T, D], BF16, kind=("ExternalOutput" if _os.environ.get("KDEBUG") else "Internal")).ap()

    with ExitStack() as es:
        esems = {e: es.enter_context(nc.semaphore(f"s_{e}")) for e in Prog.ENG}
        rings = {q: [es.enter_context(nc.semaphore(f"r_{q}{i}")) for i in range(NRING)] for q in ("sp", "pool")}
        P = Prog(nc, esems, rings)
        c.P = P
        c.sb = SB(nc)
        c.pd = [nc.alloc_psum_tensor(f"pd{i}", [128, 1024], F32) for i in range(4)]
        c.pdr = [[Res(), Res()] for _ in range(4)]
        record(c, do_ffn)
        blk = es.enter_context(nc.Block())

        @blk.tensor
        def _(e):
            P.emit("pe", e)

        @blk.scalar
        def _(e):
            P.emit("act", e)

        @blk.vector
        def _(e):
            P.emit("dve", e)

        @blk.gpsimd
        def _(e):
            P.emit("pool", e)

        @blk.sync
        def _(e):
            P.emit("sp", e)
    return nc


def const_specs(seq_lens):
    s = {"ident": ([128, 128], BF16), "cs64": ([256, 512], BF16), "dlo": ([128, 128], BF16),
         "dhi": ([128, 128], BF16), "negt": ([128, 128], BF16), "na_wm": ([128, 64], F32), "identf": ([128, 128], F32),
         "ropec": ([128, max(seq_lens)], BF16), "ropes": ([128, max(seq_lens)], BF16),
         "mask4": ([128, 4], F32), "mask2": ([128, 2], F32), "tri_f": ([128, 128], F32), "tri_b": ([128, 128], F32), "onesf": ([128, 128], F32),
         "mask_f": ([128, 512], BF16), "mask_b": ([128, 512], BF16)}
    for L in sorted(set(seq_lens)):
        s[f"dftc{L}"] = ([L, L], BF16)
        s[f"dfts{L}"] = ([L, L], BF16)
        s[f"alt{L}"] = ([128, 1], BF16)
    return s


def make_consts(seq_lens):
    bf = ml_dtypes.bfloat16
    out = {"ident": np.eye(128, dtype=np.float32).astype(bf)}
    dlo = np.zeros((128, 128), np.float32)
    dlo[np.arange(64), np.arange(64)] = 1.0
    dhi = np.zeros((128, 128), np.float32)
    dhi[np.arange(64, 128), np.arange(64, 128)] = 1.0
    out["dlo"], out["dhi"] = dlo.astype(bf), dhi.astype(bf)
    out["negt"] = np.full((128, 128), NEG, np.float32).astype(bf)
    wm = np.zeros((128, 64), np.float32)
    for p in range(128):
        col = p % 64
        for qp in range(64):
            q = 63 - qp
            sc = min(max(q - 8, 0), 48)
            wm[p, qp] = 0.0 if sc <= col < sc + 16 else NEG / 8.0
    out["na_wm"] = wm
    out["identf"] = np.eye(128, dtype=np.float32)
    Lm = max(seq_lens)
    fi = (np.arange(128) % 16).astype(np.float64)
    inv = np.power(10000.0, -fi / 16.0)
    ra = np.arange(Lm, dtype=np.float64)[None, :] * inv[:, None]
    out["ropec"] = np.cos(ra).astype(np.float32).astype(bf)
    out["ropes"] = np.sin(ra).astype(np.float32).astype(bf)
    out["mask2"] = (np.arange(128)[:, None] // 64 == np.arange(2)[None, :]).astype(np.float32)
    out["mask4"] = (np.arange(128)[:, None] // 32 == np.arange(4)[None, :]).astype(np.float32)
    jj, ll = np.arange(128)[:, None], np.arange(128)[None, :]
    out["tri_f"] = (jj <= ll).astype(np.float32)
    out["tri_b"] = (jj >= ll).astype(np.float32)
    out["onesf"] = np.ones((128, 128), np.float32)
    out["mask_f"] = np.tile(np.where(ll < jj, NEG, 0.0).astype(np.float32), (1, 4)).astype(bf)
    out["mask_b"] = np.tile(np.where(ll > jj, NEG, 0.0).astype(np.float32), (1, 4)).astype(bf)
    i64 = np.arange(64)
    ang = 2.0 * np.pi * ((i64[:, None] * i64[None, :]) % 64) / 64.0
    c64, s64 = np.cos(ang) / 8.0, np.sin(ang) / 8.0
    cs = np.zeros((256, 512), np.float64)
    for g in range(4):
        cs[g * 64:(g + 1) * 64, g * 64:(g + 1) * 64] = c64
        cs[g * 64:(g + 1) * 64, 256 + g * 64:256 + (g + 1) * 64] = -s64
    out["cs64"] = cs.astype(np.float32).astype(bf)
    for L in sorted(set(seq_lens)):
        il = np.arange(L, dtype=np.int64)
        a = 2.0 * np.pi * ((il[:, None] * il[None, :]) % L).astype(np.float64) / L
        out[f"dftc{L}"] = (np.cos(a) / np.sqrt(L)).astype(np.float32).astype(bf)
        out[f"dfts{L}"] = (np.sin(a) / np.sqrt(L)).astype(np.float32).astype(bf)
        out[f"alt{L}"] = (((-1.0) ** np.arange(128)) / np.sqrt(L)).astype(np.float32).reshape(128, 1).astype(bf)
    return out


def wload(c, q, dst, src, res):
    c.P.dma(q, dst, src, writes=[res])


def rms_rstd(c, ss, rstd, n, res_ss, res_rstd):
    P = c.P
    P.op("dve", lambda e: e.tensor_scalar(out=rstd[:, 0:n], in0=ss[:, 0:n], scalar1=1.0 / D, scalar2=EPS,
                                           op0=ALU.mult, op1=ALU.add), reads=[res_ss], writes=[res_rstd])
    P.op("act", lambda e: e.activation(out=rstd[:, 0:n], in_=rstd[:, 0:n], func=AF.Sqrt),
         reads=[res_rstd], writes=[res_rstd])
    P.op("dve", lambda e: e.reciprocal(out=rstd[:, 0:n], in_=rstd[:, 0:n]), reads=[res_rstd], writes=[res_rstd])


def transpose8(c, src_tile, src_res, dst_ap, dst_res, bank, evac="act", split=False, half=0):
    P = c.P
    pt = c.pd[bank][:, half * 512:(half + 1) * 512].bitcast(BF16).rearrange("p (k t) -> p k t", k=8)
    pr = c.pdr[bank][half]

    def f(e):
        ins = None
        for k in range(8):
            ins = e.transpose(out=pt[:, k, :], in_=src_tile[:, k * 128:(k + 1) * 128], identity=c.ident[:])
        return ins

    def part(which):
        if which == 0:
            P.op("pe", f, reads=[src_res, c.ident_r], writes=[pr])
        elif evac == "act":
            P.op("act", lambda e: e.copy(out=dst_ap, in_=pt), reads=[pr], writes=[dst_res])
        else:
            P.op("dve", lambda e: e.tensor_copy(out=dst_ap, in_=pt), reads=[pr], writes=[dst_res])
    if split:
        return part
    part(0)
    part(1)


def record(c, do_ffn):
    P, sb, nc = c.P, c.sb, c.nc
    c.ident = sb.alloc([128, 128], BF16, "ident")
    c.ident_r = Res()
    P.dma("sp", c.ident[:], c.C["ident"], writes=[c.ident_r])
    c.epst = sb.alloc([128, 1], F32, "epst")
    c.epst_r = Res()
    P.op("pool", lambda e: e.memset(c.epst[:], EPS), writes=[c.epst_r])
    c.mhalf = sb.alloc([128, 8], F32, "mhalf")
    P.op("pool", lambda e: e.memset(c.mhalf[:], -0.5), writes=[c.epst_r])
    c.gbc = sb.alloc([128, D], F32, "gbc")
    c.gbc_r = Res()
    c.base = sb.off
    tok0 = 0
    for li in range(c.depth):
        src = c.xin if li == 0 else c.Xs
        sb.off = c.base
        P.barrier()
        P.dma("sp", c.gbc[:], c.W["attn_norm_g"][li:li + 1, :].partition_broadcast(128), writes=[c.gbc_r])
        t0 = 0
        wg_pre = None
        for si, L in enumerate(c.seq_lens):
            hook = None
            sb.off = c.base
            if do_ffn and len(c.seq_lens) > 1 and si == len(c.seq_lens) - 1 and L <= 2048:
                WgP, WuP, _ = ffn_weight_tiles(c)
                sb.off = c.base + 8 * D_FF * 2
                wg_pre = (Res(), Res())

                def hook(name, WgP=WgP, WuP=WuP, r=wg_pre, li=li):
                    if name == "da":
                        load_wg(c, li, WgP, r[0])

            phase_m(c, li, src, t0, L, hook)
            t0 += L
        sb.off = c.base
        P.barrier()
        phase_f(c, li, do_ffn, wg_pre)
    P.barrier()


def phase_m(c, li, src, t0, L, hook=None):
    P, sb = c.P, c.sb
    nt = L // 128
    m0 = sb.off
    P.barrier()
    hT = sb.alloc([128, 8, L], BF16, "hT")
    hT_r = _rr(nt)
    ss = sb.alloc([128, nt], F32, "ss")
    rstd = sb.alloc([128, nt], F32, "rstd")
    ss_r, rstd_r = _rr(nt), _rr(nt)
    mm = sb.off
    NXA, NHA = 4, 3
    xts = [sb.alloc([128, D], F32, "xt") for _ in range(NXA)]
    xts_r = _rr(NXA)
    hts = [sb.alloc([128, D], BF16, "ht") for _ in range(NHA)]
    hts_r = _rr(NHA)
    junk = sb.alloc([128, D], BF16, "junk")
    junk_r = Res()

    def a0(j):
        xt, xr = xts[j % NXA], xts_r[j % NXA]
        P.dma("sp", xt[:], src[t0 + j * 128:t0 + (j + 1) * 128, :], writes=[xr])
        P.op("act", lambda e: e.activation(out=junk[:], in_=xt[:], func=AF.Square, accum_out=ss[:, j:j + 1]),
             reads=[xr], writes=[junk_r, ss_r[j]])

    def a1(j):
        xt, xr = xts[j % NXA], xts_r[j % NXA]
        ht, hr = hts[j % NHA], hts_r[j % NHA]
        P.op("act", lambda e: e.activation(out=rstd[:, j:j + 1], in_=ss[:, j:j + 1], func=AF.Sqrt,
                                           bias=c.epst[:, 0:1], scale=1.0 / D),
             reads=[ss_r[j], c.epst_r], writes=[rstd_r[j]])
        P.op("dve", lambda e: e.reciprocal(out=rstd[:, j:j + 1], in_=rstd[:, j:j + 1]),
             reads=[rstd_r[j]], writes=[rstd_r[j]])
        P.op("dve", lambda e: e.scalar_tensor_tensor(
            out=ht[:], in0=xt[:], scalar=rstd[:, j:j + 1], in1=c.gbc[:], op0=ALU.mult, op1=ALU.mult),
            reads=[xr, rstd_r[j], c.gbc_r], writes=[hr])

    def a2(j):
        ht, hr = hts[j % NHA], hts_r[j % NHA]
        transpose8(c, ht, hr, hT[:, :, j * 128:(j + 1) * 128], hT_r[j], bank=j % 2, evac="act", split=True)(0)

    def a3(j):
        ht, hr = hts[j % NHA], hts_r[j % NHA]
        transpose8(c, ht, hr, hT[:, :, j * 128:(j + 1) * 128], hT_r[j], bank=j % 2, evac="act", split=True)(1)
    pipeline(nt, [a0, a1, a2, a3])
    for name in ("fn", "da", "na", "ssm"):
        sb.off = mm
        P.barrier()
        if hook is not None:
            hook(name)
        if name in c.mix:
            MIXERS[name](c, li, t0, L, hT, hT_r)
        else:
            zero_mixer(c, name, t0, L)
    sb.off = mm
    P.barrier()
    mixer_out_proj(c, li, src, t0, L)
    sb.off = m0


def zero_mixer(c, name, t0, L):
    P, sb = c.P, c.sb
    col = {"fn": 0, "da": 256, "na": 512, "ssm": 768}[name]
    z = sb.alloc([128, 4, 256], BF16, "zz")
    zr = Res()
    P.op("pool", lambda e: e.memset(z[:], 0.0), writes=[zr])
    for b in range(L // 512):
        dst = c.Os[t0 + b * 512:t0 + (b + 1) * 512, col:col + 256].rearrange("(j p) c -> p j c", p=128)
        P.dma("sp", dst, z[:], reads=[zr])


def mixer_out_proj(c, li, src, t0, L):
    P, sb = c.P, c.sb
    nt = L // 128
    Wo = sb.alloc([128, 8, D], BF16, "Wo")
    Wo_r = Res()
    wv = c.W["w_out"][li].rearrange("(k p) c -> p k c", p=128)
    for k in range(8):
        P.dma("pool", Wo[:, k, :], wv[:, k, :], writes=[Wo_r])
    NX, NO = 4, 3
    xts = [sb.alloc([128, D], F32, "xo") for _ in range(NX)]
    xts_r = _rr(NX)
    ots = [sb.alloc([128, D], BF16, "ot") for _ in range(NO)]
    ots_r = _rr(NO)
    oTs = [sb.alloc([128, 8, 128], BF16, "oT") for _ in range(2)]
    oTs_r = _rr(2)

    def s0(j):
        r0 = t0 + j * 128
        P.dma("sp", xts[j % NX][:], src[r0:r0 + 128, :], writes=[xts_r[j % NX]])
        P.dma("sp", ots[j % NO][:], c.Os[r0:r0 + 128, :], writes=[ots_r[j % NO]])

    def s1(j):
        transpose8(c, ots[j % NO], ots_r[j % NO], oTs[j % 2][:], oTs_r[j % 2], bank=0, half=j % 2, evac="act",
                   split=True)(0)

    def s2(j):
        transpose8(c, ots[j % NO], ots_r[j % NO], oTs[j % 2][:], oTs_r[j % 2], bank=0, half=j % 2, evac="act",
                   split=True)(1)

    def s3(j):
        xt, xr = xts[j % NX], xts_r[j % NX]
        oT, oTr = oTs[j % 2], oTs_r[j % 2]
        for half in range(2):
            bank = 1 + (j * 2 + half) % 3
            ps = c.pd[bank][:, 0:512]
            pr = c.pdr[bank][0]

            def f(e, half=half, ps=ps):
                ins = None
                for k in range(8):
                    ins = e.matmul(ps, lhsT=oT[:, k, :], rhs=Wo[:, k, half * 512:(half + 1) * 512],
                                   start=(k == 0), stop=(k == 7))
                return ins
            P.op("pe", f, reads=[oTr, Wo_r], writes=[pr])

    def s4(j):
        xt, xr = xts[j % NX], xts_r[j % NX]
        r0 = t0 + j * 128
        for half in range(2):
            bank = 1 + (j * 2 + half) % 3
            ps = c.pd[bank][:, 0:512]
            pr = c.pdr[bank][0]
            P.op("dve", lambda e, half=half, ps=ps: e.tensor_tensor(
                out=xt[:, half * 512:(half + 1) * 512], in0=ps, in1=xt[:, half * 512:(half + 1) * 512], op=ALU.add),
                reads=[pr, xr], writes=[xr])
        P.dma("sp", c.Xs[r0:r0 + 128, :], xt[:], reads=[xr])
    pipeline(nt, [s0, s1, s2, s3, s4])


def ffn_weight_tiles(c):
    sb = c.sb
    Wg = sb.alloc([128, 8, D_FF], BF16, "Wg")
    Wu = sb.alloc([128, 8, D_FF], BF16, "Wu")
    Wdn = sb.alloc([128, NFF, D], BF16, "Wdn")
    return Wg, Wu, Wdn


def load_wg(c, li, Wg, Wg_r):
    gv = c.W["w_gate"][li].rearrange("(k p) c -> p k c", p=128)
    for k in range(8):
        c.P.dma("pool", Wg[:, k, :], gv[:, k, :], writes=[Wg_r])


def phase_f(c, li, do_ffn, wg_pre=None):
    P, sb = c.P, c.sb
    last = (li == c.depth - 1)
    T = c.T
    Wg, Wu, Wdn = ffn_weight_tiles(c)
    Wg_r, Wu_r, Wdn_r = Res(), Res(), Res()
    if wg_pre is not None:
        Wg_r = wg_pre[0]
    if do_ffn:
        uv = c.W["w_up"][li].rearrange("(k p) c -> p k c", p=128)
        dv = c.W["w_down"][li].rearrange("(k p) c -> p k c", p=128)
        if wg_pre is None:
            load_wg(c, li, Wg, Wg_r)
        for k in range(8):
            P.dma("pool", Wu[:, k, :], uv[:, k, :], writes=[Wu_r])
        for k in range(0, NFF, 2):
            P.dma("pool", Wdn[:, k:k + 2, :], dv[:, k:k + 2, :], writes=[Wdn_r])
    P.dma("sp", c.gbc[:], c.W["ffn_norm_g"][li:li + 1, :].partition_broadcast(128), writes=[c.gbc_r])
    gfin, gfin_r = None, Res()
    if last:
        gfin = sb.alloc([128, D], F32, "gfin")
        P.dma("sp", gfin[:], c.W["final_norm_g"][None, :].partition_broadcast(128), writes=[gfin_r])
    NXN, NXR = 2, 2
    xns = [sb.alloc([128, D], F32, "xn") for _ in range(NXN)]
    xns_r = _rr(NXN)
    xrs = [sb.alloc([128, D], F32, "xr") for _ in range(NXR)]
    xrs_r = _rr(NXR)
    hts = [sb.alloc([128, D], BF16, "hf") for _ in range(4)]
    hts_r = _rr(4)
    h2T = sb.alloc([128, 8, 512], BF16, "h2T")
    h2T_r = _rr(4)
    aT = sb.alloc([128, NFF, 512], BF16, "aT")
    aT_r = _rr(NFF)
    sgs = [sb.alloc([128, 512], F32, "sg") for _ in range(2)]
    sgs_r = _rr(2)
    junk = sb.alloc([128, D], BF16, "junkf")
    junk_r = Res()
    nblk = T // 512
    ss = sb.alloc([128, 2 * nblk * 4], F32, "ssf")
    rstd = sb.alloc([128, 2 * nblk * 4], F32, "rstdf")
    cnt = {"xn": 0, "xr": 0}

    def norm_chain(xt, xr, sc, gain, gain_r, out_ap, out_r):
        ssr, rsr = Res(), Res()
        P.op("act", lambda e: e.activation(out=junk[:], in_=xt[:], func=AF.Square, accum_out=ss[:, sc:sc + 1]),
             reads=[xr], writes=[junk_r, ssr])
        P.op("act", lambda e: e.activation(out=rstd[:, sc:sc + 1], in_=ss[:, sc:sc + 1], func=AF.Sqrt,
                                           bias=c.epst[:, 0:1], scale=1.0 / D), reads=[ssr, c.epst_r], writes=[rsr])
        P.op("dve", lambda e: e.reciprocal(out=rstd[:, sc:sc + 1], in_=rstd[:, sc:sc + 1]), reads=[rsr], writes=[rsr])
        P.op("dve", lambda e: e.scalar_tensor_tensor(out=out_ap, in0=xt[:], scalar=rstd[:, sc:sc + 1], in1=gain[:],
                                                     op0=ALU.mult, op1=ALU.mult), reads=[xr, rsr, gain_r],
             writes=[out_r])

    def pro_norm(b):
        for j in range(4):
            xt, xr = xns[cnt["xn"] % NXN], xns_r[cnt["xn"] % NXN]
            cnt["xn"] += 1
            r0 = b * 512 + j * 128
            P.dma("sp", xt[:], c.Xs[r0:r0 + 128, :], writes=[xr])
            norm_chain(xt, xr, b * 4 + j, c.gbc, c.gbc_r, hts[j][:], hts_r[j])

    def pro_T(b):
        for j in range(4):
            transpose8(c, hts[j], hts_r[j], h2T[:, :, j * 128:(j + 1) * 128], h2T_r[j], bank=0, half=j % 2,
                       evac="dve")

    def gateup(b):
        for cc in range(NFF):
            bg, bu = (1, 2) if cc % 2 == 0 else (3, 1)
            hg, hu = (0, 0) if cc % 2 == 0 else (0, 1)
            if cc % 2 == 1:
                bu, hu = 2, 1
            psg, psu = c.pd[bg][:, hg * 512:(hg + 1) * 512], c.pd[bu][:, hu * 512:(hu + 1) * 512]
            prg, pru = c.pdr[bg][hg], c.pdr[bu][hu]
            sg, sgr = sgs[cc % 2], sgs_r[cc % 2]

            def fg(e, cc=cc, ps=psg, Wx=Wg):
                ins = None
                for k in range(8):
                    ins = e.matmul(ps, lhsT=Wx[:, k, cc * 128:(cc + 1) * 128], rhs=h2T[:, k, :],
                                   start=(k == 0), stop=(k == 7))
                return ins

            def fu(e, cc=cc, ps=psu, Wx=Wu):
                ins = None
                for k in range(8):
                    ins = e.matmul(ps, lhsT=Wx[:, k, cc * 128:(cc + 1) * 128], rhs=h2T[:, k, :],
                                   start=(k == 0), stop=(k == 7))
                return ins
            P.op("pe", fg, reads=h2T_r + [Wg_r], writes=[prg])
            P.op("pe", fu, reads=h2T_r + [Wu_r], writes=[pru])
            P.op("act", lambda e, sg=sg, ps=psg: e.activation(out=sg[:], in_=ps, func=AF.Silu),
                 reads=[prg], writes=[sgr])
            P.op("dve", lambda e, sg=sg, ps=psu, cc=cc: e.tensor_tensor(out=aT[:, cc, :], in0=ps, in1=sg[:],
                                                                        op=ALU.mult),
                 reads=[pru, sgr], writes=[aT_r[cc]])

    def down(b):
        for j in range(4):
            xt, xr = xrs[cnt["xr"] % NXR], xrs_r[cnt["xr"] % NXR]
            cnt["xr"] += 1
            r0 = b * 512 + j * 128
            P.dma("sp", xt[:], c.Xs[r0:r0 + 128, :], writes=[xr])
            if do_ffn:
                for half in range(2):
                    bank, hs = ((1, 0), (2, 0)) [half] if j % 2 == 0 else ((3, 0), (3, 1))[half]
                    ps = c.pd[bank][:, hs * 512:(hs + 1) * 512]
                    pr = c.pdr[bank][hs]

                    def fd(e, j=j, half=half, ps=ps):
                        ins = None
                        for cc in range(NFF):
                            ins = e.matmul(ps, lhsT=aT[:, cc, j * 128:(j + 1) * 128],
                                           rhs=Wdn[:, cc, half * 512:(half + 1) * 512],
                                           start=(cc == 0), stop=(cc == NFF - 1))
                        return ins
                    P.op("pe", fd, reads=aT_r + [Wdn_r], writes=[pr])
                    P.op("dve", lambda e, xt=xt, half=half, ps=ps: e.tensor_tensor(
                        out=xt[:, half * 512:(half + 1) * 512], in0=ps, in1=xt[:, half * 512:(half + 1) * 512],
                        op=ALU.add), reads=[pr, xr], writes=[xr])
            if last:
                norm_chain(xt, xr, nblk * 4 + b * 4 + j, gfin, gfin_r, xt[:], xr)
                P.dma("sp", c.y[r0:r0 + 128, :], xt[:], reads=[xr])
            else:
                P.dma("sp", c.Xs[r0:r0 + 128, :], xt[:], reads=[xr])

    if do_ffn:
        pro_norm(0)
        pro_T(0)
    for b in range(nblk):
        if do_ffn:
            gateup(b)
            if b + 1 < nblk:
                pro_norm(b + 1)
        down(b)
        if do_ffn and b + 1 < nblk:
            pro_T(b + 1)


def evac(c, i, out, in_, reads, writes):
    if i % 2 == 0:
        c.P.op("act", lambda e: e.copy(out=out, in_=in_), reads=reads, writes=writes)
    else:
        c.P.op("dve", lambda e: e.tensor_copy(out=out, in_=in_), reads=reads, writes=writes)


def mixer_fn(c, li, t0, L, hT, hT_r):
    P, sb = c.P, c.sb
    nt, nb = L // 128, L // 512
    win = c.W["w_in"][li].rearrange("(k p) c -> p k c", p=128)
    Wi = sb.alloc([128, 8, 256], BF16, "fnWi")
    Wi_r = Res()
    P.dma("pool", Wi[:], win[:, :, OFF_FN:OFF_FN + 256], writes=[Wi_r])
    CS = sb.alloc([128, 2, 512], BF16, "fnCS")
    CS_r = Res()
    P.dma("sp", CS[:], c.C["cs64"].rearrange("(c p) n -> p c n", p=128), writes=[CS_r])
    Wf = sb.alloc([128, 2, 256], BF16, "fnWf")
    Wf_r = Res()
    P.dma("pool", Wf[:], c.W["w_fourier"][li].rearrange("(c p) n -> p c n", p=128), writes=[Wf_r])
    WT = sb.alloc([128, 2, D], BF16, "fnWT")
    WT_r = Res()
    for cc in range(2):
        pt = c.pd[0][:, 0:512].bitcast(BF16).rearrange("p (k t) -> p k t", k=8)
        pr = c.pdr[0][0]

        def f(e, cc=cc, pt=pt):
            ins = None
            for k in range(8):
                ins = e.transpose(out=pt[:, k, :], in_=Wi[:, k, cc * 128:(cc + 1) * 128], identity=c.ident[:])
            return ins
        P.op("pe", f, reads=[Wi_r, c.ident_r], writes=[pr])
        P.op("act", lambda e, cc=cc, pt=pt: e.copy(out=WT[:, cc, :].rearrange("p (k t) -> p k t", k=8), in_=pt),
             reads=[pr], writes=[WT_r])
    Wcs = sb.alloc([128, 8, 512], BF16, "fnWcs")
    Wcs_r = Res()
    for k in range(8):
        bank = 1 + k % 2
        ps, pr = c.pd[bank][:, 0:512], c.pdr[bank][0]

        def f(e, k=k, ps=ps):
            ins = None
            for cc in range(2):
                ins = e.matmul(ps, lhsT=WT[:, cc, k * 128:(k + 1) * 128], rhs=CS[:, cc, :], start=(cc == 0),
                               stop=(cc == 1))
            return ins
        P.op("pe", f, reads=[WT_r, CS_r], writes=[pr])
        evac(c, k, Wcs[:, k, :], ps, [pr], [Wcs_r])
    Acs = sb.alloc([128, nt, 512], BF16, "fnA")
    Acs_r = Res()
    for j in range(nt):
        bank = 1 + j % 3
        ps, pr = c.pd[bank][:, 0:512], c.pdr[bank][0]

        def f(e, j=j, ps=ps):
            ins = None
            for k in range(8):
                ins = e.matmul(ps, lhsT=hT[:, k, j * 128:(j + 1) * 128], rhs=Wcs[:, k, :], start=(k == 0),
                               stop=(k == 7))
            return ins
        P.op("pe", f, reads=[hT_r[j], Wcs_r], writes=[pr])
        evac(c, j, Acs[:, j, :], ps, [pr], [Acs_r])
    G = min(8, nt)
    ng = nt // G
    stripes = [(sb.alloc([128, G, 512], BF16, "fnSc"), sb.alloc([128, G, 512], BF16, "fnSs"), Res()) for _ in range(2)]
    cv = c.C[f"dftc{L}"].rearrange("(a p) n -> p a n", p=128)
    sv = c.C[f"dfts{L}"].rearrange("(a p) n -> p a n", p=128)
    FT = sb.alloc([128, 2, L], BF16, "fnFT")
    FT_r = _rr(nb)
    alt = sb.alloc([128, 1], BF16, "fnalt")
    alt_r = Res()
    P.dma("sp", alt[:], c.C[f"alt{L}"], writes=[alt_r])
    for cc in range(2):
        ps, pr = c.pd[0][:, cc * 512:cc * 512 + 1], c.pdr[0][cc]

        def fh(e, cc=cc, ps=ps):
            ins = None
            for ja in range(nt):
                ins = e.matmul(ps, lhsT=Acs[:, ja, cc * 128:(cc + 1) * 128], rhs=alt[:], start=(ja == 0),
                               stop=(ja == nt - 1))
            return ins
        P.op("pe", fh, reads=[Acs_r, alt_r], writes=[pr])
        P.op("dve", lambda e, cc=cc, ps=ps: e.tensor_copy(out=FT[:, cc, L // 2:L // 2 + 1], in_=ps), reads=[pr],
             writes=[FT_r[(L // 2) // 512]])
    tmps = [(sb.alloc([128, 512], F32, "fntmp"), Res()) for _ in range(2)]
    si = 0
    ti = 0
    for b in range(nb // 2):
        pb = (0, 1) if b % 2 == 0 else (2, 3)
        Fc = [c.pd[pb[0]][:, 0:512], c.pd[pb[1]][:, 0:512]]
        Fs = [c.pd[pb[0]][:, 512:1024], c.pd[pb[1]][:, 512:1024]]
        Fc_r = [c.pdr[pb[0]][0], c.pdr[pb[1]][0]]
        Fs_r = [c.pdr[pb[0]][1], c.pdr[pb[1]][1]]
        for g in range(ng):
            Sc, Ss, Sr = stripes[si % 2]
            si += 1
            P.dma("sp", Sc[:], cv[:, g * G:(g + 1) * G, b * 512:(b + 1) * 512], writes=[Sr])
            P.dma("sp", Ss[:], sv[:, g * G:(g + 1) * G, b * 512:(b + 1) * 512], writes=[Sr])

            def f(e, g=g, Sc=Sc, Ss=Ss, Fc=Fc, Fs=Fs):
                ins = None
                for a in range(G):
                    ja = g * G + a
                    for cc in range(2):
                        e.matmul(Fc[cc], lhsT=Acs[:, ja, cc * 128:(cc + 1) * 128], rhs=Sc[:, a, :],
                                 start=(ja == 0), stop=(ja == nt - 1))
                        ins = e.matmul(Fs[cc], lhsT=Acs[:, ja, 256 + cc * 128:256 + (cc + 1) * 128], rhs=Ss[:, a, :],
                                       start=(ja == 0), stop=(ja == nt - 1))
                return ins
            P.op("pe", f, reads=[Acs_r, Sr], writes=Fc_r + Fs_r)
        hi = L - b * 512
        mb = (hi - 1) // 512
        for cc in range(2):
            tmp, tmp_r = tmps[ti % 2]
            ti += 1
            P.op("act", lambda e, cc=cc, tmp=tmp, Fs=Fs: e.copy(out=tmp[:], in_=Fs[cc]), reads=[Fs_r[cc]],
                 writes=[tmp_r])
            P.op("dve", lambda e, cc=cc, tmp=tmp, Fc=Fc, b=b: e.tensor_tensor(
                out=FT[:, cc, b * 512:(b + 1) * 512], in0=Fc[cc], in1=tmp[:], op=ALU.add),
                reads=[Fc_r[cc], tmp_r], writes=[FT_r[b]])
            i0 = 1 if b == 0 else 0
            n = 512 - i0
            dstv = FT[:, cc, hi - 511:hi - 511 + n]
            P.op("dve", lambda e, cc=cc, tmp=tmp, Fc=Fc, dstv=dstv, i0=i0: e.tensor_tensor(
                out=dstv[:, ::-1], in0=Fc[cc][:, i0:512], in1=tmp[:, i0:512], op=ALU.subtract),
                reads=[Fc_r[cc], tmp_r], writes=[FT_r[mb], FT_r[min(mb + 1, nb - 1)], FT_r[max(mb - 1, 0)]])
    obuf = [(sb.alloc([128, 4, 256], BF16, "fnO"), Res()) for _ in range(2)]
    for j in range(nt):
        ob, obr = obuf[(j // 4) % 2]
        bank = 1 + j % 3
        ps, pr = c.pd[bank][:, 0:256], c.pdr[bank][0]

        def f(e, j=j, ps=ps):
            ins = None
            for cc in range(2):
                ins = e.matmul(ps, lhsT=FT[:, cc, j * 128:(j + 1) * 128], rhs=Wf[:, cc, :], start=(cc == 0),
                               stop=(cc == 1))
            return ins
        P.op("pe", f, reads=[FT_r[j // 4], Wf_r], writes=[pr])
        evac(c, j, ob[:, j % 4, :], ps, [pr], [obr])
        if j % 4 == 3:
            b = j // 4
            dst = c.Os[t0 + b * 512:t0 + (b + 1) * 512, 0:256].rearrange("(j p) c -> p j c", p=128)
            P.dma("sp", dst, ob[:], reads=[obr])


def mixer_na(c, li, t0, L, hT, hT_r):
    P, sb, nc = c.P, c.sb, c.nc
    nt, nb, rows = L // 128, L // 512, L // 64
    win = c.W["w_in"][li].rearrange("(k p) c -> p k c", p=128)
    Wn = sb.alloc([128, 8, 768], BF16, "naW")
    Wn_r = Res()
    for k in range(0, 8, 2):
        P.dma("pool", Wn[:, k:k + 2, :], win[:, k:k + 2, OFF_NA:OFF_NA + 768], writes=[Wn_r])
    dlo = sb.alloc([128, 128], BF16, "dlo")
    dhi = sb.alloc([128, 128], BF16, "dhi")
    negt = sb.alloc([128, 128], BF16, "negt")
    wm = sb.alloc([128, 64], F32, "nawm")
    cst_r = Res()
    P.dma("sp", dlo[:], c.C["dlo"], writes=[cst_r])
    P.dma("sp", dhi[:], c.C["dhi"], writes=[cst_r])
    P.dma("sp", negt[:], c.C["negt"], writes=[cst_r])
    P.dma("sp", wm[:], c.C["na_wm"], writes=[cst_r])
    NP = 128 + 1860 + 128
    zt = sb.alloc([1, NP], F32, "naz")
    zt_r, rp_r = Res(), Res()
    P.op("pool", lambda e: e.memset(zt[:], 0.0), writes=[zt_r])
    P.dma("sp", c.rpad.ap()[None, :], zt[:], reads=[zt_r], writes=[rp_r])
    P.dma("sp", c.rpad.ap()[128:128 + 1860], c.W["na_rpb"][li].rearrange("h r d -> (h r d)"), writes=[rp_r])
    TMr = sb.alloc([128, 4, 16, 64], F32, "naTMr")
    TMr_r = Res()
    off_lo = 128 - 48 - 31
    for h in range(4):
        P.dma("sp", TMr[0:64, h], bass.AP(c.rpad, off_lo + 465 * h, [[1, 64], [31, 16], [1, 64]]), reads=[rp_r],
              writes=[TMr_r])
        P.dma("sp", TMr[64:128, h], bass.AP(c.rpad, off_lo + 31 + 465 * h, [[1, 64], [31, 16], [1, 64]]),
              reads=[rp_r], writes=[TMr_r])
    TMv = TMr[:].rearrange("p h r q -> p (h r) q")
    P.op("dve", lambda e: e.tensor_tensor(out=TMv, in0=TMv, in1=wm[:].unsqueeze(1).broadcast_to([128, 64, 64]),
                                          op=ALU.add), reads=[TMr_r, cst_r], writes=[TMr_r])
    TM2 = sb.alloc([128, 4, 16, 64], BF16, "naTM2")
    TM2_r = Res()
    P.op("dve", lambda e: e.tensor_scalar(out=TM2[:].rearrange("p h r q -> p (h r) q"), in0=TMv[:, :, ::-1],
                                          scalar1=8.0, scalar2=None, op0=ALU.mult), reads=[TMr_r], writes=[TM2_r])
    qT = sb.alloc([128, 2, L], BF16, "naq")
    kT = sb.alloc([128, 2, L], BF16, "nak")
    qk_r = _rr(nb)
    V = sb.alloc([128, nt, 4, 65], BF16, "nav")
    V_r = Res()
    P.op("pool", lambda e: e.memset(V[:, :, :, 64:65], 1.0), writes=[V_r])
    ei = 0
    for b in range(nb):
        for which, dstT in ((0, qT), (1, kT)):
            for cc in range(2):
                bank = ei % 4
                ps, pr = c.pd[bank][:, 0:512], c.pdr[bank][0]
                col = which * 256 + cc * 128

                def f(e, b=b, col=col, ps=ps):
                    ins = None
                    for k in range(8):
                        ins = e.matmul(ps, lhsT=Wn[:, k, col:col + 128], rhs=hT[:, k, b * 512:(b + 1) * 512],
                                       start=(k == 0), stop=(k == 7))
                    return ins
                P.op("pe", f, reads=hT_r[b * 4:b * 4 + 4] + [Wn_r], writes=[pr])
                evac(c, ei, dstT[:, cc, b * 512:(b + 1) * 512], ps, [pr], [qk_r[b]])
                ei += 1
    for j in range(nt):
        bank = ei % 4
        ps, pr = c.pd[bank][:, 0:256], c.pdr[bank][0]

        def f(e, j=j, ps=ps):
            ins = None
            for k in range(8):
                ins = e.matmul(ps, lhsT=hT[:, k, j * 128:(j + 1) * 128], rhs=Wn[:, k, 512:768], start=(k == 0),
                               stop=(k == 7))
            return ins
        P.op("pe", f, reads=[hT_r[j], Wn_r], writes=[pr])
        evac(c, ei, V[:, j, :, 0:64], ps.rearrange("p (h d) -> p h d", h=4), [pr], [V_r])
        ei += 1
    Pts = [(sb.alloc([128, 640], BF16, "naP"), Res()) for _ in range(2)]
    obuf = [(sb.alloc([128, 4, 256], BF16, "naO"), Res()) for _ in range(2)]
    rec = sb.alloc([128, 2, 4], F32, "narec")
    rec_r = Res()
    allqk = qk_r
    units = [(j, h) for j in range(nt) for h in range(4)]
    Ob = [c.pd[3][:, 0:260].rearrange("p (h d) -> p h d", h=4), c.pd[3][:, 512:772].rearrange("p (h d) -> p h d", h=4)]
    Ob_r = c.pdr[3]

    def geom(j):
        r0 = 2 * j
        sts = [min(max(r0 + jq - 4, 0), rows - 8) for jq in range(2)]
        Rs, Re = sts[0], sts[1] + 8
        return r0, sts, Rs, (Re - Rs + 1) // 2

    mask2 = sb.alloc([128, 2], F32, "nam2")
    P.dma("sp", mask2[:], c.C["mask2"], writes=[cst_r])
    qms = [(sb.alloc([128, 4, 128], BF16, "naqm"), Res()) for _ in range(2)]

    def expand(j2):
        qm2, qm2_r = qms[j2 % 2]
        for hh in range(4):
            P.op("dve", lambda e, hh=hh: e.tensor_scalar(
                out=qm2[:, hh, :], in0=qT[:, hh // 2, j2 * 128:(j2 + 1) * 128], scalar1=mask2[:, hh % 2:hh % 2 + 1],
                scalar2=None, op0=ALU.mult), reads=[qk_r[j2 // 4], cst_r], writes=[qm2_r])

    def n_s(i):
        j, h = units[i]
        r0, sts, Rs, nkt = geom(j)
        cc, pb = h // 2, (h % 2) * 64
        S, S_r = c.pd[1 + i % 2], c.pdr[1 + i % 2]
        qm, qm_r = qms[j % 2]
        if i == 0:
            expand(0)
        if h == 1 and j + 1 < nt:
            expand(j + 1)

        def fs(e):
            ins = None
            for t in range(nkt):
                kt = Rs // 2 + t
                blk = S[:, t * 128:(t + 1) * 128]
                e.matmul(blk, lhsT=kT[:, cc, kt * 128:(kt + 1) * 128], rhs=qm[:, h, :], start=True, stop=False,
                         skip_group_check=True)
                for jq in range(2):
                    rq = r0 + jq
                    R = Rs + 2 * t
                    v_lo = sts[jq] <= R < sts[jq] + 8
                    v_hi = sts[jq] <= R + 1 < sts[jq] + 8
                    dr = R - rq + 7
                    sub = S[:, t * 128 + jq * 64:t * 128 + (jq + 1) * 64]
                    if not v_lo and not v_hi:
                        ins = e.matmul(sub, lhsT=c.ident[:], rhs=negt[:, 0:64], start=False, stop=True,
                                       skip_group_check=True)
                        continue
                    assert -1 <= dr <= 14
                    ins = e.matmul(sub, lhsT=c.ident[:], rhs=TM2[:, h, dr + 1, :], start=False, stop=True,
                                   skip_group_check=True)
                    if not v_lo:
                        ins = e.matmul(sub, lhsT=dlo[:], rhs=negt[:, 0:64], start=False, stop=True,
                                       skip_group_check=True)
                    if not v_hi:
                        ins = e.matmul(sub, lhsT=dhi[:], rhs=negt[:, 0:64], start=False, stop=True,
                                       skip_group_check=True)
            return ins
        P.op("pe", fs, reads=allqk + [TM2_r, cst_r, c.ident_r, qm_r], writes=S_r)

    def n_e(i):
        j, h = units[i]
        nkt = geom(j)[3]
        S, S_r = c.pd[1 + i % 2], c.pdr[1 + i % 2]
        Pt, Pt_r = Pts[i % 2]
        P.op("act", lambda e: e.activation(out=Pt[:, 0:nkt * 128], in_=S[:, 0:nkt * 128], func=AF.Exp, scale=0.125),
             reads=S_r, writes=[Pt_r])

    def n_o(i):
        j, h = units[i]
        r0, sts, Rs, nkt = geom(j)
        Pt, Pt_r = Pts[i % 2]
        O, O_r = Ob[j % 2], Ob_r[j % 2]

        def fo(e):
            ins = None
            for t in range(nkt):
                kt = Rs // 2 + t
                ins = e.matmul(O[:, h, :], lhsT=Pt[:, t * 128:(t + 1) * 128], rhs=V[:, kt, h, :],
                               start=(t == 0), stop=(t == nkt - 1), skip_group_check=True)
            return ins
        P.op("pe", fo, reads=[Pt_r, V_r], writes=[O_r])
        if h == 3:
            ob, obr = obuf[(j // 4) % 2]
            P.op("dve", lambda e: e.reciprocal(out=rec[:, j % 2, :], in_=O[:, :, 64]), reads=[O_r], writes=[rec_r])
            P.op("dve", lambda e: e.tensor_tensor(
                out=ob[:, j % 4, :].rearrange("p (h d) -> p h d", h=4), in0=O[:, :, 0:64],
                in1=rec[:, j % 2, :].unsqueeze(2).broadcast_to([128, 4, 64]), op=ALU.mult), reads=[O_r, rec_r],
                writes=[obr])
            if j % 4 == 3:
                b = j // 4
                dst = c.Os[t0 + b * 512:t0 + (b + 1) * 512, 512:768].rearrange("(j p) c -> p j c", p=128)
                P.dma("sp", dst, ob[:], reads=[obr])
    pipeline(len(units), [n_s, n_e, n_o])


def mixer_da(c, li, t0, L, hT, hT_r):
    P, sb = c.P, c.sb
    nt, nb = L // 128, L // 512
    lam0 = 0.8 - 0.6 * math.exp(-0.3 * li)
    win = c.W["w_in"][li].rearrange("(k p) c -> p k c", p=128)
    Wd = sb.alloc([128, 8, 768], BF16, "daW")
    Wd_r = Res()
    for k in range(0, 8, 2):
        P.dma("pool", Wd[:, k:k + 2, :], win[:, k:k + 2, OFF_DA:OFF_DA + 768], writes=[Wd_r])
    Wsw = sb.alloc([128, 8, 512], BF16, "daWs")
    Wsw_r = Res()
    for k in range(8):
        src = Wd[:, k, 0:512].rearrange("p (g t i) -> p g t i", t=2, i=16)
        dst = Wsw[:, k, :].rearrange("p (g t i) -> p g t i", t=2, i=16)
        P.op("dve", lambda e, src=src, dst=dst: e.tensor_scalar(out=dst[:, :, 0, :], in0=src[:, :, 1, :], scalar1=-1.0,
                                                                scalar2=None, op0=ALU.mult),
             reads=[Wd_r], writes=[Wsw_r])
        P.op("dve", lambda e, src=src, dst=dst: e.tensor_copy(out=dst[:, :, 1, :], in_=src[:, :, 0, :]),
             reads=[Wd_r], writes=[Wsw_r])
    rc = sb.alloc([128, L], BF16, "darc")
    rs = sb.alloc([128, L], BF16, "dars")
    idf = sb.alloc([128, 128], F32, "daidf")
    gsub = sb.alloc([128, 64], F32, "dagsub")
    lvt = sb.alloc([128, 128], F32, "dalv")
    cst_r, lv_r = Res(), Res()
    P.dma("sp", rc[:], c.C["ropec"][:, 0:L], writes=[cst_r])
    P.dma("sp", rs[:], c.C["ropes"][:, 0:L], writes=[cst_r])
    P.dma("sp", idf[:], c.C["identf"], writes=[cst_r])
    P.dma("sp", gsub[:], c.W["diff_subln_g"][li:li + 1, :].partition_broadcast(128), writes=[cst_r])
    P.dma("sp", lvt[:], c.W["diff_lambda"][li:li + 1].rearrange("o a b -> o (a b)").partition_broadcast(128),
          writes=[lv_r])
    P.op("dve", lambda e: e.tensor_scalar(out=gsub[:], in0=gsub[:], scalar1=1.0 - lam0, scalar2=None, op0=ALU.mult),
         reads=[cst_r], writes=[cst_r])
    lp = sb.alloc([128, 2, 32], F32, "dalp")
    lsum = sb.alloc([128, 2], F32, "dals")
    lam = sb.alloc([128, 1], F32, "dalam")
    lv4 = lvt[:].rearrange("p (a t i) -> p a t i", t=2, i=32)
    P.op("dve", lambda e: e.tensor_tensor(out=lp[:], in0=lv4[:, :, 0, :], in1=lv4[:, :, 1, :], op=ALU.mult),
         reads=[lv_r], writes=[lv_r])
    P.op("dve", lambda e: e.reduce_sum(out=lsum[:], in_=lp[:], axis=AX.X), reads=[lv_r], writes=[lv_r])
    P.op("act", lambda e: e.activation(out=lsum[:], in_=lsum[:], func=AF.Exp), reads=[lv_r], writes=[lv_r])
    P.op("dve", lambda e: e.tensor_tensor(out=lam[:], in0=lsum[:, 0:1], in1=lsum[:, 1:2], op=ALU.subtract),
         reads=[lv_r], writes=[lv_r])
    P.op("dve", lambda e: e.tensor_scalar(out=lam[:], in0=lam[:], scalar1=lam0, scalar2=None, op0=ALU.add),
         reads=[lv_r], writes=[lv_r])
    qT = sb.alloc([128, 2, L], BF16, "daq")
    kT = sb.alloc([128, 2, L], BF16, "dak")
    mask4 = sb.alloc([128, 4], F32, "dam4")
    P.dma("sp", mask4[:], c.C["mask4"], writes=[cst_r])
    qms = [(sb.alloc([128, 8, 512], BF16, "daqm"), Res()) for _ in range(2)]
    qk_r = _rr(nb)
    Vf = sb.alloc([128, nt, 4 * 65 + 64], BF16, "dav")
    V = Vf[:, :, 0:260].rearrange("p t (h d) -> p t h d", h=4)
    V_r = Res()
    P.op("pool", lambda e: e.memset(Vf[:, :, 260:324], 0.0), writes=[V_r])
    P.op("pool", lambda e: e.memset(V[:, :, :, 64:65], 1.0), writes=[V_r])
    tmps = [(sb.alloc([128, 512], F32, "dat1"), sb.alloc([128, 512], F32, "dat2"), Res()) for _ in range(2)]
    ei = 0
    for b in range(nb):
        for which, dstT in ((0, qT), (1, kT)):
            for ch in range(2):
                M = 128
                col = which * 256 + ch * 128
                bank = ei % 2
                ps1, ps2 = c.pd[bank][0:M, 0:512], c.pd[bank][0:M, 512:1024]
                pr = c.pdr[bank]
                t1, t2, tr = tmps[ei % 2]
                ei += 1

                def f(e, b=b, col=col, M=M, ps1=ps1, ps2=ps2):
                    ins = None
                    for k in range(8):
                        e.matmul(ps1, lhsT=Wd[:, k, col:col + M], rhs=hT[:, k, b * 512:(b + 1) * 512],
                                 start=(k == 0), stop=(k == 7))
                    for k in range(8):
                        ins = e.matmul(ps2, lhsT=Wsw[:, k, col:col + M], rhs=hT[:, k, b * 512:(b + 1) * 512],
                                       start=(k == 0), stop=(k == 7))
                    return ins
                P.op("pe", f, reads=hT_r[b * 4:b * 4 + 4] + [Wd_r, Wsw_r], writes=pr)
                P.op("dve", lambda e, t1=t1, ps1=ps1, M=M, b=b: e.tensor_tensor(
                    out=t1[0:M, :], in0=ps1, in1=rc[0:M, b * 512:(b + 1) * 512], op=ALU.mult),
                    reads=[pr[0], cst_r], writes=[tr])
                P.op("dve", lambda e, t2=t2, ps2=ps2, M=M, b=b: e.tensor_tensor(
                    out=t2[0:M, :], in0=ps2, in1=rs[0:M, b * 512:(b + 1) * 512], op=ALU.mult),
                    reads=[pr[1], cst_r], writes=[tr])
                P.op("dve", lambda e, t1=t1, t2=t2, M=M, dstT=dstT, ch=ch, b=b: e.tensor_tensor(
                    out=dstT[0:M, ch, b * 512:(b + 1) * 512], in0=t1[0:M, :], in1=t2[0:M, :], op=ALU.add),
                    reads=[tr], writes=[qk_r[b]])
    for j in range(nt):
        bank = 2 + j % 2
        ps, pr = c.pd[bank][:, 0:256], c.pdr[bank][0]

        def f(e, j=j, ps=ps):
            ins = None
            for k in range(8):
                ins = e.matmul(ps, lhsT=hT[:, k, j * 128:(j + 1) * 128], rhs=Wd[:, k, 512:768], start=(k == 0),
                               stop=(k == 7))
            return ins
        P.op("pe", f, reads=[hT_r[j], Wd_r], writes=[pr])
        evac(c, j, V[:, j, :, 0:64], ps.rearrange("p (h d) -> p h d", h=4), [pr], [V_r])
    Pts = [(sb.alloc([128, 1024], BF16, "daP"), Res()) for _ in range(2)]
    OT = [sb.alloc([65, 512], F32, "daOT") for _ in range(2)]
    OT_r = _rr(2)
    obuf = [(sb.alloc([128, 4, 256], BF16, "daO"), Res()) for _ in range(2)]
    r12 = sb.alloc([128, 2, 4], F32, "dar")
    ab = sb.alloc([128, 2, 4, 64], F32, "daab")
    dd = sb.alloc([128, 4, 64], F32, "dad")
    sq = sb.alloc([128, 4, 64], F32, "dasq")
    ssq = sb.alloc([128, 4], F32, "dassq")
    w_r = Res()
    scale = 32 ** -0.5
    units = [(qb, h, kt) for qb in range(nb) for h in range(4) for kt in range(nt)]
    acc = [c.pd[2][0:65, 0:512], c.pd[3][0:65, 0:512]]
    accf = [c.pd[2][:, 0:512], c.pd[3][:, 0:512]]
    acc_r = [c.pdr[2][0], c.pdr[3][0]]
    tp = [c.pd[2][:, 512:772].rearrange("p (t d) -> p t d", t=4),
          c.pd[3][:, 512:772].rearrange("p (t d) -> p t d", t=4)]
    tp_r = [c.pdr[2][1], c.pdr[3][1]]
    fill = [c.pd[2][:, 772:1024], c.pd[3][:, 772:1024]]
    fill_r = Res()
    NFILL = DA_NFILL
    deferred = []

    def u_s(i):
        qb, h, kt = units[i]
        S, S_r = c.pd[i % 2], c.pdr[i % 2]

        qm, qm_r = qms[qb % 2]

        def expand(qb2):
            qm2, qm2_r = qms[qb2 % 2]
            for hb in range(8):
                P.op("dve", lambda e, hb=hb: e.tensor_scalar(
                    out=qm2[:, hb, :], in0=qT[:, hb // 4, qb2 * 512:(qb2 + 1) * 512],
                    scalar1=mask4[:, hb % 4:hb % 4 + 1], scalar2=None, op0=ALU.mult),
                    reads=[qk_r[qb2], cst_r], writes=[qm2_r])
        if i == 0:
            expand(0)
        if h == 1 and kt == 0 and qb + 1 < nb:
            expand(qb + 1)

        def fs(e):
            ins = None
            for br in range(2):
                hb = 2 * h + br
                ins = e.matmul(S[:, br * 512:(br + 1) * 512], lhsT=kT[:, hb // 4, kt * 128:(kt + 1) * 128],
                               rhs=qm[:, hb, :], start=True, stop=True)
            return ins
        P.op("pe", fs, reads=[qm_r, qk_r[kt // 4]], writes=S_r)
        if NFILL:
            def ff(e):
                ins = None
                for k in range(NFILL):
                    ins = e.matmul(fill[k % 2], lhsT=Vf[:, kt, 0:128], rhs=qT[:, 0, 0:252], start=True, stop=True,
                                   skip_group_check=True)
                return ins
            P.op("pe", ff, reads=[V_r, qk_r[0]], writes=[fill_r], inc=False)
        for dfn in [d for d in deferred if d[0] <= i]:
            deferred.remove(dfn)
            dfn[1]()

    def u_e(i):
        S, S_r = c.pd[i % 2], c.pdr[i % 2]
        Pt, Pt_r = Pts[i % 2]
        P.op("act", lambda e: e.activation(out=Pt[:], in_=S[:], func=AF.Exp, scale=scale), reads=S_r, writes=[Pt_r])

    def u_o(i):
        qb, h, kt = units[i]
        Pt, Pt_r = Pts[i % 2]

        def fo(e):
            ins = None
            for br in range(2):
                ins = e.matmul(accf[br], lhsT=Vf[:, kt, h * 65:h * 65 + 128], rhs=Pt[:, br * 512:(br + 1) * 512],
                               start=(kt == 0), stop=(kt == nt - 1))
            return ins
        P.op("pe", fo, reads=[Pt_r, V_r], writes=acc_r)
        if kt == nt - 1:
            epilogue(i, qb, h)

    def epilogue(i, qb, h):
        ob, obr = obuf[qb % 2]
        for br in range(2):
            P.op("dve", lambda e, br=br: e.tensor_copy(out=OT[br][:], in_=acc[br]), reads=[acc_r[br]],
                 writes=[OT_r[br]])

        def rest():
            def ft(e):
                ins = None
                for br in range(2):
                    for qt in range(4):
                        ins = e.transpose(out=tp[br][:, qt, :], in_=OT[br][0:65, qt * 128:(qt + 1) * 128],
                                          identity=idf[0:65, 0:65])
                return ins
            P.op("pe", ft, reads=OT_r + [cst_r], writes=tp_r)
            for br in range(2):
                P.op("dve", lambda e, br=br: e.reciprocal(out=r12[:, br, :], in_=tp[br][:, :, 64]),
                     reads=[tp_r[br]], writes=[w_r])
            P.op("dve", lambda e: e.tensor_scalar(out=r12[:, 1, :], in0=r12[:, 1, :], scalar1=lam[:, 0:1], scalar2=None,
                                                  op0=ALU.mult), reads=[w_r, lv_r], writes=[w_r])
            for br in range(2):
                P.op("dve", lambda e, br=br: e.tensor_tensor(
                    out=ab[:, br], in0=tp[br][:, :, 0:64], in1=r12[:, br, :].unsqueeze(2).broadcast_to([128, 4, 64]),
                    op=ALU.mult), reads=[tp_r[br], w_r], writes=[w_r])
            P.op("dve", lambda e: e.tensor_tensor(out=dd[:], in0=ab[:, 0], in1=ab[:, 1], op=ALU.subtract),
                 reads=[w_r], writes=[w_r])
            P.op("dve", lambda e: e.tensor_tensor(out=sq[:], in0=dd[:], in1=dd[:], op=ALU.mult),
                 reads=[w_r], writes=[w_r])
            P.op("dve", lambda e: e.reduce_sum(out=ssq[:], in_=sq[:], axis=AX.X), reads=[w_r], writes=[w_r])
            P.op("dve", lambda e: e.tensor_scalar(out=ssq[:], in0=ssq[:], scalar1=1.0 / 64, scalar2=EPS, op0=ALU.mult,
                                                  op1=ALU.add), reads=[w_r], writes=[w_r])
            P.op("pool", lambda e: e.tensor_tensor(out=ssq[:], in0=ssq[:], in1=c.mhalf[:, 0:4], op=ALU.pow),
                 reads=[w_r, c.epst_r], writes=[w_r])
            P.op("dve", lambda e: e.tensor_tensor(out=dd[:], in0=dd[:],
                                                  in1=ssq[:].unsqueeze(2).broadcast_to([128, 4, 64]), op=ALU.mult),
                 reads=[w_r], writes=[w_r])
            P.op("dve", lambda e: e.tensor_tensor(
                out=ob[:, :, h * 64:(h + 1) * 64], in0=dd[:], in1=gsub[:].unsqueeze(1).broadcast_to([128, 4, 64]),
                op=ALU.mult), reads=[w_r, cst_r], writes=[obr])
            if h == 3:
                dst = c.Os[t0 + qb * 512:t0 + (qb + 1) * 512, 256:512].rearrange("(j p) c -> p j c", p=128)
                P.dma("sp", dst, ob[:], reads=[obr])
        deferred.append((i + 4, rest))
    pipeline(len(units), [u_s, u_e, u_o])
    for dfn in list(deferred):
        dfn[1]()


def mixer_ssm(c, li, t0, L, hT, hT_r):
    P, sb, nc = c.P, c.sb, c.nc
    nt, nb = L // 128, L // 512
    win = c.W["w_in"][li].rearrange("(k p) c -> p k c", p=128)
    Ws = sb.alloc([128, 8, 1032], BF16, "smW")
    Ws_r = Res()
    for k in range(0, 8, 2):
        P.dma("pool", Ws[:, k:k + 2, :], win[:, k:k + 2, OFF_SSM:OFF_SSM + 1032], writes=[Ws_r])
    cst_r = Res()
    tri = [sb.alloc([128, 128], F32, "smtri") for _ in range(2)]
    msk = [sb.alloc([128, 512], BF16, "smmsk") for _ in range(2)]
    onesf = sb.alloc([128, 128], F32, "smones")
    for d, nm in enumerate(("f", "b")):
        P.dma("sp", tri[d][:], c.C["tri_" + nm], writes=[cst_r])
        P.dma("sp", msk[d][:], c.C["mask_" + nm], writes=[cst_r])
    P.dma("sp", onesf[:], c.C["onesf"], writes=[cst_r])
    cw = sb.alloc([128, 6, 5], F32, "smcw")
    cb = sb.alloc([128, 6], F32, "smcb")
    dtb = sb.alloc([128, 8], F32, "smdtb")
    alog = sb.alloc([128, 8], F32, "smalog")
    Dv = sb.alloc([128, 4], F32, "smD")
    ng = sb.alloc([128, 256], F32, "smng")
    for j in range(5):
        P.dma("sp", cw[:, :, j], c.W["ssm_conv_w"][li, j].rearrange("(c p) -> p c", p=128), writes=[cst_r], slow=True)
    P.dma("sp", cb[:], c.W["ssm_conv_b"][li].rearrange("(c p) -> p c", p=128), writes=[cst_r], slow=True)
    P.dma("sp", dtb[:], c.W["ssm_dt_bias"][li:li + 1].rearrange("o a b -> o (a b)").partition_broadcast(128),
          writes=[cst_r])
    P.dma("sp", alog[:], c.W["ssm_A_log"][li:li + 1].rearrange("o a b -> o (a b)").partition_broadcast(128),
          writes=[cst_r])
    P.dma("sp", Dv[:], c.W["ssm_D"][li:li + 1, :].partition_broadcast(128), writes=[cst_r])
    P.dma("sp", ng[:], c.W["ssm_norm_g"][li:li + 1, :].partition_broadcast(128), writes=[cst_r])
    P.op("act", lambda e: e.activation(out=alog[:], in_=alog[:], func=AF.Exp), reads=[cst_r], writes=[cst_r])
    P.op("dve", lambda e: e.tensor_scalar(out=alog[:], in0=alog[:], scalar1=-1.0, scalar2=None, op0=ALU.mult),
         reads=[cst_r], writes=[cst_r])
    xcT = sb.alloc([128, 6, L], BF16, "smxc")
    xc_r = Res()
    dtv = sb.alloc([128, nt, 8], F32, "smdt")
    adt = sb.alloc([128, nt, 8], F32, "smadt")
    dt_r = Res()
    yf = sb.alloc([128, nt, 256], BF16, "smyf")
    yf_r = _rr(nt)
    psd = c.pd[3][:, 512:512 + nt * 8].rearrange("p (t e) -> p t e", e=8)

    def fdt(e):
        ins = None
        for j in range(nt):
            for k in range(8):
                ins = e.matmul(psd[:, j, :], lhsT=hT[:, k, j * 128:(j + 1) * 128], rhs=Ws[:, k, 1024:1032],
                               start=(k == 0), stop=(k == 7), skip_group_check=True)
        return ins
    P.op("pe", fdt, reads=hT_r + [Ws_r], writes=[c.pdr[3][1]])
    P.op("dve", lambda e: e.tensor_tensor(out=dtv[:], in0=psd, in1=dtb[:].unsqueeze(1).broadcast_to([128, nt, 8]),
                                          op=ALU.add), reads=[c.pdr[3][1], cst_r], writes=[dt_r])
    P.op("act", lambda e: e.activation(out=dtv[:], in_=dtv[:], func=AF.Exp), reads=[dt_r], writes=[dt_r])
    P.op("act", lambda e: e.activation(out=dtv[:], in_=dtv[:], func=AF.Ln, bias=1.0), reads=[dt_r], writes=[dt_r])
    P.op("dve", lambda e: e.tensor_tensor(out=adt[:], in0=dtv[:], in1=alog[:].unsqueeze(1).broadcast_to([128, nt, 8]),
                                          op=ALU.mult), reads=[dt_r, cst_r], writes=[dt_r])
    szall = sb.alloc([128, nt, 256], BF16, "smsz")
    sz_r = Res()
    for j in range(nt):
        bank = j % 3
        ps, pr = c.pd[bank][:, 512:768], c.pdr[bank][1]

        def fz(e, j=j, ps=ps):
            ins = None
            for k in range(8):
                ins = e.matmul(ps, lhsT=hT[:, k, j * 128:(j + 1) * 128], rhs=Ws[:, k, 0:256], start=(k == 0),
                               stop=(k == 7))
            return ins
        P.op("pe", fz, reads=[hT_r[j], Ws_r], writes=[pr])
        P.op("act", lambda e, j=j, ps=ps: e.activation(out=szall[:, j, :], in_=ps, func=AF.Silu), reads=[pr],
             writes=[sz_r])
    ov = sb.off
    idf = sb.alloc([128, 128], F32, "smidf")
    idf_r = Res()
    P.dma("sp", idf[:], c.C["identf"], writes=[idf_r])
    Dg = sb.alloc([128, 6, 5, 128], BF16, "smDg")
    Dg_r = Res()
    for ch in range(6):
        for j in range(5):
            P.op("dve", lambda e, ch=ch, j=j: e.tensor_scalar(out=Dg[:, ch, j, :], in0=idf[:],
                                                               scalar1=cw[:, ch, j:j + 1], scalar2=None, op0=ALU.mult),
                 reads=[idf_r, cst_r], writes=[Dg_r])
    pres = [(sb.alloc([128, L + 4], BF16, "smpre"), Res()) for _ in range(2)]
    for pre, pre_r in pres:
        P.op("pool", lambda e, pre=pre: e.memset(pre[:, 0:2], 0.0), writes=[pre_r])
        P.op("pool", lambda e, pre=pre: e.memset(pre[:, L + 2:L + 4], 0.0), writes=[pre_r])
    for ch in range(6):
        pre, pre_r = pres[ch % 2]
        for b in range(nb):
            bank = b % 3
            ps, pr = c.pd[bank][:, 0:512], c.pdr[bank][0]

            def f(e, ch=ch, b=b, ps=ps):
                ins = None
                for k in range(8):
                    ins = e.matmul(ps, lhsT=Ws[:, k, 256 + ch * 128:256 + (ch + 1) * 128],
                                   rhs=hT[:, k, b * 512:(b + 1) * 512], start=(k == 0), stop=(k == 7))
                return ins
            P.op("pe", f, reads=hT_r[b * 4:b * 4 + 4] + [Ws_r], writes=[pr])
            evac(c, b, pre[:, 2 + b * 512:2 + (b + 1) * 512], ps, [pr], [pre_r])
        for b in range(nb):
            ps, pr = c.pd[3][:, (b % 2) * 512:(b % 2 + 1) * 512], c.pdr[3][b % 2]

            def fc(e, ch=ch, b=b, ps=ps, pre=pre):
                ins = None
                for j in range(5):
                    ins = e.matmul(ps, lhsT=Dg[:, ch, j, :], rhs=pre[:, b * 512 + j:b * 512 + j + 512],
                                   start=(j == 0), stop=(j == 4))
                return ins
            P.op("pe", fc, reads=[pre_r, Dg_r], writes=[pr])
            P.op("act", lambda e, ch=ch, b=b, ps=ps: e.activation(out=xcT[:, ch, b * 512:(b + 1) * 512], in_=ps,
                                                                   func=AF.Silu, bias=cb[:, ch:ch + 1]),
                 reads=[pr, cst_r], writes=[xc_r])
    P.barrier()
    sb.off = ov
    hst = sb.alloc([128, 2, 256], F32, "smh")
    hbf = sb.alloc([128, 2, 256], BF16, "smhb")
    h_r = [Res(), Res()]
    obuf = [(sb.alloc([128, 4, 256], BF16, "smO"), Res()) for _ in range(2)]
    ocnt = [0] * (nt // 4)
    for d in range(2):
        P.op("pool", lambda e, d=d: e.memset(hst[:, d, :], 0.0), writes=[h_r[d]])
        P.op("pool", lambda e, d=d: e.memset(hbf[:, d, :], 0.0), writes=[h_r[d]])

    class B_:
        pass
    Bs = []
    for d in range(2):
        b_ = B_()
        b_.xb = sb.alloc([128, 4, 128], BF16, "smxb")
        b_.xtm = b_.xb[:, 0:2, :].rearrange("p k t -> p (k t)")
        b_.btm = b_.xb[:, 2:4, :].rearrange("p k t -> p (k t)")
        b_.rhsA = sb.alloc([128, 4, 128], F32, "smrA")
        b_.dec = sb.alloc([128, 4, 128], F32, "smdec")
        b_.MT = sb.alloc([128, 4, 128], BF16, "smMT")
        b_.sm4 = sb.alloc([128, 6, 4], F32, "sm4")
        b_.xdt = sb.alloc([128, 4, 64], BF16, "smxdt")
        b_.xdd = sb.alloc([128, 4, 64], BF16, "smxdd")
        b_.yt = sb.alloc([128, 4, 64], F32, "smyt")
        b_.y2 = sb.alloc([128, 256], F32, "smy2")
        b_.sq = sb.alloc([128, 256], F32, "smsq")
        b_.s2 = sb.alloc([128, 2], F32, "sms2")
        b_.tm_r, b_.w_r, b_.a_r, b_.m_r, b_.x_r, b_.c_r = Res(), Res(), Res(), Res(), Res(), Res()
        P0, P1 = c.pd[2 * d], c.pd[2 * d + 1]
        b_.accBf = P0[:, 0:512]
        b_.accB = P0[:, 0:512].rearrange("p (h l) -> p h l", h=4)
        b_.accB_r = c.pdr[2 * d][0]
        b_.cbT = P0[:, 512:768].rearrange("p (g l) -> p g l", g=2)
        b_.tpp = P0[:, 768:1024].bitcast(BF16).rearrange("p (k t) -> p k t", k=4)
        b_.p0b_r = c.pdr[2 * d][1]
        b_.yd = P1[:, 0:256].rearrange("p (h d) -> p h d", h=4)
        b_.yo = P1[:, 256:512].rearrange("p (h d) -> p h d", h=4)
        b_.y_r = c.pdr[2 * d + 1][0]
        b_.stp = P1[:, 512:768].rearrange("p (h d) -> p h d", h=4)
        b_.acol = P1[:, 768:772]
        b_.st_r = c.pdr[2 * d + 1][1]
        Bs.append(b_)

    def chunk(d, i):
        b_ = Bs[d]
        cidx = i if d == 0 else nt - 1 - i
        first = i < nt // 2
        last = 127 if d == 0 else 0
        cs = slice(cidx * 128, (cidx + 1) * 128)
        xtm, btm, rhsA, dec, MT, sm4, xdt, xdd, yt, y2, sq, s2 = (b_.xtm, b_.btm, b_.rhsA, b_.dec, b_.MT, b_.sm4, b_.xdt,
                                                                  b_.xdd, b_.yt, b_.y2, b_.sq, b_.s2)

        def ftp(e):
            ins = None
            for k in range(4):
                ins = e.transpose(out=b_.tpp[:, k, :], in_=xcT[:, k, cs], identity=c.ident[:])
            return ins
        P.op("pe", ftp, reads=[xc_r, c.ident_r], writes=[b_.p0b_r])
        P.op("act", lambda e: e.copy(out=b_.xb[:], in_=b_.tpp), reads=[b_.p0b_r], writes=[b_.tm_r])
        av = adt[:, cidx, d * 4:(d + 1) * 4]
        P.op("pool", lambda e: e.tensor_tensor(
            out=rhsA[:], in0=tri[d][:].unsqueeze(1).broadcast_to([128, 4, 128]),
            in1=av.unsqueeze(2).broadcast_to([128, 4, 128]), op=ALU.mult), reads=[dt_r, cst_r], writes=[b_.a_r])

        def facc(e):
            e.matmul(b_.accBf, lhsT=onesf[:], rhs=rhsA[:].rearrange("p h l -> p (h l)"), start=True,
                     stop=False, skip_group_check=True)
            e.matmul(b_.accBf, lhsT=c.ident[:], rhs=msk[d][:], start=False, stop=True, skip_group_check=True)
            return e.matmul(b_.acol, lhsT=tri[d][:], rhs=av, start=True, stop=True, skip_group_check=True)
        P.op("pe", facc, reads=[b_.a_r, cst_r, dt_r, c.ident_r], writes=[b_.accB_r, b_.st_r])

        def fcb(e):
            ins = None
            for g in range(2):
                ins = e.matmul(b_.cbT[:, g, :], lhsT=xcT[:, 2 + g, cs], rhs=xcT[:, 4 + g, cs], start=True, stop=True,
                               skip_group_check=True)
            return ins
        P.op("pe", fcb, reads=[xc_r], writes=[b_.p0b_r])
        P.op("dve", lambda e: e.tensor_scalar(out=sm4[:, 0, :], in0=b_.acol, scalar1=-1.0, scalar2=None, op0=ALU.mult),
             reads=[b_.st_r], writes=[b_.w_r])
        P.op("act", lambda e: e.activation(out=sm4[:, 1, :], in_=b_.acol, func=AF.Exp), reads=[b_.st_r],
             writes=[b_.w_r])
        P.op("dve", lambda e: e.tensor_tensor(out=sm4[:, 4, :], in0=b_.accB[:, :, last], in1=sm4[:, 0, :], op=ALU.add),
             reads=[b_.accB_r, b_.w_r], writes=[b_.w_r])
        P.op("act", lambda e: e.activation(out=sm4[:, 2, :], in_=sm4[:, 4, :], func=AF.Exp), reads=[b_.w_r],
             writes=[b_.w_r])
        P.op("act", lambda e: e.activation(out=sm4[:, 3, :], in_=b_.accB[:, :, last], func=AF.Exp),
             reads=[b_.accB_r], writes=[b_.w_r])
        P.op("dve", lambda e: e.tensor_tensor(out=dec[:], in0=b_.accB,
                                              in1=sm4[:, 0, :].unsqueeze(2).broadcast_to([128, 4, 128]), op=ALU.add),
             reads=[b_.accB_r, b_.w_r], writes=[b_.m_r])
        P.op("act", lambda e: e.activation(out=dec[:], in_=dec[:], func=AF.Exp), reads=[b_.m_r], writes=[b_.m_r])
        P.op("dve", lambda e: e.tensor_tensor(
            out=MT[:].rearrange("p (g k) l -> p g k l", g=2),
            in0=b_.cbT.unsqueeze(2).broadcast_to([128, 2, 2, 128]),
            in1=dec[:].rearrange("p (g k) l -> p g k l", g=2), op=ALU.mult), reads=[b_.p0b_r, b_.m_r],
            writes=[b_.m_r])
        dv = dtv[:, cidx, d * 4:(d + 1) * 4]
        P.op("pool", lambda e: e.tensor_tensor(
            out=xdt[:], in0=xtm.rearrange("p (h d) -> p h d", h=4),
            in1=dv.unsqueeze(2).broadcast_to([128, 4, 64]), op=ALU.mult), reads=[b_.tm_r, dt_r], writes=[b_.x_r])
        P.op("pool", lambda e: e.tensor_tensor(
            out=xdd[:], in0=xdt[:], in1=sm4[:, 2, :].unsqueeze(2).broadcast_to([128, 4, 64]), op=ALU.mult),
            reads=[b_.x_r, b_.w_r], writes=[b_.x_r])

        def fy(e):
            ins = None
            for h in range(4):
                ins = e.matmul(b_.yd[:, h, :], lhsT=MT[:, h, :], rhs=xdt[:, h, :], start=True, stop=True,
                               skip_group_check=True)
            for g in range(2):
                ins = e.matmul(b_.yo[:, 2 * g:2 * g + 2, :], lhsT=xcT[:, 4 + g, cs],
                               rhs=hbf[:, d, g * 128:(g + 1) * 128], start=True, stop=True, skip_group_check=True)
            return ins
        P.op("pe", fy, reads=[b_.m_r, b_.x_r, xc_r, h_r[d]], writes=[b_.y_r])

        def fst(e):
            ins = None
            for g in range(2):
                ins = e.matmul(b_.stp[:, 2 * g:2 * g + 2, :], lhsT=btm[:, g * 128:(g + 1) * 128],
                               rhs=xdd[:, 2 * g:2 * g + 2, :], start=True, stop=True, skip_group_check=True)
            return ins
        P.op("pe", fst, reads=[b_.tm_r, b_.x_r], writes=[b_.st_r])
        hv = hst[:, d, :].rearrange("p (h e) -> p h e", h=4)
        P.op("dve", lambda e: e.tensor_tensor(out=hv, in0=hv, in1=sm4[:, 3, :].unsqueeze(2).broadcast_to([128, 4, 64]),
                                              op=ALU.mult), reads=[b_.w_r, h_r[d], b_.y_r], writes=[h_r[d]])
        P.op("dve", lambda e: e.tensor_tensor(out=hv, in0=b_.stp, in1=hv, op=ALU.add), reads=[b_.st_r, h_r[d]],
             writes=[h_r[d]])
        P.op("act", lambda e: e.copy(out=hbf[:, d, :], in_=hst[:, d, :]), reads=[h_r[d]], writes=[h_r[d]])
        P.op("dve", lambda e: e.tensor_tensor(out=yt[:], in0=b_.yo,
                                              in1=sm4[:, 1, :].unsqueeze(2).broadcast_to([128, 4, 64]), op=ALU.mult),
             reads=[b_.y_r, b_.w_r], writes=[b_.c_r])
        if first:
            P.op("dve", lambda e: e.tensor_tensor(
                out=yf[:, cidx, :].rearrange("p (h d) -> p h d", h=4), in0=b_.yd, in1=yt[:], op=ALU.add),
                reads=[b_.y_r, b_.c_r], writes=[yf_r[cidx]])
            return
        y2v = y2[:].rearrange("p (h d) -> p h d", h=4)
        P.op("dve", lambda e: e.tensor_tensor(out=y2v, in0=b_.yd, in1=yt[:], op=ALU.add), reads=[b_.y_r, b_.c_r],
             writes=[b_.c_r])
        P.op("dve", lambda e: e.tensor_tensor(out=y2[:], in0=y2[:], in1=yf[:, cidx, :], op=ALU.add),
             reads=[b_.c_r, yf_r[cidx]], writes=[b_.c_r])
        P.op("dve", lambda e: e.tensor_tensor(
            out=yt[:], in0=xtm.rearrange("p (h d) -> p h d", h=4),
            in1=Dv[:].unsqueeze(2).broadcast_to([128, 4, 64]), op=ALU.mult), reads=[b_.tm_r, cst_r, b_.c_r],
            writes=[b_.c_r])
        P.op("dve", lambda e: e.tensor_tensor(out=y2v, in0=y2v, in1=yt[:], op=ALU.add), reads=[b_.c_r],
             writes=[b_.c_r])
        P.op("dve", lambda e: e.tensor_tensor(out=y2[:], in0=y2[:], in1=szall[:, cidx, :], op=ALU.mult),
             reads=[b_.c_r, sz_r], writes=[b_.c_r])
        P.op("dve", lambda e: e.tensor_tensor(out=sq[:], in0=y2[:], in1=y2[:], op=ALU.mult), reads=[b_.c_r],
             writes=[b_.c_r])
        P.op("dve", lambda e: e.reduce_sum(out=s2[:], in_=sq[:].rearrange("p (g d) -> p g d", g=2), axis=AX.X),
             reads=[b_.c_r], writes=[b_.c_r])
        P.op("dve", lambda e: e.tensor_scalar(out=s2[:], in0=s2[:], scalar1=1.0 / 128, scalar2=EPS, op0=ALU.mult,
                                              op1=ALU.add), reads=[b_.c_r], writes=[b_.c_r])
        P.op("pool", lambda e: e.tensor_tensor(out=s2[:], in0=s2[:], in1=c.mhalf[:, 0:2], op=ALU.pow),
             reads=[b_.c_r, c.epst_r], writes=[b_.c_r])
        P.op("dve", lambda e: e.tensor_tensor(
            out=y2[:].rearrange("p (g d) -> p g d", g=2), in0=y2[:].rearrange("p (g d) -> p g d", g=2),
            in1=s2[:].unsqueeze(2).broadcast_to([128, 2, 128]), op=ALU.mult), reads=[b_.c_r], writes=[b_.c_r])
        bq = cidx // 4
        ob, obr = obuf[d]
        P.op("pool", lambda e: e.tensor_tensor(out=ob[:, cidx % 4, :], in0=y2[:], in1=ng[:], op=ALU.mult),
             reads=[b_.c_r, cst_r], writes=[obr])
        ocnt[bq] += 1
        if ocnt[bq] == 4:
            dst = c.Os[t0 + bq * 512:t0 + (bq + 1) * 512, 768:1024].rearrange("(j p) c -> p j c", p=128)
            P.dma("sp", dst, ob[:], reads=[obr])
    for i in range(nt):
        la = P.capture(lambda: chunk(0, i))
        lb = P.capture(lambda: chunk(1, i))
        P.replay([la, lb])


MIXERS = {"fn": mixer_fn, "na": mixer_na, "da": mixer_da, "ssm": mixer_ssm}


SEQ_LENS = (4096, 2048, 2048)
_CACHE = {}


def kernel(**inputs):
    xp = np.ascontiguousarray(inputs["x_prompt"], dtype=np.float32)
    xs = np.ascontiguousarray(inputs["x_sample"], dtype=np.float32)
    if "nc" not in _CACHE:
        _CACHE["nc"] = build(SEQ_LENS)
        _CACHE["consts"] = make_consts(SEQ_LENS)
    nc = _CACHE["nc"]
    consts = _CACHE["consts"]
    wts = {k: np.ascontiguousarray(inputs[k], dtype=np.float32) for k in WEIGHT_SHAPES}
    in_maps = []
    for i in range(8):
        xin = np.concatenate([xs[i], xp[2 * i], xp[2 * i + 1]], axis=0)
        m = {"xin": xin}
        m.update(wts)
        m.update(consts)
        in_maps.append(m)
    res = run_bass_kernel_spmd(nc, in_maps, core_ids=list(range(8)))
    yp = np.empty_like(xp)
    ys = np.empty_like(xs)
    for i in range(8):
        yy = res.results[i]["y"]
        ys[i] = yy[0:4096]
        yp[2 * i] = yy[4096:6144]
        yp[2 * i + 1] = yy[6144:8192]
    return (yp, ys)
```

```python
import math
import numpy as np
import ml_dtypes
import concourse.bass as bass
import concourse.mybir as mybir
from concourse.bass_utils import run_bass_kernel_spmd

F32 = mybir.dt.float32
BF16 = mybir.dt.bfloat16
AF = mybir.ActivationFunctionType
ALU = mybir.AluOpType
AX = mybir.AxisListType

D = 1024
DEPTH = 4
GW = 256
D_IN = 2824
D_FF = 2816
NFF = D_FF // 128
OFF_FN, OFF_DA, OFF_NA, OFF_SSM = 0, 256, 1024, 1792
EPS = 1e-6
NEG = -240000.0
SBUF_BASE = 16640
SBUF_LIMIT = 229376
NRING = 12
DA_NFILL = 0


class Res:
    __slots__ = ("w", "r")

    def __init__(self):
        self.w = {}
        self.r = {}


def _merge(dst, src):
    for k, v in src.items():
        if dst.get(k, 0) < v:
            dst[k] = v


class Prog:
    ENG = ("pe", "act", "dve", "pool", "sp")

    def __init__(self, nc, esems, rings):
        self.nc = nc
        self.h = {"pe": nc.tensor, "act": nc.scalar, "dve": nc.vector, "pool": nc.gpsimd, "sp": nc.sync}
        self.esem = esems
        self.cnt = {e: 0 for e in self.ENG}
        self.seen = {e: {} for e in self.ENG}
        self.lists = {e: [] for e in self.ENG}
        self.ring = rings
        self.ringcnt = {q: [0] * len(rings[q]) for q in rings}
        self.ringpos = {q: 0 for q in rings}
        self.ninst = 0
        self._cap = None

    def capture(self, fn):
        old = self._cap
        self._cap = []
        fn()
        ops, self._cap = self._cap, old
        return ops

    def replay(self, lists, weights=None):
        idx = [0] * len(lists)
        weights = weights or [1] * len(lists)
        while any(idx[k] < len(lists[k]) for k in range(len(lists))):
            for k in range(len(lists)):
                for _ in range(weights[k]):
                    if idx[k] < len(lists[k]):
                        kind, args, kw = lists[k][idx[k]]
                        idx[k] += 1
                        (self.op if kind == "op" else self.dma)(*args, **kw)

    def _waits(self, e, toks):
        out = []
        seen = self.seen[e]
        for k, v in toks.items():
            if e == "pe" and k == self.esem["pe"]:
                continue
            if seen.get(k, 0) < v:
                seen[k] = v
                out.append((k, v))
        return out

    def op(self, e, fn, reads=(), writes=(), inc=True):
        if self._cap is not None:
            self._cap.append(("op", (e, fn), dict(reads=reads, writes=writes, inc=inc)))
            return
        toks = {}
        for r in reads:
            _merge(toks, r.w)
        for w in writes:
            _merge(toks, w.w)
            _merge(toks, w.r)
        waits = self._waits(e, toks)
        sem = self.esem[e]
        if inc:
            self.cnt[e] += 1
            val = self.cnt[e]
        else:
            val = self.cnt[e] + 1
        for r in reads:
            if r.r.get(sem, 0) < val:
                r.r[sem] = val
        for w in writes:
            if w.w.get(sem, 0) < val:
                w.w[sem] = val
        self.lists[e].append((waits, fn, (sem, 1) if inc else None))
        self.ninst += 1

    def dma(self, q, out, in_, reads=(), writes=(), slow=False):
        if self._cap is not None:
            self._cap.append(("dma", (q, out, in_), dict(reads=reads, writes=writes, slow=slow)))
            return
        k = self.ringpos[q]
        self.ringpos[q] = (k + 1) % len(self.ring[q])
        sem = self.ring[q][k]
        toks = {}
        for r in reads:
            _merge(toks, r.w)
        for w in writes:
            _merge(toks, w.w)
            _merge(toks, w.r)
        prev = self.ringcnt[q][k]
        if prev > 0:
            toks[sem] = max(toks.get(sem, 0), 16 * prev)
        waits = self._waits(q, toks)
        self.ringcnt[q][k] += 1
        val = 16 * self.ringcnt[q][k]
        for r in reads:
            if r.r.get(sem, 0) < val:
                r.r[sem] = val
        for w in writes:
            if w.w.get(sem, 0) < val:
                w.w[sem] = val
        self.lists[q].append((waits, lambda eng: eng.dma_start(out=out, in_=in_, allow_slow_non_contiguous=slow), (sem, 16)))
        self.ninst += 1

    def barrier(self):
        toks = {}
        for e in self.ENG:
            if self.cnt[e] > 0:
                toks[self.esem[e]] = self.cnt[e]
        for q in self.ring:
            for k, s in enumerate(self.ring[q]):
                if self.ringcnt[q][k] > 0:
                    toks[s] = 16 * self.ringcnt[q][k]
        for e in self.ENG:
            waits = self._waits(e, dict(toks))
            if waits:
                self.lists[e].append((waits, None, None))

    def emit(self, e, eng):
        for waits, fn, inc in self.lists[e]:
            for s, v in waits:
                eng.wait_ge(s, v)
            if fn is not None:
                ins = fn(eng)
                if inc is not None:
                    ins.then_inc(inc[0], inc[1])


class SB:
    def __init__(self, nc):
        self.nc = nc
        self.off = SBUF_BASE
        self.n = 0

    def alloc(self, shape, dt, name="t"):
        esz = 2 if dt == BF16 else 4
        nb = esz
        for s in shape[1:]:
            nb *= s
        nb = (nb + 31) // 32 * 32
        assert self.off + nb <= SBUF_LIMIT, f"SBUF overflow {name}: {self.off}+{nb}"
        self.n += 1
        t = self.nc.alloc_sbuf_tensor_at(f"{name}{self.n}", list(shape), dt, offset=self.off)
        self.off += nb
        return t


def _rr(n):
    return [Res() for _ in range(n)]


def pipeline(n, stages, skew=1):
    ns = len(stages)
    for step in range(n + (ns - 1) * skew):
        for s_ in range(ns - 1, -1, -1):
            i = step - s_ * skew
            if 0 <= i < n:
                stages[s_](i)


WEIGHT_SHAPES = {
    "attn_norm_g": (DEPTH, D), "w_in": (DEPTH, D, D_IN), "w_fourier": (DEPTH, GW, GW),
    "diff_lambda": (DEPTH, 4, 32), "diff_subln_g": (DEPTH, 64), "na_rpb": (DEPTH, 4, 15, 31),
    "ssm_conv_w": (DEPTH, 5, 768), "ssm_conv_b": (DEPTH, 768), "ssm_dt_bias": (DEPTH, 2, 4),
    "ssm_A_log": (DEPTH, 2, 4), "ssm_D": (DEPTH, 4), "ssm_norm_g": (DEPTH, GW),
    "w_out": (DEPTH, D, D), "ffn_norm_g": (DEPTH, D), "w_gate": (DEPTH, D, D_FF),
    "w_up": (DEPTH, D, D_FF), "w_down": (DEPTH, D_FF, D), "final_norm_g": (D,),
}


class Ctx:
    pass


def build(seq_lens, depth=DEPTH, mix=("fn", "da", "na", "ssm"), do_ffn=True):
    from contextlib import ExitStack
    T = sum(seq_lens)
    nc = bass.Bass("TRN2", target_bir_lowering=False)
    c = Ctx()
    c.nc, c.T, c.seq_lens, c.depth, c.mix = nc, T, seq_lens, depth, mix

    def din(name, shape, dt=F32):
        return nc.dram_tensor(name, list(shape), dt, kind="ExternalInput").ap()

    c.xin = din("xin", [T, D])
    c.W = {name: din(name, shape) for name, shape in WEIGHT_SHAPES.items()}
    c.C = {name: din(name, shape, dt) for name, (shape, dt) in const_specs(seq_lens).items()}
    c.y = nc.dram_tensor("y", [T, D], F32, kind="ExternalOutput").ap()
    c.rpad = nc.dram_tensor("rpad_scr", [128 + 1860 + 128], F32)
    c.winb = nc.dram_tensor("winb_scr", [depth, D, D_IN], BF16).ap()
    c.woutb = nc.dram_tensor("woutb_scr", [depth, D, D], BF16).ap()
    c.winb_r = [Res() for _ in range(depth)]
    import os as _os
    c.Xs = nc.dram_tensor("xs_scr", [T, D], F32, kind=("ExternalOutput" if _os.environ.get("KDEBUG") else "Internal")).ap()
    import os as _os
    c.Os = nc.dram_tensor("os_scr", [T, D], BF16, kind=("ExternalOutput" if _os.environ.get("KDEBUG") else "Internal")).ap()

    with ExitStack() as es:
        esems = {e: es.enter_context(nc.semaphore(f"s_{e}")) for e in Prog.ENG}
        rings = {q: [es.enter_context(nc.semaphore(f"r_{q}{i}")) for i in range(NRING)] for q in ("sp", "pool")}
        P = Prog(nc, esems, rings)
        c.P = P
        c.sb = SB(nc)
        c.pd = [nc.alloc_psum_tensor(f"pd{i}", [128, 1024], F32) for i in range(4)]
        c.pdr = [[Res(), Res()] for _ in range(4)]
        record(c, do_ffn)
        blk = es.enter_context(nc.Block())

        @blk.tensor
        def _(e):
            P.emit("pe", e)

        @blk.scalar
        def _(e):
            P.emit("act", e)

        @blk.vector
        def _(e):
            P.emit("dve", e)

        @blk.gpsimd
        def _(e):
            P.emit("pool", e)

        @blk.sync
        def _(e):
            P.emit("sp", e)
    return nc


def const_specs(seq_lens):
    s = {"ident": ([128, 128], BF16), "cs64": ([256, 512], BF16), "dlo": ([128, 128], BF16),
         "dhi": ([128, 128], BF16), "negt": ([128, 128], BF16), "na_wm": ([128, 64], F32), "identf": ([128, 128], F32),
         "ropec": ([128, max(seq_lens)], BF16), "ropes": ([128, max(seq_lens)], BF16),
         "mask4": ([128, 4], F32), "mask2": ([128, 2], F32), "tri_f": ([128, 128], F32), "tri_b": ([128, 128], F32), "onesf": ([128, 128], F32),
         "mask_f": ([128, 512], BF16), "mask_b": ([128, 512], BF16)}
    for L in sorted(set(seq_lens)):
        s[f"dftc{L}"] = ([L, L], BF16)
        s[f"dfts{L}"] = ([L, L], BF16)
        s[f"alt{L}"] = ([128, 1], BF16)
    return s


def make_consts(seq_lens):
    bf = ml_dtypes.bfloat16
    out = {"ident": np.eye(128, dtype=np.float32).astype(bf)}
    dlo = np.zeros((128, 128), np.float32)
    dlo[np.arange(64), np.arange(64)] = 1.0
    dhi = np.zeros((128, 128), np.float32)
    dhi[np.arange(64, 128), np.arange(64, 128)] = 1.0
    out["dlo"], out["dhi"] = dlo.astype(bf), dhi.astype(bf)
    out["negt"] = np.full((128, 128), NEG, np.float32).astype(bf)
    wm = np.zeros((128, 64), np.float32)
    for p in range(128):
        col = p % 64
        for qp in range(64):
            q = 63 - qp
            sc = min(max(q - 8, 0), 48)
            wm[p, qp] = 0.0 if sc <= col < sc + 16 else NEG / 8.0
    out["na_wm"] = wm
    out["identf"] = np.eye(128, dtype=np.float32)
    Lm = max(seq_lens)
    fi = (np.arange(128) % 16).astype(np.float64)
    inv = np.power(10000.0, -fi / 16.0)
    ra = np.arange(Lm, dtype=np.float64)[None, :] * inv[:, None]
    out["ropec"] = np.cos(ra).astype(np.float32).astype(bf)
    out["ropes"] = np.sin(ra).astype(np.float32).astype(bf)
    out["mask2"] = (np.arange(128)[:, None] // 64 == np.arange(2)[None, :]).astype(np.float32)
    out["mask4"] = (np.arange(128)[:, None] // 32 == np.arange(4)[None, :]).astype(np.float32)
    jj, ll = np.arange(128)[:, None], np.arange(128)[None, :]
    out["tri_f"] = (jj <= ll).astype(np.float32)
    out["tri_b"] = (jj >= ll).astype(np.float32)
    out["onesf"] = np.ones((128, 128), np.float32)
    out["mask_f"] = np.tile(np.where(ll < jj, NEG, 0.0).astype(np.float32), (1, 4)).astype(bf)
    out["mask_b"] = np.tile(np.where(ll > jj, NEG, 0.0).astype(np.float32), (1, 4)).astype(bf)
    i64 = np.arange(64)
    ang = 2.0 * np.pi * ((i64[:, None] * i64[None, :]) % 64) / 64.0
    c64, s64 = np.cos(ang) / 8.0, np.sin(ang) / 8.0
    cs = np.zeros((256, 512), np.float64)
    for g in range(4):
        cs[g * 64:(g + 1) * 64, g * 64:(g + 1) * 64] = c64
        cs[g * 64:(g + 1) * 64, 256 + g * 64:256 + (g + 1) * 64] = -s64
    out["cs64"] = cs.astype(np.float32).astype(bf)
    for L in sorted(set(seq_lens)):
        il = np.arange(L, dtype=np.int64)
        a = 2.0 * np.pi * ((il[:, None] * il[None, :]) % L).astype(np.float64) / L
        out[f"dftc{L}"] = (np.cos(a) / np.sqrt(L)).astype(np.float32).astype(bf)
        out[f"dfts{L}"] = (np.sin(a) / np.sqrt(L)).astype(np.float32).astype(bf)
        out[f"alt{L}"] = (((-1.0) ** np.arange(128)) / np.sqrt(L)).astype(np.float32).reshape(128, 1).astype(bf)
    return out


def wload(c, q, dst, src, res):
    c.P.dma(q, dst, src, writes=[res])


def rms_rstd(c, ss, rstd, n, res_ss, res_rstd):
    P = c.P
    P.op("dve", lambda e: e.tensor_scalar(out=rstd[:, 0:n], in0=ss[:, 0:n], scalar1=1.0 / D, scalar2=EPS,
                                           op0=ALU.mult, op1=ALU.add), reads=[res_ss], writes=[res_rstd])
    P.op("act", lambda e: e.activation(out=rstd[:, 0:n], in_=rstd[:, 0:n], func=AF.Sqrt),
         reads=[res_rstd], writes=[res_rstd])
    P.op("dve", lambda e: e.reciprocal(out=rstd[:, 0:n], in_=rstd[:, 0:n]), reads=[res_rstd], writes=[res_rstd])


def transpose8(c, src_tile, src_res, dst_ap, dst_res, bank, evac="act", split=False, half=0):
    P = c.P
    pt = c.pd[bank][:, half * 512:(half + 1) * 512].bitcast(BF16).rearrange("p (k t) -> p k t", k=8)
    pr = c.pdr[bank][half]

    def f(e):
        ins = None
        for k in range(8):
            ins = e.transpose(out=pt[:, k, :], in_=src_tile[:, k * 128:(k + 1) * 128], identity=c.ident[:])
        return ins

    def part(which):
        if which == 0:
            P.op("pe", f, reads=[src_res, c.ident_r], writes=[pr])
        elif evac == "act":
            P.op("act", lambda e: e.copy(out=dst_ap, in_=pt), reads=[pr], writes=[dst_res])
        else:
            P.op("dve", lambda e: e.tensor_copy(out=dst_ap, in_=pt), reads=[pr], writes=[dst_res])
    if split:
        return part
    part(0)
    part(1)


def precast_weights(c, li):
    for k in range(8):
        c.P.dma("pool", c.winb[li, k * 128:(k + 1) * 128, :], c.W["w_in"][li, k * 128:(k + 1) * 128, :],
                writes=[c.winb_r[li]])
    for k in range(0, 8, 2):
        c.P.dma("pool", c.woutb[li, k * 128:(k + 2) * 128, :], c.W["w_out"][li, k * 128:(k + 2) * 128, :],
                writes=[c.winb_r[li]])


def record(c, do_ffn):
    P, sb, nc = c.P, c.sb, c.nc
    c.ident = sb.alloc([128, 128], BF16, "ident")
    c.ident_r = Res()
    P.dma("sp", c.ident[:], c.C["ident"], writes=[c.ident_r])
    c.epst = sb.alloc([128, 1], F32, "epst")
    c.epst_r = Res()
    P.op("pool", lambda e: e.memset(c.epst[:], EPS), writes=[c.epst_r])
    c.mhalf = sb.alloc([128, 8], F32, "mhalf")
    P.op("pool", lambda e: e.memset(c.mhalf[:], -0.5), writes=[c.epst_r])
    c.gbc = sb.alloc([128, D], F32, "gbc")
    c.gbc_r = Res()
    c.base = sb.off
    tok0 = 0
    precast_weights(c, 0)
    for li in range(c.depth):
        src = c.xin if li == 0 else c.Xs
        sb.off = c.base
        P.barrier()
        P.dma("sp", c.gbc[:], c.W["attn_norm_g"][li:li + 1, :].partition_broadcast(128), writes=[c.gbc_r])
        t0 = 0
        wg_pre = None
        for si, L in enumerate(c.seq_lens):
            hook = None
            sb.off = c.base
            if do_ffn and len(c.seq_lens) > 1 and si == len(c.seq_lens) - 1 and L <= 2048:
                WgP, WuP, _ = ffn_weight_tiles(c)
                sb.off = c.base + 8 * D_FF * 2
                wg_pre = (Res(), Res())

                def hook(name, WgP=WgP, WuP=WuP, r=wg_pre, li=li):
                    if name == "da":
                        load_wg(c, li, WgP, r[0])

            phase_m(c, li, src, t0, L, hook)
            t0 += L
        sb.off = c.base
        P.barrier()
        phase_f(c, li, do_ffn, wg_pre)
    P.barrier()


def phase_m(c, li, src, t0, L, hook=None):
    P, sb = c.P, c.sb
    nt = L // 128
    m0 = sb.off
    P.barrier()
    hT = sb.alloc([128, 8, L], BF16, "hT")
    hT_r = _rr(nt)
    ss = sb.alloc([128, nt], F32, "ss")
    rstd = sb.alloc([128, nt], F32, "rstd")
    ss_r, rstd_r = _rr(nt), _rr(nt)
    mm = sb.off
    NXA, NHA = 4, 3
    xts = [sb.alloc([128, D], F32, "xt") for _ in range(NXA)]
    xts_r = _rr(NXA)
    hts = [sb.alloc([128, D], BF16, "ht") for _ in range(NHA)]
    hts_r = _rr(NHA)
    junk = sb.alloc([128, D], BF16, "junk")
    junk_r = Res()

    def a0(j):
        xt, xr = xts[j % NXA], xts_r[j % NXA]
        P.dma("sp", xt[:], src[t0 + j * 128:t0 + (j + 1) * 128, :], writes=[xr])
        P.op("act", lambda e: e.activation(out=junk[:], in_=xt[:], func=AF.Square, accum_out=ss[:, j:j + 1]),
             reads=[xr], writes=[junk_r, ss_r[j]])

    def a1(j):
        xt, xr = xts[j % NXA], xts_r[j % NXA]
        ht, hr = hts[j % NHA], hts_r[j % NHA]
        P.op("act", lambda e: e.activation(out=rstd[:, j:j + 1], in_=ss[:, j:j + 1], func=AF.Sqrt,
                                           bias=c.epst[:, 0:1], scale=1.0 / D),
             reads=[ss_r[j], c.epst_r], writes=[rstd_r[j]])
        P.op("dve", lambda e: e.reciprocal(out=rstd[:, j:j + 1], in_=rstd[:, j:j + 1]),
             reads=[rstd_r[j]], writes=[rstd_r[j]])
        P.op("dve", lambda e: e.scalar_tensor_tensor(
            out=ht[:], in0=xt[:], scalar=rstd[:, j:j + 1], in1=c.gbc[:], op0=ALU.mult, op1=ALU.mult),
            reads=[xr, rstd_r[j], c.gbc_r], writes=[hr])

    def a2(j):
        ht, hr = hts[j % NHA], hts_r[j % NHA]
        transpose8(c, ht, hr, hT[:, :, j * 128:(j + 1) * 128], hT_r[j], bank=j % 2, evac="act", split=True)(0)

    def a3(j):
        ht, hr = hts[j % NHA], hts_r[j % NHA]
        transpose8(c, ht, hr, hT[:, :, j * 128:(j + 1) * 128], hT_r[j], bank=j % 2, evac="act", split=True)(1)
    pipeline(nt, [a0, a1, a2, a3])
    for name in ("fn", "da", "na", "ssm"):
        sb.off = mm
        P.barrier()
        if hook is not None:
            hook(name)
        if name in c.mix:
            MIXERS[name](c, li, t0, L, hT, hT_r)
        else:
            zero_mixer(c, name, t0, L)
    sb.off = mm
    P.barrier()
    mixer_out_proj(c, li, src, t0, L)
    sb.off = m0


def zero_mixer(c, name, t0, L):
    P, sb = c.P, c.sb
    col = {"fn": 0, "da": 256, "na": 512, "ssm": 768}[name]
    z = sb.alloc([128, 4, 256], BF16, "zz")
    zr = Res()
    P.op("pool", lambda e: e.memset(z[:], 0.0), writes=[zr])
    for b in range(L // 512):
        dst = c.Os[t0 + b * 512:t0 + (b + 1) * 512, col:col + 256].rearrange("(j p) c -> p j c", p=128)
        P.dma("sp", dst, z[:], reads=[zr])


def mixer_out_proj(c, li, src, t0, L):
    P, sb = c.P, c.sb
    nt = L // 128
    Wo = sb.alloc([128, 8, D], BF16, "Wo")
    Wo_r = Res()
    wv = c.woutb[li].rearrange("(k p) c -> p k c", p=128)
    for k in range(0, 8, 4):
        P.dma("sp", Wo[:, k:k + 4, :], wv[:, k:k + 4, :], reads=[c.winb_r[li]], writes=[Wo_r])
    NX, NO = 4, 3
    xts = [sb.alloc([128, D], F32, "xo") for _ in range(NX)]
    xts_r = _rr(NX)
    ots = [sb.alloc([128, D], BF16, "ot") for _ in range(NO)]
    ots_r = _rr(NO)
    oTs = [sb.alloc([128, 8, 128], BF16, "oT") for _ in range(2)]
    oTs_r = _rr(2)

    def s0(j):
        r0 = t0 + j * 128
        P.dma("sp", xts[j % NX][:], src[r0:r0 + 128, :], writes=[xts_r[j % NX]])
        P.dma("sp", ots[j % NO][:], c.Os[r0:r0 + 128, :], writes=[ots_r[j % NO]])

    def s1(j):
        transpose8(c, ots[j % NO], ots_r[j % NO], oTs[j % 2][:], oTs_r[j % 2], bank=0, half=j % 2, evac="act",
                   split=True)(0)

    def s2(j):
        transpose8(c, ots[j % NO], ots_r[j % NO], oTs[j % 2][:], oTs_r[j % 2], bank=0, half=j % 2, evac="act",
                   split=True)(1)

    def s3(j):
        xt, xr = xts[j % NX], xts_r[j % NX]
        oT, oTr = oTs[j % 2], oTs_r[j % 2]
        for half in range(2):
            bank = 1 + (j * 2 + half) % 3
            ps = c.pd[bank][:, 0:512]
            pr = c.pdr[bank][0]

            def f(e, half=half, ps=ps):
                ins = None
                for k in range(8):
                    ins = e.matmul(ps, lhsT=oT[:, k, :], rhs=Wo[:, k, half * 512:(half + 1) * 512],
                                   start=(k == 0), stop=(k == 7))
                return ins
            P.op("pe", f, reads=[oTr, Wo_r], writes=[pr])

    def s4(j):
        xt, xr = xts[j % NX], xts_r[j % NX]
        r0 = t0 + j * 128
        for half in range(2):
            bank = 1 + (j * 2 + half) % 3
            ps = c.pd[bank][:, 0:512]
            pr = c.pdr[bank][0]
            P.op("dve", lambda e, half=half, ps=ps: e.tensor_tensor(
                out=xt[:, half * 512:(half + 1) * 512], in0=ps, in1=xt[:, half * 512:(half + 1) * 512], op=ALU.add),
                reads=[pr, xr], writes=[xr])
        P.dma("pool", c.Xs[r0:r0 + 128, :], xt[:], reads=[xr])
    pipeline(nt, [s0, s1, s2, s3, s4])


def ffn_weight_tiles(c):
    sb = c.sb
    Wg = sb.alloc([128, 8, D_FF], BF16, "Wg")
    Wu = sb.alloc([128, 8, D_FF], BF16, "Wu")
    Wdn = sb.alloc([128, NFF, D], BF16, "Wdn")
    return Wg, Wu, Wdn


def load_wg(c, li, Wg, Wg_r):
    gv = c.W["w_gate"][li].rearrange("(k p) c -> p k c", p=128)
    for k in range(8):
        c.P.dma("pool", Wg[:, k, :], gv[:, k, :], writes=[Wg_r])


def phase_f(c, li, do_ffn, wg_pre=None):
    P, sb = c.P, c.sb
    last = (li == c.depth - 1)
    T = c.T
    Wg, Wu, Wdn = ffn_weight_tiles(c)
    Wg_r, Wu_r, Wdn_r = Res(), Res(), Res()
    if wg_pre is not None:
        Wg_r = wg_pre[0]
    if do_ffn:
        uv = c.W["w_up"][li].rearrange("(k p) c -> p k c", p=128)
        dv = c.W["w_down"][li].rearrange("(k p) c -> p k c", p=128)
        if wg_pre is None:
            load_wg(c, li, Wg, Wg_r)
        for k in range(8):
            P.dma("pool", Wu[:, k, :], uv[:, k, :], writes=[Wu_r])
        for k in range(0, NFF, 2):
            P.dma("pool", Wdn[:, k:k + 2, :], dv[:, k:k + 2, :], writes=[Wdn_r])
    if li + 1 < c.depth:
        precast_weights(c, li + 1)
    P.dma("sp", c.gbc[:], c.W["ffn_norm_g"][li:li + 1, :].partition_broadcast(128), writes=[c.gbc_r])
    gfin, gfin_r = None, Res()
    if last:
        gfin = sb.alloc([128, D], F32, "gfin")
        P.dma("sp", gfin[:], c.W["final_norm_g"][None, :].partition_broadcast(128), writes=[gfin_r])
    NXN, NXR = 2, 2
    xns = [sb.alloc([128, D], F32, "xn") for _ in range(NXN)]
    xns_r = _rr(NXN)
    xrs = [sb.alloc([128, D], F32, "xr") for _ in range(NXR)]
    xrs_r = _rr(NXR)
    hts = [sb.alloc([128, D], BF16, "hf") for _ in range(4)]
    hts_r = _rr(4)
    h2T = sb.alloc([128, 8, 512], BF16, "h2T")
    h2T_r = _rr(4)
    aT = sb.alloc([128, NFF, 512], BF16, "aT")
    aT_r = _rr(NFF)
    sgs = [sb.alloc([128, 512], F32, "sg") for _ in range(2)]
    sgs_r = _rr(2)
    junk = sb.alloc([128, D], BF16, "junkf")
    junk_r = Res()
    nblk = T // 512
    ss = sb.alloc([128, 2 * nblk * 4], F32, "ssf")
    rstd = sb.alloc([128, 2 * nblk * 4], F32, "rstdf")
    cnt = {"xn": 0, "xr": 0}

    def norm_chain(xt, xr, sc, gain, gain_r, out_ap, out_r):
        ssr, rsr = Res(), Res()
        P.op("act", lambda e: e.activation(out=junk[:], in_=xt[:], func=AF.Square, accum_out=ss[:, sc:sc + 1]),
             reads=[xr], writes=[junk_r, ssr])
        P.op("act", lambda e: e.activation(out=rstd[:, sc:sc + 1], in_=ss[:, sc:sc + 1], func=AF.Sqrt,
                                           bias=c.epst[:, 0:1], scale=1.0 / D), reads=[ssr, c.epst_r], writes=[rsr])
        P.op("dve", lambda e: e.reciprocal(out=rstd[:, sc:sc + 1], in_=rstd[:, sc:sc + 1]), reads=[rsr], writes=[rsr])
        P.op("dve", lambda e: e.scalar_tensor_tensor(out=out_ap, in0=xt[:], scalar=rstd[:, sc:sc + 1], in1=gain[:],
                                                     op0=ALU.mult, op1=ALU.mult), reads=[xr, rsr, gain_r],
             writes=[out_r])

    def pro_norm(b):
        for j in range(4):
            xt, xr = xns[cnt["xn"] % NXN], xns_r[cnt["xn"] % NXN]
            cnt["xn"] += 1
            r0 = b * 512 + j * 128
            P.dma("sp", xt[:], c.Xs[r0:r0 + 128, :], writes=[xr])
            norm_chain(xt, xr, b * 4 + j, c.gbc, c.gbc_r, hts[j][:], hts_r[j])

    def pro_T(b):
        for j in range(4):
            transpose8(c, hts[j], hts_r[j], h2T[:, :, j * 128:(j + 1) * 128], h2T_r[j], bank=0, half=j % 2,
                       evac="dve")

    def gateup(b):
        for cc in range(NFF):
            bg, bu = (1, 2) if cc % 2 == 0 else (3, 1)
            hg, hu = (0, 0) if cc % 2 == 0 else (0, 1)
            if cc % 2 == 1:
                bu, hu = 2, 1
            psg, psu = c.pd[bg][:, hg * 512:(hg + 1) * 512], c.pd[bu][:, hu * 512:(hu + 1) * 512]
            prg, pru = c.pdr[bg][hg], c.pdr[bu][hu]
            sg, sgr = sgs[cc % 2], sgs_r[cc % 2]

            def fg(e, cc=cc, ps=psg, Wx=Wg):
                ins = None
                for k in range(8):
                    ins = e.matmul(ps, lhsT=Wx[:, k, cc * 128:(cc + 1) * 128], rhs=h2T[:, k, :],
                                   start=(k == 0), stop=(k == 7))
                return ins

            def fu(e, cc=cc, ps=psu, Wx=Wu):
                ins = None
                for k in range(8):
                    ins = e.matmul(ps, lhsT=Wx[:, k, cc * 128:(cc + 1) * 128], rhs=h2T[:, k, :],
                                   start=(k == 0), stop=(k == 7))
                return ins
            P.op("pe", fg, reads=h2T_r + [Wg_r], writes=[prg])
            P.op("pe", fu, reads=h2T_r + [Wu_r], writes=[pru])
            P.op("act", lambda e, sg=sg, ps=psg: e.activation(out=sg[:], in_=ps, func=AF.Silu),
                 reads=[prg], writes=[sgr])
            P.op("dve", lambda e, sg=sg, ps=psu, cc=cc: e.tensor_tensor(out=aT[:, cc, :], in0=ps, in1=sg[:],
                                                                        op=ALU.mult),
                 reads=[pru, sgr], writes=[aT_r[cc]])

    def down(b):
        for j in range(4):
            xt, xr = xrs[cnt["xr"] % NXR], xrs_r[cnt["xr"] % NXR]
            cnt["xr"] += 1
            r0 = b * 512 + j * 128
            P.dma("sp", xt[:], c.Xs[r0:r0 + 128, :], writes=[xr])
            if do_ffn:
                for half in range(2):
                    bank, hs = ((1, 0), (2, 0)) [half] if j % 2 == 0 else ((3, 0), (3, 1))[half]
                    ps = c.pd[bank][:, hs * 512:(hs + 1) * 512]
                    pr = c.pdr[bank][hs]

                    def fd(e, j=j, half=half, ps=ps):
                        ins = None
                        for cc in range(NFF):
                            ins = e.matmul(ps, lhsT=aT[:, cc, j * 128:(j + 1) * 128],
                                           rhs=Wdn[:, cc, half * 512:(half + 1) * 512],
                                           start=(cc == 0), stop=(cc == NFF - 1))
                        return ins
                    P.op("pe", fd, reads=aT_r + [Wdn_r], writes=[pr])
                    P.op("dve", lambda e, xt=xt, half=half, ps=ps: e.tensor_tensor(
                        out=xt[:, half * 512:(half + 1) * 512], in0=ps, in1=xt[:, half * 512:(half + 1) * 512],
                        op=ALU.add), reads=[pr, xr], writes=[xr])
            if last:
                norm_chain(xt, xr, nblk * 4 + b * 4 + j, gfin, gfin_r, xt[:], xr)
                P.dma("pool", c.y[r0:r0 + 128, :], xt[:], reads=[xr])
            else:
                P.dma("pool", c.Xs[r0:r0 + 128, :], xt[:], reads=[xr])

    if do_ffn:
        pro_norm(0)
        pro_T(0)
    for b in range(nblk):
        if do_ffn:
            gateup(b)
            if b + 1 < nblk:
                pro_norm(b + 1)
        down(b)
        if do_ffn and b + 1 < nblk:
            pro_T(b + 1)


def evac(c, i, out, in_, reads, writes):
    if i % 2 == 0:
        c.P.op("act", lambda e: e.copy(out=out, in_=in_), reads=reads, writes=writes)
    else:
        c.P.op("dve", lambda e: e.tensor_copy(out=out, in_=in_), reads=reads, writes=writes)


def mixer_fn(c, li, t0, L, hT, hT_r):
    P, sb = c.P, c.sb
    nt, nb = L // 128, L // 512
    win = c.winb[li].rearrange("(k p) c -> p k c", p=128)
    Wi = sb.alloc([128, 8, 256], BF16, "fnWi")
    Wi_r = Res()
    P.dma("sp", Wi[:], win[:, :, OFF_FN:OFF_FN + 256], reads=[c.winb_r[li]], writes=[Wi_r])
    CS = sb.alloc([128, 2, 512], BF16, "fnCS")
    CS_r = Res()
    P.dma("sp", CS[:], c.C["cs64"].rearrange("(c p) n -> p c n", p=128), writes=[CS_r])
    Wf = sb.alloc([128, 2, 256], BF16, "fnWf")
    Wf_r = Res()
    P.dma("pool", Wf[:], c.W["w_fourier"][li].rearrange("(c p) n -> p c n", p=128), writes=[Wf_r])
    WT = sb.alloc([128, 2, D], BF16, "fnWT")
    WT_r = Res()
    for cc in range(2):
        pt = c.pd[0][:, 0:512].bitcast(BF16).rearrange("p (k t) -> p k t", k=8)
        pr = c.pdr[0][0]

        def f(e, cc=cc, pt=pt):
            ins = None
            for k in range(8):
                ins = e.transpose(out=pt[:, k, :], in_=Wi[:, k, cc * 128:(cc + 1) * 128], identity=c.ident[:])
            return ins
        P.op("pe", f, reads=[Wi_r, c.ident_r], writes=[pr])
        P.op("act", lambda e, cc=cc, pt=pt: e.copy(out=WT[:, cc, :].rearrange("p (k t) -> p k t", k=8), in_=pt),
             reads=[pr], writes=[WT_r])
    Wcs = sb.alloc([128, 8, 512], BF16, "fnWcs")
    Wcs_r = Res()
    for k in range(8):
        bank = 1 + k % 2
        ps, pr = c.pd[bank][:, 0:512], c.pdr[bank][0]

        def f(e, k=k, ps=ps):
            ins = None
            for cc in range(2):
                ins = e.matmul(ps, lhsT=WT[:, cc, k * 128:(k + 1) * 128], rhs=CS[:, cc, :], start=(cc == 0),
                               stop=(cc == 1))
            return ins
        P.op("pe", f, reads=[WT_r, CS_r], writes=[pr])
        evac(c, k, Wcs[:, k, :], ps, [pr], [Wcs_r])
    Acs = sb.alloc([128, nt, 512], BF16, "fnA")
    Acs_r = Res()
    for j in range(nt):
        bank = 1 + j % 3
        ps, pr = c.pd[bank][:, 0:512], c.pdr[bank][0]

        def f(e, j=j, ps=ps):
            ins = None
            for k in range(8):
                ins = e.matmul(ps, lhsT=hT[:, k, j * 128:(j + 1) * 128], rhs=Wcs[:, k, :], start=(k == 0),
                               stop=(k == 7))
            return ins
        P.op("pe", f, reads=[hT_r[j], Wcs_r], writes=[pr])
        evac(c, j, Acs[:, j, :], ps, [pr], [Acs_r])
    G = min(8, nt)
    ng = nt // G
    stripes = [(sb.alloc([128, G, 512], BF16, "fnSc"), sb.alloc([128, G, 512], BF16, "fnSs"), Res()) for _ in range(2)]
    cv = c.C[f"dftc{L}"].rearrange("(a p) n -> p a n", p=128)
    sv = c.C[f"dfts{L}"].rearrange("(a p) n -> p a n", p=128)
    FT = sb.alloc([128, 2, L], BF16, "fnFT")
    FT_r = _rr(nb)
    alt = sb.alloc([128, 1], BF16, "fnalt")
    alt_r = Res()
    P.dma("sp", alt[:], c.C[f"alt{L}"], writes=[alt_r])
    for cc in range(2):
        ps, pr = c.pd[0][:, cc * 512:cc * 512 + 1], c.pdr[0][cc]

        def fh(e, cc=cc, ps=ps):
            ins = None
            for ja in range(nt):
                ins = e.matmul(ps, lhsT=Acs[:, ja, cc * 128:(cc + 1) * 128], rhs=alt[:], start=(ja == 0),
                               stop=(ja == nt - 1))
            return ins
        P.op("pe", fh, reads=[Acs_r, alt_r], writes=[pr])
        P.op("dve", lambda e, cc=cc, ps=ps: e.tensor_copy(out=FT[:, cc, L // 2:L // 2 + 1], in_=ps), reads=[pr],
             writes=[FT_r[(L // 2) // 512]])
    tmps = [(sb.alloc([128, 512], F32, "fntmp"), Res()) for _ in range(2)]
    si = 0
    ti = 0
    for b in range(nb // 2):
        pb = (0, 1) if b % 2 == 0 else (2, 3)
        Fc = [c.pd[pb[0]][:, 0:512], c.pd[pb[1]][:, 0:512]]
        Fs = [c.pd[pb[0]][:, 512:1024], c.pd[pb[1]][:, 512:1024]]
        Fc_r = [c.pdr[pb[0]][0], c.pdr[pb[1]][0]]
        Fs_r = [c.pdr[pb[0]][1], c.pdr[pb[1]][1]]
        for g in range(ng):
            Sc, Ss, Sr = stripes[si % 2]
            si += 1
            P.dma("sp", Sc[:], cv[:, g * G:(g + 1) * G, b * 512:(b + 1) * 512], writes=[Sr])
            P.dma("sp", Ss[:], sv[:, g * G:(g + 1) * G, b * 512:(b + 1) * 512], writes=[Sr])

            def f(e, g=g, Sc=Sc, Ss=Ss, Fc=Fc, Fs=Fs):
                ins = None
                for a in range(G):
                    ja = g * G + a
                    for cc in range(2):
                        e.matmul(Fc[cc], lhsT=Acs[:, ja, cc * 128:(cc + 1) * 128], rhs=Sc[:, a, :],
                                 start=(ja == 0), stop=(ja == nt - 1))
                        ins = e.matmul(Fs[cc], lhsT=Acs[:, ja, 256 + cc * 128:256 + (cc + 1) * 128], rhs=Ss[:, a, :],
                                       start=(ja == 0), stop=(ja == nt - 1))
                return ins
            P.op("pe", f, reads=[Acs_r, Sr], writes=Fc_r + Fs_r)
        hi = L - b * 512
        mb = (hi - 1) // 512
        for cc in range(2):
            tmp, tmp_r = tmps[ti % 2]
            ti += 1
            P.op("act", lambda e, cc=cc, tmp=tmp, Fs=Fs: e.copy(out=tmp[:], in_=Fs[cc]), reads=[Fs_r[cc]],
                 writes=[tmp_r])
            P.op("dve", lambda e, cc=cc, tmp=tmp, Fc=Fc, b=b: e.tensor_tensor(
                out=FT[:, cc, b * 512:(b + 1) * 512], in0=Fc[cc], in1=tmp[:], op=ALU.add),
                reads=[Fc_r[cc], tmp_r], writes=[FT_r[b]])
            i0 = 1 if b == 0 else 0
            n = 512 - i0
            dstv = FT[:, cc, hi - 511:hi - 511 + n]
            P.op("dve", lambda e, cc=cc, tmp=tmp, Fc=Fc, dstv=dstv, i0=i0: e.tensor_tensor(
                out=dstv[:, ::-1], in0=Fc[cc][:, i0:512], in1=tmp[:, i0:512], op=ALU.subtract),
                reads=[Fc_r[cc], tmp_r], writes=[FT_r[mb], FT_r[min(mb + 1, nb - 1)], FT_r[max(mb - 1, 0)]])
    obuf = [(sb.alloc([128, 4, 256], BF16, "fnO"), Res()) for _ in range(2)]
    for j in range(nt):
        ob, obr = obuf[(j // 4) % 2]
        bank = 1 + j % 3
        ps, pr = c.pd[bank][:, 0:256], c.pdr[bank][0]

        def f(e, j=j, ps=ps):
            ins = None
            for cc in range(2):
                ins = e.matmul(ps, lhsT=FT[:, cc, j * 128:(j + 1) * 128], rhs=Wf[:, cc, :], start=(cc == 0),
                               stop=(cc == 1))
            return ins
        P.op("pe", f, reads=[FT_r[j // 4], Wf_r], writes=[pr])
        evac(c, j, ob[:, j % 4, :], ps, [pr], [obr])
        if j % 4 == 3:
            b = j // 4
            dst = c.Os[t0 + b * 512:t0 + (b + 1) * 512, 0:256].rearrange("(j p) c -> p j c", p=128)
            P.dma("sp", dst, ob[:], reads=[obr])


def mixer_na(c, li, t0, L, hT, hT_r):
    P, sb, nc = c.P, c.sb, c.nc
    nt, nb, rows = L // 128, L // 512, L // 64
    win = c.winb[li].rearrange("(k p) c -> p k c", p=128)
    Wn = sb.alloc([128, 8, 768], BF16, "naW")
    Wn_r = Res()
    for k in range(0, 8, 4):
        P.dma("sp", Wn[:, k:k + 4, :], win[:, k:k + 4, OFF_NA:OFF_NA + 768], reads=[c.winb_r[li]], writes=[Wn_r])
    dlo = sb.alloc([128, 128], BF16, "dlo")
    dhi = sb.alloc([128, 128], BF16, "dhi")
    negt = sb.alloc([128, 128], BF16, "negt")
    wm = sb.alloc([128, 64], F32, "nawm")
    cst_r = Res()
    P.dma("sp", dlo[:], c.C["dlo"], writes=[cst_r])
    P.dma("sp", dhi[:], c.C["dhi"], writes=[cst_r])
    P.dma("sp", negt[:], c.C["negt"], writes=[cst_r])
    P.dma("sp", wm[:], c.C["na_wm"], writes=[cst_r])
    NP = 128 + 1860 + 128
    zt = sb.alloc([1, NP], F32, "naz")
    zt_r, rp_r = Res(), Res()
    P.op("pool", lambda e: e.memset(zt[:], 0.0), writes=[zt_r])
    P.dma("sp", c.rpad.ap()[None, :], zt[:], reads=[zt_r], writes=[rp_r])
    P.dma("sp", c.rpad.ap()[128:128 + 1860], c.W["na_rpb"][li].rearrange("h r d -> (h r d)"), writes=[rp_r])
    TMr = sb.alloc([128, 4, 16, 64], F32, "naTMr")
    TMr_r = Res()
    off_lo = 128 - 48 - 31
    for h in range(4):
        P.dma("sp", TMr[0:64, h], bass.AP(c.rpad, off_lo + 465 * h, [[1, 64], [31, 16], [1, 64]]), reads=[rp_r],
              writes=[TMr_r])
        P.dma("sp", TMr[64:128, h], bass.AP(c.rpad, off_lo + 31 + 465 * h, [[1, 64], [31, 16], [1, 64]]),
              reads=[rp_r], writes=[TMr_r])
    TMv = TMr[:].rearrange("p h r q -> p (h r) q")
    P.op("dve", lambda e: e.tensor_tensor(out=TMv, in0=TMv, in1=wm[:].unsqueeze(1).broadcast_to([128, 64, 64]),
                                          op=ALU.add), reads=[TMr_r, cst_r], writes=[TMr_r])
    TM2 = sb.alloc([128, 4, 16, 64], BF16, "naTM2")
    TM2_r = Res()
    P.op("dve", lambda e: e.tensor_scalar(out=TM2[:].rearrange("p h r q -> p (h r) q"), in0=TMv[:, :, ::-1],
                                          scalar1=8.0, scalar2=None, op0=ALU.mult), reads=[TMr_r], writes=[TM2_r])
    qT = sb.alloc([128, 2, L], BF16, "naq")
    kT = sb.alloc([128, 2, L], BF16, "nak")
    qk_r = _rr(nb)
    V = sb.alloc([128, nt, 4, 65], BF16, "nav")
    V_r = Res()
    P.op("pool", lambda e: e.memset(V[:, :, :, 64:65], 1.0), writes=[V_r])
    ei = 0
    for b in range(nb):
        for which, dstT in ((0, qT), (1, kT)):
            for cc in range(2):
                bank = ei % 4
                ps, pr = c.pd[bank][:, 0:512], c.pdr[bank][0]
                col = which * 256 + cc * 128

                def f(e, b=b, col=col, ps=ps):
                    ins = None
                    for k in range(8):
                        ins = e.matmul(ps, lhsT=Wn[:, k, col:col + 128], rhs=hT[:, k, b * 512:(b + 1) * 512],
                                       start=(k == 0), stop=(k == 7))
                    return ins
                P.op("pe", f, reads=hT_r[b * 4:b * 4 + 4] + [Wn_r], writes=[pr])
                evac(c, ei, dstT[:, cc, b * 512:(b + 1) * 512], ps, [pr], [qk_r[b]])
                ei += 1
    for j in range(nt):
        bank = ei % 4
        ps, pr = c.pd[bank][:, 0:256], c.pdr[bank][0]

        def f(e, j=j, ps=ps):
            ins = None
            for k in range(8):
                ins = e.matmul(ps, lhsT=hT[:, k, j * 128:(j + 1) * 128], rhs=Wn[:, k, 512:768], start=(k == 0),
                               stop=(k == 7))
            return ins
        P.op("pe", f, reads=[hT_r[j], Wn_r], writes=[pr])
        evac(c, ei, V[:, j, :, 0:64], ps.rearrange("p (h d) -> p h d", h=4), [pr], [V_r])
        ei += 1
    Pts = [(sb.alloc([128, 640], BF16, "naP"), Res()) for _ in range(2)]
    obuf = [(sb.alloc([128, 4, 256], BF16, "naO"), Res()) for _ in range(2)]
    rec = sb.alloc([128, 2, 4], F32, "narec")
    rec_r = Res()
    allqk = qk_r
    units = [(j, h) for j in range(nt) for h in range(4)]
    Ob = [c.pd[3][:, 0:260].rearrange("p (h d) -> p h d", h=4), c.pd[3][:, 512:772].rearrange("p (h d) -> p h d", h=4)]
    Ob_r = c.pdr[3]

    def geom(j):
        r0 = 2 * j
        sts = [min(max(r0 + jq - 4, 0), rows - 8) for jq in range(2)]
        Rs, Re = sts[0], sts[1] + 8
        return r0, sts, Rs, (Re - Rs + 1) // 2

    mask2 = sb.alloc([128, 2], F32, "nam2")
    P.dma("sp", mask2[:], c.C["mask2"], writes=[cst_r])
    qms = [(sb.alloc([128, 4, 128], BF16, "naqm"), Res()) for _ in range(2)]

    def expand(j2):
        qm2, qm2_r = qms[j2 % 2]
        for hh in range(4):
            P.op("dve", lambda e, hh=hh: e.tensor_scalar(
                out=qm2[:, hh, :], in0=qT[:, hh // 2, j2 * 128:(j2 + 1) * 128], scalar1=mask2[:, hh % 2:hh % 2 + 1],
                scalar2=None, op0=ALU.mult), reads=[qk_r[j2 // 4], cst_r], writes=[qm2_r])

    def n_s(i):
        j, h = units[i]
        r0, sts, Rs, nkt = geom(j)
        cc, pb = h // 2, (h % 2) * 64
        S, S_r = c.pd[1 + i % 2], c.pdr[1 + i % 2]
        qm, qm_r = qms[j % 2]
        if i == 0:
            expand(0)
        if h == 1 and j + 1 < nt:
            expand(j + 1)

        def fs(e):
            ins = None
            for t in range(nkt):
                kt = Rs // 2 + t
                blk = S[:, t * 128:(t + 1) * 128]
                e.matmul(blk, lhsT=kT[:, cc, kt * 128:(kt + 1) * 128], rhs=qm[:, h, :], start=True, stop=False,
                         skip_group_check=True)
                for jq in range(2):
                    rq = r0 + jq
                    R = Rs + 2 * t
                    v_lo = sts[jq] <= R < sts[jq] + 8
                    v_hi = sts[jq] <= R + 1 < sts[jq] + 8
                    dr = R - rq + 7
                    sub = S[:, t * 128 + jq * 64:t * 128 + (jq + 1) * 64]
                    if not v_lo and not v_hi:
                        ins = e.matmul(sub, lhsT=c.ident[:], rhs=negt[:, 0:64], start=False, stop=True,
                                       skip_group_check=True)
                        continue
                    assert -1 <= dr <= 14
                    ins = e.matmul(sub, lhsT=c.ident[:], rhs=TM2[:, h, dr + 1, :], start=False, stop=True,
                                   skip_group_check=True)
                    if not v_lo:
                        ins = e.matmul(sub, lhsT=dlo[:], rhs=negt[:, 0:64], start=False, stop=True,
                                       skip_group_check=True)
                    if not v_hi:
                        ins = e.matmul(sub, lhsT=dhi[:], rhs=negt[:, 0:64], start=False, stop=True,
                                       skip_group_check=True)
            return ins
        P.op("pe", fs, reads=allqk + [TM2_r, cst_r, c.ident_r, qm_r], writes=S_r)

    def n_e(i):
        j, h = units[i]
        nkt = geom(j)[3]
        S, S_r = c.pd[1 + i % 2], c.pdr[1 + i % 2]
        Pt, Pt_r = Pts[i % 2]
        P.op("act", lambda e: e.activation(out=Pt[:, 0:nkt * 128], in_=S[:, 0:nkt * 128], func=AF.Exp, scale=0.125),
             reads=S_r, writes=[Pt_r])

    def n_o(i):
        j, h = units[i]
        r0, sts, Rs, nkt = geom(j)
        Pt, Pt_r = Pts[i % 2]
        O, O_r = Ob[j % 2], Ob_r[j % 2]

        def fo(e):
            ins = None
            for t in range(nkt):
                kt = Rs // 2 + t
                ins = e.matmul(O[:, h, :], lhsT=Pt[:, t * 128:(t + 1) * 128], rhs=V[:, kt, h, :],
                               start=(t == 0), stop=(t == nkt - 1), skip_group_check=True)
            return ins
        P.op("pe", fo, reads=[Pt_r, V_r], writes=[O_r])
        if h == 3:
            ob, obr = obuf[(j // 4) % 2]
            P.op("dve", lambda e: e.reciprocal(out=rec[:, j % 2, :], in_=O[:, :, 64]), reads=[O_r], writes=[rec_r])
            P.op("dve", lambda e: e.tensor_tensor(
                out=ob[:, j % 4, :].rearrange("p (h d) -> p h d", h=4), in0=O[:, :, 0:64],
                in1=rec[:, j % 2, :].unsqueeze(2).broadcast_to([128, 4, 64]), op=ALU.mult), reads=[O_r, rec_r],
                writes=[obr])
            if j % 4 == 3:
                b = j // 4
                dst = c.Os[t0 + b * 512:t0 + (b + 1) * 512, 512:768].rearrange("(j p) c -> p j c", p=128)
                P.dma("sp", dst, ob[:], reads=[obr])
    pipeline(len(units), [n_s, n_e, n_o])


def mixer_da(c, li, t0, L, hT, hT_r):
    P, sb = c.P, c.sb
    nt, nb = L // 128, L // 512
    lam0 = 0.8 - 0.6 * math.exp(-0.3 * li)
    win = c.winb[li].rearrange("(k p) c -> p k c", p=128)
    Wd = sb.alloc([128, 8, 768], BF16, "daW")
    Wd_r = Res()
    for k in range(0, 8, 4):
        P.dma("sp", Wd[:, k:k + 4, :], win[:, k:k + 4, OFF_DA:OFF_DA + 768], reads=[c.winb_r[li]], writes=[Wd_r])
    Wsw = sb.alloc([128, 8, 512], BF16, "daWs")
    Wsw_r = Res()
    for k in range(8):
        src = Wd[:, k, 0:512].rearrange("p (g t i) -> p g t i", t=2, i=16)
        dst = Wsw[:, k, :].rearrange("p (g t i) -> p g t i", t=2, i=16)
        P.op("dve", lambda e, src=src, dst=dst: e.tensor_scalar(out=dst[:, :, 0, :], in0=src[:, :, 1, :], scalar1=-1.0,
                                                                scalar2=None, op0=ALU.mult),
             reads=[Wd_r], writes=[Wsw_r])
        P.op("dve", lambda e, src=src, dst=dst: e.tensor_copy(out=dst[:, :, 1, :], in_=src[:, :, 0, :]),
             reads=[Wd_r], writes=[Wsw_r])
    rc = sb.alloc([128, L], BF16, "darc")
    rs = sb.alloc([128, L], BF16, "dars")
    idf = sb.alloc([128, 128], F32, "daidf")
    gsub = sb.alloc([128, 64], F32, "dagsub")
    lvt = sb.alloc([128, 128], F32, "dalv")
    cst_r, lv_r = Res(), Res()
    P.dma("sp", rc[:], c.C["ropec"][:, 0:L], writes=[cst_r])
    P.dma("sp", rs[:], c.C["ropes"][:, 0:L], writes=[cst_r])
    P.dma("sp", idf[:], c.C["identf"], writes=[cst_r])
    P.dma("sp", gsub[:], c.W["diff_subln_g"][li:li + 1, :].partition_broadcast(128), writes=[cst_r])
    P.dma("sp", lvt[:], c.W["diff_lambda"][li:li + 1].rearrange("o a b -> o (a b)").partition_broadcast(128),
          writes=[lv_r])
    P.op("dve", lambda e: e.tensor_scalar(out=gsub[:], in0=gsub[:], scalar1=1.0 - lam0, scalar2=None, op0=ALU.mult),
         reads=[cst_r], writes=[cst_r])
    lp = sb.alloc([128, 2, 32], F32, "dalp")
    lsum = sb.alloc([128, 2], F32, "dals")
    lam = sb.alloc([128, 1], F32, "dalam")
    lv4 = lvt[:].rearrange("p (a t i) -> p a t i", t=2, i=32)
    P.op("dve", lambda e: e.tensor_tensor(out=lp[:], in0=lv4[:, :, 0, :], in1=lv4[:, :, 1, :], op=ALU.mult),
         reads=[lv_r], writes=[lv_r])
    P.op("dve", lambda e: e.reduce_sum(out=lsum[:], in_=lp[:], axis=AX.X), reads=[lv_r], writes=[lv_r])
    P.op("act", lambda e: e.activation(out=lsum[:], in_=lsum[:], func=AF.Exp), reads=[lv_r], writes=[lv_r])
    P.op("dve", lambda e: e.tensor_tensor(out=lam[:], in0=lsum[:, 0:1], in1=lsum[:, 1:2], op=ALU.subtract),
         reads=[lv_r], writes=[lv_r])
    P.op("dve", lambda e: e.tensor_scalar(out=lam[:], in0=lam[:], scalar1=lam0, scalar2=None, op0=ALU.add),
         reads=[lv_r], writes=[lv_r])
    qT = sb.alloc([128, 2, L], BF16, "daq")
    kT = sb.alloc([128, 2, L], BF16, "dak")
    mask4 = sb.alloc([128, 4], F32, "dam4")
    P.dma("sp", mask4[:], c.C["mask4"], writes=[cst_r])
    qms = [(sb.alloc([128, 8, 512], BF16, "daqm"), Res()) for _ in range(2)]
    qk_r = _rr(nb)
    Vf = sb.alloc([128, nt, 4 * 65 + 64], BF16, "dav")
    V = Vf[:, :, 0:260].rearrange("p t (h d) -> p t h d", h=4)
    V_r = Res()
    P.op("pool", lambda e: e.memset(Vf[:, :, 260:324], 0.0), writes=[V_r])
    P.op("pool", lambda e: e.memset(V[:, :, :, 64:65], 1.0), writes=[V_r])
    tmps = [(sb.alloc([128, 512], F32, "dat1"), sb.alloc([128, 512], F32, "dat2"), Res()) for _ in range(2)]
    ei = 0
    for b in range(nb):
        for which, dstT in ((0, qT), (1, kT)):
            for ch in range(2):
                M = 128
                col = which * 256 + ch * 128
                bank = ei % 2
                ps1, ps2 = c.pd[bank][0:M, 0:512], c.pd[bank][0:M, 512:1024]
                pr = c.pdr[bank]
                t1, t2, tr = tmps[ei % 2]
                ei += 1

                def f(e, b=b, col=col, M=M, ps1=ps1, ps2=ps2):
                    ins = None
                    for k in range(8):
                        e.matmul(ps1, lhsT=Wd[:, k, col:col + M], rhs=hT[:, k, b * 512:(b + 1) * 512],
                                 start=(k == 0), stop=(k == 7))
                    for k in range(8):
                        ins = e.matmul(ps2, lhsT=Wsw[:, k, col:col + M], rhs=hT[:, k, b * 512:(b + 1) * 512],
                                       start=(k == 0), stop=(k == 7))
                    return ins
                P.op("pe", f, reads=hT_r[b * 4:b * 4 + 4] + [Wd_r, Wsw_r], writes=pr)
                P.op("dve", lambda e, t1=t1, ps1=ps1, M=M, b=b: e.tensor_tensor(
                    out=t1[0:M, :], in0=ps1, in1=rc[0:M, b * 512:(b + 1) * 512], op=ALU.mult),
                    reads=[pr[0], cst_r], writes=[tr])
                P.op("dve", lambda e, t2=t2, ps2=ps2, M=M, b=b: e.tensor_tensor(
                    out=t2[0:M, :], in0=ps2, in1=rs[0:M, b * 512:(b + 1) * 512], op=ALU.mult),
                    reads=[pr[1], cst_r], writes=[tr])
                P.op("dve", lambda e, t1=t1, t2=t2, M=M, dstT=dstT, ch=ch, b=b: e.tensor_tensor(
                    out=dstT[0:M, ch, b * 512:(b + 1) * 512], in0=t1[0:M, :], in1=t2[0:M, :], op=ALU.add),
                    reads=[tr], writes=[qk_r[b]])
    for j in range(nt):
        bank = 2 + j % 2
        ps, pr = c.pd[bank][:, 0:256], c.pdr[bank][0]

        def f(e, j=j, ps=ps):
            ins = None
            for k in range(8):
                ins = e.matmul(ps, lhsT=hT[:, k, j * 128:(j + 1) * 128], rhs=Wd[:, k, 512:768], start=(k == 0),
                               stop=(k == 7))
            return ins
        P.op("pe", f, reads=[hT_r[j], Wd_r], writes=[pr])
        evac(c, j, V[:, j, :, 0:64], ps.rearrange("p (h d) -> p h d", h=4), [pr], [V_r])
    Pts = [(sb.alloc([128, 1024], BF16, "daP"), Res()) for _ in range(2)]
    OT = [sb.alloc([65, 512], F32, "daOT") for _ in range(2)]
    OT_r = _rr(2)
    obuf = [(sb.alloc([128, 4, 256], BF16, "daO"), Res()) for _ in range(2)]
    r12 = sb.alloc([128, 2, 4], F32, "dar")
    ab = sb.alloc([128, 2, 4, 64], F32, "daab")
    dd = sb.alloc([128, 4, 64], F32, "dad")
    sq = sb.alloc([128, 4, 64], F32, "dasq")
    ssq = sb.alloc([128, 4], F32, "dassq")
    w_r = Res()
    scale = 32 ** -0.5
    units = [(qb, h, kt) for qb in range(nb) for h in range(4) for kt in range(nt)]
    acc = [c.pd[2][0:65, 0:512], c.pd[3][0:65, 0:512]]
    accf = [c.pd[2][:, 0:512], c.pd[3][:, 0:512]]
    acc_r = [c.pdr[2][0], c.pdr[3][0]]
    tp = [c.pd[2][:, 512:772].rearrange("p (t d) -> p t d", t=4),
          c.pd[3][:, 512:772].rearrange("p (t d) -> p t d", t=4)]
    tp_r = [c.pdr[2][1], c.pdr[3][1]]
    fill = [c.pd[2][:, 772:1024], c.pd[3][:, 772:1024]]
    fill_r = Res()
    NFILL = DA_NFILL
    deferred = []

    def u_s(i):
        qb, h, kt = units[i]
        S, S_r = c.pd[i % 2], c.pdr[i % 2]

        qm, qm_r = qms[qb % 2]

        def expand(qb2):
            qm2, qm2_r = qms[qb2 % 2]
            for hb in range(8):
                P.op("dve", lambda e, hb=hb: e.tensor_scalar(
                    out=qm2[:, hb, :], in0=qT[:, hb // 4, qb2 * 512:(qb2 + 1) * 512],
                    scalar1=mask4[:, hb % 4:hb % 4 + 1], scalar2=None, op0=ALU.mult),
                    reads=[qk_r[qb2], cst_r], writes=[qm2_r])
        if i == 0:
            expand(0)
        if h == 1 and kt == 0 and qb + 1 < nb:
            expand(qb + 1)

        def fs(e):
            ins = None
            for br in range(2):
                hb = 2 * h + br
                ins = e.matmul(S[:, br * 512:(br + 1) * 512], lhsT=kT[:, hb // 4, kt * 128:(kt + 1) * 128],
                               rhs=qm[:, hb, :], start=True, stop=True)
            return ins
        P.op("pe", fs, reads=[qm_r, qk_r[kt // 4]], writes=S_r)
        if NFILL:
            def ff(e):
                ins = None
                for k in range(NFILL):
                    ins = e.matmul(fill[k % 2], lhsT=Vf[:, kt, 0:128], rhs=qT[:, 0, 0:252], start=True, stop=True,
                                   skip_group_check=True)
                return ins
            P.op("pe", ff, reads=[V_r, qk_r[0]], writes=[fill_r], inc=False)
        for dfn in [d for d in deferred if d[0] <= i]:
            deferred.remove(dfn)
            dfn[1]()

    def u_e(i):
        S, S_r = c.pd[i % 2], c.pdr[i % 2]
        Pt, Pt_r = Pts[i % 2]
        P.op("act", lambda e: e.activation(out=Pt[:], in_=S[:], func=AF.Exp, scale=scale), reads=S_r, writes=[Pt_r])

    def u_o(i):
        qb, h, kt = units[i]
        Pt, Pt_r = Pts[i % 2]

        def fo(e):
            ins = None
            for br in range(2):
                ins = e.matmul(accf[br], lhsT=Vf[:, kt, h * 65:h * 65 + 128], rhs=Pt[:, br * 512:(br + 1) * 512],
                               start=(kt == 0), stop=(kt == nt - 1))
            return ins
        P.op("pe", fo, reads=[Pt_r, V_r], writes=acc_r)
        if kt == nt - 1:
            epilogue(i, qb, h)

    def epilogue(i, qb, h):
        ob, obr = obuf[qb % 2]
        for br in range(2):
            P.op("dve", lambda e, br=br: e.tensor_copy(out=OT[br][:], in_=acc[br]), reads=[acc_r[br]],
                 writes=[OT_r[br]])

        def rest():
            def ft(e):
                ins = None
                for br in range(2):
                    for qt in range(4):
                        ins = e.transpose(out=tp[br][:, qt, :], in_=OT[br][0:65, qt * 128:(qt + 1) * 128],
                                          identity=idf[0:65, 0:65])
                return ins
            P.op("pe", ft, reads=OT_r + [cst_r], writes=tp_r)
            for br in range(2):
                P.op("dve", lambda e, br=br: e.reciprocal(out=r12[:, br, :], in_=tp[br][:, :, 64]),
                     reads=[tp_r[br]], writes=[w_r])
            P.op("dve", lambda e: e.tensor_scalar(out=r12[:, 1, :], in0=r12[:, 1, :], scalar1=lam[:, 0:1], scalar2=None,
                                                  op0=ALU.mult), reads=[w_r, lv_r], writes=[w_r])
            for br in range(2):
                P.op("dve", lambda e, br=br: e.tensor_tensor(
                    out=ab[:, br], in0=tp[br][:, :, 0:64], in1=r12[:, br, :].unsqueeze(2).broadcast_to([128, 4, 64]),
                    op=ALU.mult), reads=[tp_r[br], w_r], writes=[w_r])
            P.op("dve", lambda e: e.tensor_tensor(out=dd[:], in0=ab[:, 0], in1=ab[:, 1], op=ALU.subtract),
                 reads=[w_r], writes=[w_r])
            P.op("dve", lambda e: e.tensor_tensor(out=sq[:], in0=dd[:], in1=dd[:], op=ALU.mult),
                 reads=[w_r], writes=[w_r])
            P.op("dve", lambda e: e.reduce_sum(out=ssq[:], in_=sq[:], axis=AX.X), reads=[w_r], writes=[w_r])
            P.op("dve", lambda e: e.tensor_scalar(out=ssq[:], in0=ssq[:], scalar1=1.0 / 64, scalar2=EPS, op0=ALU.mult,
                                                  op1=ALU.add), reads=[w_r], writes=[w_r])
            P.op("pool", lambda e: e.tensor_tensor(out=ssq[:], in0=ssq[:], in1=c.mhalf[:, 0:4], op=ALU.pow),
                 reads=[w_r, c.epst_r], writes=[w_r])
            P.op("dve", lambda e: e.tensor_tensor(out=dd[:], in0=dd[:],
                                                  in1=ssq[:].unsqueeze(2).broadcast_to([128, 4, 64]), op=ALU.mult),
                 reads=[w_r], writes=[w_r])
            P.op("dve", lambda e: e.tensor_tensor(
                out=ob[:, :, h * 64:(h + 1) * 64], in0=dd[:], in1=gsub[:].unsqueeze(1).broadcast_to([128, 4, 64]),
                op=ALU.mult), reads=[w_r, cst_r], writes=[obr])
            if h == 3:
                dst = c.Os[t0 + qb * 512:t0 + (qb + 1) * 512, 256:512].rearrange("(j p) c -> p j c", p=128)
                P.dma("sp", dst, ob[:], reads=[obr])
        deferred.append((i + 4, rest))
    pipeline(len(units), [u_s, u_e, u_o])
    for dfn in list(deferred):
        dfn[1]()


def mixer_ssm(c, li, t0, L, hT, hT_r):
    P, sb, nc = c.P, c.sb, c.nc
    nt, nb = L // 128, L // 512
    win = c.winb[li].rearrange("(k p) c -> p k c", p=128)
    Ws = sb.alloc([128, 8, 1032], BF16, "smW")
    Ws_r = Res()
    for k in range(0, 8, 4):
        P.dma("sp", Ws[:, k:k + 4, :], win[:, k:k + 4, OFF_SSM:OFF_SSM + 1032], reads=[c.winb_r[li]], writes=[Ws_r])
    cst_r = Res()
    tri = [sb.alloc([128, 128], F32, "smtri") for _ in range(2)]
    msk = [sb.alloc([128, 512], BF16, "smmsk") for _ in range(2)]
    onesf = sb.alloc([128, 128], F32, "smones")
    for d, nm in enumerate(("f", "b")):
        P.dma("sp", tri[d][:], c.C["tri_" + nm], writes=[cst_r])
        P.dma("sp", msk[d][:], c.C["mask_" + nm], writes=[cst_r])
    P.dma("sp", onesf[:], c.C["onesf"], writes=[cst_r])
    cw = sb.alloc([128, 6, 5], F32, "smcw")
    cb = sb.alloc([128, 6], F32, "smcb")
    dtb = sb.alloc([128, 8], F32, "smdtb")
    alog = sb.alloc([128, 8], F32, "smalog")
    Dv = sb.alloc([128, 4], F32, "smD")
    ng = sb.alloc([128, 256], F32, "smng")
    for j in range(5):
        P.dma("sp", cw[:, :, j], c.W["ssm_conv_w"][li, j].rearrange("(c p) -> p c", p=128), writes=[cst_r], slow=True)
    P.dma("sp", cb[:], c.W["ssm_conv_b"][li].rearrange("(c p) -> p c", p=128), writes=[cst_r], slow=True)
    P.dma("sp", dtb[:], c.W["ssm_dt_bias"][li:li + 1].rearrange("o a b -> o (a b)").partition_broadcast(128),
          writes=[cst_r])
    P.dma("sp", alog[:], c.W["ssm_A_log"][li:li + 1].rearrange("o a b -> o (a b)").partition_broadcast(128),
          writes=[cst_r])
    P.dma("sp", Dv[:], c.W["ssm_D"][li:li + 1, :].partition_broadcast(128), writes=[cst_r])
    P.dma("sp", ng[:], c.W["ssm_norm_g"][li:li + 1, :].partition_broadcast(128), writes=[cst_r])
    P.op("act", lambda e: e.activation(out=alog[:], in_=alog[:], func=AF.Exp), reads=[cst_r], writes=[cst_r])
    P.op("dve", lambda e: e.tensor_scalar(out=alog[:], in0=alog[:], scalar1=-1.0, scalar2=None, op0=ALU.mult),
         reads=[cst_r], writes=[cst_r])
    xcT = sb.alloc([128, 6, L], BF16, "smxc")
    xc_r = Res()
    dtv = sb.alloc([128, nt, 8], F32, "smdt")
    adt = sb.alloc([128, nt, 8], F32, "smadt")
    dt_r = Res()
    yf = sb.alloc([128, nt, 256], BF16, "smyf")
    yf_r = _rr(nt)
    psd = c.pd[3][:, 512:512 + nt * 8].rearrange("p (t e) -> p t e", e=8)

    def fdt(e):
        ins = None
        for j in range(nt):
            for k in range(8):
                ins = e.matmul(psd[:, j, :], lhsT=hT[:, k, j * 128:(j + 1) * 128], rhs=Ws[:, k, 1024:1032],
                               start=(k == 0), stop=(k == 7), skip_group_check=True)
        return ins
    P.op("pe", fdt, reads=hT_r + [Ws_r], writes=[c.pdr[3][1]])
    P.op("dve", lambda e: e.tensor_tensor(out=dtv[:], in0=psd, in1=dtb[:].unsqueeze(1).broadcast_to([128, nt, 8]),
                                          op=ALU.add), reads=[c.pdr[3][1], cst_r], writes=[dt_r])
    P.op("act", lambda e: e.activation(out=dtv[:], in_=dtv[:], func=AF.Exp), reads=[dt_r], writes=[dt_r])
    P.op("act", lambda e: e.activation(out=dtv[:], in_=dtv[:], func=AF.Ln, bias=1.0), reads=[dt_r], writes=[dt_r])
    P.op("dve", lambda e: e.tensor_tensor(out=adt[:], in0=dtv[:], in1=alog[:].unsqueeze(1).broadcast_to([128, nt, 8]),
                                          op=ALU.mult), reads=[dt_r, cst_r], writes=[dt_r])
    szall = sb.alloc([128, nt, 256], BF16, "smsz")
    sz_r = Res()
    for j in range(nt):
        bank = j % 3
        ps, pr = c.pd[bank][:, 512:768], c.pdr[bank][1]

        def fz(e, j=j, ps=ps):
            ins = None
            for k in range(8):
                ins = e.matmul(ps, lhsT=hT[:, k, j * 128:(j + 1) * 128], rhs=Ws[:, k, 0:256], start=(k == 0),
                               stop=(k == 7))
            return ins
        P.op("pe", fz, reads=[hT_r[j], Ws_r], writes=[pr])
        P.op("act", lambda e, j=j, ps=ps: e.activation(out=szall[:, j, :], in_=ps, func=AF.Silu), reads=[pr],
             writes=[sz_r])
    ov = sb.off
    idf = sb.alloc([128, 128], F32, "smidf")
    idf_r = Res()
    P.dma("sp", idf[:], c.C["identf"], writes=[idf_r])
    Dg = sb.alloc([128, 6, 5, 128], BF16, "smDg")
    Dg_r = Res()
    for ch in range(6):
        for j in range(5):
            P.op("dve", lambda e, ch=ch, j=j: e.tensor_scalar(out=Dg[:, ch, j, :], in0=idf[:],
                                                               scalar1=cw[:, ch, j:j + 1], scalar2=None, op0=ALU.mult),
                 reads=[idf_r, cst_r], writes=[Dg_r])
    pres = [(sb.alloc([128, L + 4], BF16, "smpre"), Res()) for _ in range(2)]
    for pre, pre_r in pres:
        P.op("pool", lambda e, pre=pre: e.memset(pre[:, 0:2], 0.0), writes=[pre_r])
        P.op("pool", lambda e, pre=pre: e.memset(pre[:, L + 2:L + 4], 0.0), writes=[pre_r])
    for ch in range(6):
        pre, pre_r = pres[ch % 2]
        for b in range(nb):
            bank = b % 3
            ps, pr = c.pd[bank][:, 0:512], c.pdr[bank][0]

            def f(e, ch=ch, b=b, ps=ps):
                ins = None
                for k in range(8):
                    ins = e.matmul(ps, lhsT=Ws[:, k, 256 + ch * 128:256 + (ch + 1) * 128],
                                   rhs=hT[:, k, b * 512:(b + 1) * 512], start=(k == 0), stop=(k == 7))
                return ins
            P.op("pe", f, reads=hT_r[b * 4:b * 4 + 4] + [Ws_r], writes=[pr])
            evac(c, b, pre[:, 2 + b * 512:2 + (b + 1) * 512], ps, [pr], [pre_r])
        for b in range(nb):
            ps, pr = c.pd[3][:, (b % 2) * 512:(b % 2 + 1) * 512], c.pdr[3][b % 2]

            def fc(e, ch=ch, b=b, ps=ps, pre=pre):
                ins = None
                for j in range(5):
                    ins = e.matmul(ps, lhsT=Dg[:, ch, j, :], rhs=pre[:, b * 512 + j:b * 512 + j + 512],
                                   start=(j == 0), stop=(j == 4))
                return ins
            P.op("pe", fc, reads=[pre_r, Dg_r], writes=[pr])
            P.op("act", lambda e, ch=ch, b=b, ps=ps: e.activation(out=xcT[:, ch, b * 512:(b + 1) * 512], in_=ps,
                                                                   func=AF.Silu, bias=cb[:, ch:ch + 1]),
                 reads=[pr, cst_r], writes=[xc_r])
    P.barrier()
    sb.off = ov
    hst = sb.alloc([128, 2, 256], F32, "smh")
    hbf = sb.alloc([128, 2, 256], BF16, "smhb")
    h_r = [Res(), Res()]
    obuf = [(sb.alloc([128, 4, 256], BF16, "smO"), Res()) for _ in range(2)]
    ocnt = [0] * (nt // 4)
    for d in range(2):
        P.op("pool", lambda e, d=d: e.memset(hst[:, d, :], 0.0), writes=[h_r[d]])
        P.op("pool", lambda e, d=d: e.memset(hbf[:, d, :], 0.0), writes=[h_r[d]])

    class B_:
        pass
    Bs = []
    for d in range(2):
        b_ = B_()
        b_.xb = sb.alloc([128, 4, 128], BF16, "smxb")
        b_.xtm = b_.xb[:, 0:2, :].rearrange("p k t -> p (k t)")
        b_.btm = b_.xb[:, 2:4, :].rearrange("p k t -> p (k t)")
        b_.rhsA = sb.alloc([128, 4, 128], F32, "smrA")
        b_.dec = sb.alloc([128, 4, 128], F32, "smdec")
        b_.MT = sb.alloc([128, 4, 128], BF16, "smMT")
        b_.sm4 = sb.alloc([128, 6, 4], F32, "sm4")
        b_.xdt = sb.alloc([128, 4, 64], BF16, "smxdt")
        b_.xdd = sb.alloc([128, 4, 64], BF16, "smxdd")
        b_.yt = sb.alloc([128, 4, 64], F32, "smyt")
        b_.y2 = sb.alloc([128, 256], F32, "smy2")
        b_.sq = sb.alloc([128, 256], F32, "smsq")
        b_.s2 = sb.alloc([128, 2], F32, "sms2")
        b_.tm_r, b_.w_r, b_.a_r, b_.m_r, b_.x_r, b_.c_r = Res(), Res(), Res(), Res(), Res(), Res()
        P0, P1 = c.pd[2 * d], c.pd[2 * d + 1]
        b_.accBf = P0[:, 0:512]
        b_.accB = P0[:, 0:512].rearrange("p (h l) -> p h l", h=4)
        b_.accB_r = c.pdr[2 * d][0]
        b_.cbT = P0[:, 512:768].rearrange("p (g l) -> p g l", g=2)
        b_.tpp = P0[:, 768:1024].bitcast(BF16).rearrange("p (k t) -> p k t", k=4)
        b_.p0b_r = c.pdr[2 * d][1]
        b_.yd = P1[:, 0:256].rearrange("p (h d) -> p h d", h=4)
        b_.yo = P1[:, 256:512].rearrange("p (h d) -> p h d", h=4)
        b_.y_r = c.pdr[2 * d + 1][0]
        b_.stp = P1[:, 512:768].rearrange("p (h d) -> p h d", h=4)
        b_.acol = P1[:, 768:772]
        b_.st_r = c.pdr[2 * d + 1][1]
        Bs.append(b_)

    def chunk(d, i):
        b_ = Bs[d]
        cidx = i if d == 0 else nt - 1 - i
        first = i < nt // 2
        last = 127 if d == 0 else 0
        cs = slice(cidx * 128, (cidx + 1) * 128)
        xtm, btm, rhsA, dec, MT, sm4, xdt, xdd, yt, y2, sq, s2 = (b_.xtm, b_.btm, b_.rhsA, b_.dec, b_.MT, b_.sm4, b_.xdt,
                                                                  b_.xdd, b_.yt, b_.y2, b_.sq, b_.s2)

        def ftp(e):
            ins = None
            for k in range(4):
                ins = e.transpose(out=b_.tpp[:, k, :], in_=xcT[:, k, cs], identity=c.ident[:])
            return ins
        P.op("pe", ftp, reads=[xc_r, c.ident_r], writes=[b_.p0b_r])
        P.op("act", lambda e: e.copy(out=b_.xb[:], in_=b_.tpp), reads=[b_.p0b_r], writes=[b_.tm_r])
        av = adt[:, cidx, d * 4:(d + 1) * 4]
        P.op("pool", lambda e: e.tensor_tensor(
            out=rhsA[:], in0=tri[d][:].unsqueeze(1).broadcast_to([128, 4, 128]),
            in1=av.unsqueeze(2).broadcast_to([128, 4, 128]), op=ALU.mult), reads=[dt_r, cst_r], writes=[b_.a_r])

        def facc(e):
            e.matmul(b_.accBf, lhsT=onesf[:], rhs=rhsA[:].rearrange("p h l -> p (h l)"), start=True,
                     stop=False, skip_group_check=True)
            e.matmul(b_.accBf, lhsT=c.ident[:], rhs=msk[d][:], start=False, stop=True, skip_group_check=True)
            return e.matmul(b_.acol, lhsT=tri[d][:], rhs=av, start=True, stop=True, skip_group_check=True)
        P.op("pe", facc, reads=[b_.a_r, cst_r, dt_r, c.ident_r], writes=[b_.accB_r, b_.st_r])

        def fcb(e):
            ins = None
            for g in range(2):
                ins = e.matmul(b_.cbT[:, g, :], lhsT=xcT[:, 2 + g, cs], rhs=xcT[:, 4 + g, cs], start=True, stop=True,
                               skip_group_check=True)
            return ins
        P.op("pe", fcb, reads=[xc_r], writes=[b_.p0b_r])
        P.op("dve", lambda e: e.tensor_scalar(out=sm4[:, 0, :], in0=b_.acol, scalar1=-1.0, scalar2=None, op0=ALU.mult),
             reads=[b_.st_r], writes=[b_.w_r])
        P.op("act", lambda e: e.activation(out=sm4[:, 1, :], in_=b_.acol, func=AF.Exp), reads=[b_.st_r],
             writes=[b_.w_r])
        P.op("dve", lambda e: e.tensor_tensor(out=sm4[:, 4, :], in0=b_.accB[:, :, last], in1=sm4[:, 0, :], op=ALU.add),
             reads=[b_.accB_r, b_.w_r], writes=[b_.w_r])
        P.op("act", lambda e: e.activation(out=sm4[:, 2, :], in_=sm4[:, 4, :], func=AF.Exp), reads=[b_.w_r],
             writes=[b_.w_r])
        P.op("act", lambda e: e.activation(out=sm4[:, 3, :], in_=b_.accB[:, :, last], func=AF.Exp),
             reads=[b_.accB_r], writes=[b_.w_r])
        P.op("dve", lambda e: e.tensor_tensor(out=dec[:], in0=b_.accB,
                                              in1=sm4[:, 0, :].unsqueeze(2).broadcast_to([128, 4, 128]), op=ALU.add),
             reads=[b_.accB_r, b_.w_r], writes=[b_.m_r])
        P.op("act", lambda e: e.activation(out=dec[:], in_=dec[:], func=AF.Exp), reads=[b_.m_r], writes=[b_.m_r])
        P.op("dve", lambda e: e.tensor_tensor(
            out=MT[:].rearrange("p (g k) l -> p g k l", g=2),
            in0=b_.cbT.unsqueeze(2).broadcast_to([128, 2, 2, 128]),
            in1=dec[:].rearrange("p (g k) l -> p g k l", g=2), op=ALU.mult), reads=[b_.p0b_r, b_.m_r],
            writes=[b_.m_r])
        dv = dtv[:, cidx, d * 4:(d + 1) * 4]
        P.op("pool", lambda e: e.tensor_tensor(
            out=xdt[:], in0=xtm.rearrange("p (h d) -> p h d", h=4),
            in1=dv.unsqueeze(2).broadcast_to([128, 4, 64]), op=ALU.mult), reads=[b_.tm_r, dt_r], writes=[b_.x_r])
        P.op("pool", lambda e: e.tensor_tensor(
            out=xdd[:], in0=xdt[:], in1=sm4[:, 2, :].unsqueeze(2).broadcast_to([128, 4, 64]), op=ALU.mult),
            reads=[b_.x_r, b_.w_r], writes=[b_.x_r])

        def fy(e):
            ins = None
            for h in range(4):
                ins = e.matmul(b_.yd[:, h, :], lhsT=MT[:, h, :], rhs=xdt[:, h, :], start=True, stop=True,
                               skip_group_check=True)
            for g in range(2):
                ins = e.matmul(b_.yo[:, 2 * g:2 * g + 2, :], lhsT=xcT[:, 4 + g, cs],
                               rhs=hbf[:, d, g * 128:(g + 1) * 128], start=True, stop=True, skip_group_check=True)
            return ins
        P.op("pe", fy, reads=[b_.m_r, b_.x_r, xc_r, h_r[d]], writes=[b_.y_r])

        def fst(e):
            ins = None
            for g in range(2):
                ins = e.matmul(b_.stp[:, 2 * g:2 * g + 2, :], lhsT=btm[:, g * 128:(g + 1) * 128],
                               rhs=xdd[:, 2 * g:2 * g + 2, :], start=True, stop=True, skip_group_check=True)
            return ins
        P.op("pe", fst, reads=[b_.tm_r, b_.x_r], writes=[b_.st_r])
        hv = hst[:, d, :].rearrange("p (h e) -> p h e", h=4)
        P.op("dve", lambda e: e.tensor_tensor(out=hv, in0=hv, in1=sm4[:, 3, :].unsqueeze(2).broadcast_to([128, 4, 64]),
                                              op=ALU.mult), reads=[b_.w_r, h_r[d], b_.y_r], writes=[h_r[d]])
        P.op("dve", lambda e: e.tensor_tensor(out=hv, in0=b_.stp, in1=hv, op=ALU.add), reads=[b_.st_r, h_r[d]],
             writes=[h_r[d]])
        P.op("act", lambda e: e.copy(out=hbf[:, d, :], in_=hst[:, d, :]), reads=[h_r[d]], writes=[h_r[d]])
        P.op("dve", lambda e: e.tensor_tensor(out=yt[:], in0=b_.yo,
                                              in1=sm4[:, 1, :].unsqueeze(2).broadcast_to([128, 4, 64]), op=ALU.mult),
             reads=[b_.y_r, b_.w_r], writes=[b_.c_r])
        if first:
            P.op("dve", lambda e: e.tensor_tensor(
                out=yf[:, cidx, :].rearrange("p (h d) -> p h d", h=4), in0=b_.yd, in1=yt[:], op=ALU.add),
                reads=[b_.y_r, b_.c_r], writes=[yf_r[cidx]])
            return
        y2v = y2[:].rearrange("p (h d) -> p h d", h=4)
        P.op("dve", lambda e: e.tensor_tensor(out=y2v, in0=b_.yd, in1=yt[:], op=ALU.add), reads=[b_.y_r, b_.c_r],
             writes=[b_.c_r])
        P.op("dve", lambda e: e.tensor_tensor(out=y2[:], in0=y2[:], in1=yf[:, cidx, :], op=ALU.add),
             reads=[b_.c_r, yf_r[cidx]], writes=[b_.c_r])
        P.op("dve", lambda e: e.tensor_tensor(
            out=yt[:], in0=xtm.rearrange("p (h d) -> p h d", h=4),
            in1=Dv[:].unsqueeze(2).broadcast_to([128, 4, 64]), op=ALU.mult), reads=[b_.tm_r, cst_r, b_.c_r],
            writes=[b_.c_r])
        P.op("dve", lambda e: e.tensor_tensor(out=y2v, in0=y2v, in1=yt[:], op=ALU.add), reads=[b_.c_r],
             writes=[b_.c_r])
        P.op("dve", lambda e: e.tensor_tensor(out=y2[:], in0=y2[:], in1=szall[:, cidx, :], op=ALU.mult),
             reads=[b_.c_r, sz_r], writes=[b_.c_r])
        P.op("dve", lambda e: e.tensor_tensor(out=sq[:], in0=y2[:], in1=y2[:], op=ALU.mult), reads=[b_.c_r],
             writes=[b_.c_r])
        P.op("dve", lambda e: e.reduce_sum(out=s2[:], in_=sq[:].rearrange("p (g d) -> p g d", g=2), axis=AX.X),
             reads=[b_.c_r], writes=[b_.c_r])
        P.op("dve", lambda e: e.tensor_scalar(out=s2[:], in0=s2[:], scalar1=1.0 / 128, scalar2=EPS, op0=ALU.mult,
                                              op1=ALU.add), reads=[b_.c_r], writes=[b_.c_r])
        P.op("pool", lambda e: e.tensor_tensor(out=s2[:], in0=s2[:], in1=c.mhalf[:, 0:2], op=ALU.pow),
             reads=[b_.c_r, c.epst_r], writes=[b_.c_r])
        P.op("dve", lambda e: e.tensor_tensor(
            out=y2[:].rearrange("p (g d) -> p g d", g=2), in0=y2[:].rearrange("p (g d) -> p g d", g=2),
            in1=s2[:].unsqueeze(2).broadcast_to([128, 2, 128]), op=ALU.mult), reads=[b_.c_r], writes=[b_.c_r])
        bq = cidx // 4
        ob, obr = obuf[d]
        P.op("pool", lambda e: e.tensor_tensor(out=ob[:, cidx % 4, :], in0=y2[:], in1=ng[:], op=ALU.mult),
             reads=[b_.c_r, cst_r], writes=[obr])
        ocnt[bq] += 1
        if ocnt[bq] == 4:
            dst = c.Os[t0 + bq * 512:t0 + (bq + 1) * 512, 768:1024].rearrange("(j p) c -> p j c", p=128)
            P.dma("sp", dst, ob[:], reads=[obr])
    for i in range(nt):
        la = P.capture(lambda: chunk(0, i))
        lb = P.capture(lambda: chunk(1, i))
        P.replay([la, lb])


MIXERS = {"fn": mixer_fn, "na": mixer_na, "da": mixer_da, "ssm": mixer_ssm}


SEQ_LENS = (4096, 2048, 2048)
_CACHE = {}


def kernel(**inputs):
    xp = np.ascontiguousarray(inputs["x_prompt"], dtype=np.float32)
    xs = np.ascontiguousarray(inputs["x_sample"], dtype=np.float32)
    if "nc" not in _CACHE:
        _CACHE["nc"] = build(SEQ_LENS)
        _CACHE["consts"] = make_consts(SEQ_LENS)
    nc = _CACHE["nc"]
    consts = _CACHE["consts"]
    wts = {k: np.ascontiguousarray(inputs[k], dtype=np.float32) for k in WEIGHT_SHAPES}
    in_maps = []
    for i in range(8):
        xin = np.concatenate([xs[i], xp[2 * i], xp[2 * i + 1]], axis=0)
        m = {"xin": xin}
        m.update(wts)
        m.update(consts)
        in_maps.append(m)
    res = run_bass_kernel_spmd(nc, in_maps, core_ids=list(range(8)))
    yp = np.empty_like(xp)
    ys = np.empty_like(xs)
    for i in range(8):
        yy = res.results[i]["y"]
        ys[i] = yy[0:4096]
        yp[2 * i] = yy[4096:6144]
        yp[2 * i + 1] = yy[6144:8192]
    return (yp, ys)
```

```python
import math
import numpy as np
import ml_dtypes
import concourse.bass as bass
import concourse.mybir as mybir
from concourse.bass_utils import run_bass_kernel_spmd

F32 = mybir.dt.float32
BF16 = mybir.dt.bfloat16
AF = mybir.ActivationFunctionType
ALU = mybir.AluOpType
AX = mybir.AxisListType

D = 1024
DEPTH = 4
GW = 256
D_IN = 2824
D_FF = 2816
NFF = D_FF // 128
OFF_FN, OFF_DA, OFF_NA, OFF_SSM = 0, 256, 1024, 1792
EPS = 1e-6
NEG = -240000.0
SBUF_BASE = 16640
SBUF_LIMIT = 229376
NRING = 12
DA_NFILL = 0


class Res:
    __slots__ = ("w", "r")

    def __init__(self):
        self.w = {}
        self.r = {}


def _merge(dst, src):
    for k, v in src.items():
        if dst.get(k, 0) < v:
            dst[k] = v


class Prog:
    ENG = ("pe", "act", "dve", "pool", "sp")

    def __init__(self, nc, esems, rings):
        self.nc = nc
        self.h = {"pe": nc.tensor, "act": nc.scalar, "dve": nc.vector, "pool": nc.gpsimd, "sp": nc.sync}
        self.esem = esems
        self.cnt = {e: 0 for e in self.ENG}
        self.seen = {e: {} for e in self.ENG}
        self.lists = {e: [] for e in self.ENG}
        self.ring = rings
        self.ringcnt = {q: [0] * len(rings[q]) for q in rings}
        self.ringpos = {q: 0 for q in rings}
        self.ninst = 0
        self._cap = None

    def capture(self, fn):
        old = self._cap
        self._cap = []
        fn()
        ops, self._cap = self._cap, old
        return ops

    def replay(self, lists, weights=None):
        idx = [0] * len(lists)
        weights = weights or [1] * len(lists)
        while any(idx[k] < len(lists[k]) for k in range(len(lists))):
            for k in range(len(lists)):
                for _ in range(weights[k]):
                    if idx[k] < len(lists[k]):
                        kind, args, kw = lists[k][idx[k]]
                        idx[k] += 1
                        (self.op if kind == "op" else self.dma)(*args, **kw)

    def _waits(self, e, toks):
        out = []
        seen = self.seen[e]
        for k, v in toks.items():
            if e == "pe" and k == self.esem["pe"]:
                continue
            if seen.get(k, 0) < v:
                seen[k] = v
                out.append((k, v))
        return out

    def op(self, e, fn, reads=(), writes=(), inc=True):
        if self._cap is not None:
            self._cap.append(("op", (e, fn), dict(reads=reads, writes=writes, inc=inc)))
            return
        toks = {}
        for r in reads:
            _merge(toks, r.w)
        for w in writes:
            _merge(toks, w.w)
            _merge(toks, w.r)
        waits = self._waits(e, toks)
        sem = self.esem[e]
        if inc:
            self.cnt[e] += 1
            val = self.cnt[e]
        else:
            val = self.cnt[e] + 1
        for r in reads:
            if r.r.get(sem, 0) < val:
                r.r[sem] = val
        for w in writes:
            if w.w.get(sem, 0) < val:
                w.w[sem] = val
        self.lists[e].append((waits, fn, (sem, 1) if inc else None))
        self.ninst += 1

    def dma(self, q, out, in_, reads=(), writes=(), slow=False):
        if self._cap is not None:
            self._cap.append(("dma", (q, out, in_), dict(reads=reads, writes=writes, slow=slow)))
            return
        k = self.ringpos[q]
        self.ringpos[q] = (k + 1) % len(self.ring[q])
        sem = self.ring[q][k]
        toks = {}
        for r in reads:
            _merge(toks, r.w)
        for w in writes:
            _merge(toks, w.w)
            _merge(toks, w.r)
        prev = self.ringcnt[q][k]
        if prev > 0:
            toks[sem] = max(toks.get(sem, 0), 16 * prev)
        waits = self._waits(q, toks)
        self.ringcnt[q][k] += 1
        val = 16 * self.ringcnt[q][k]
        for r in reads:
            if r.r.get(sem, 0) < val:
                r.r[sem] = val
        for w in writes:
            if w.w.get(sem, 0) < val:
                w.w[sem] = val
        self.lists[q].append((waits, lambda eng: eng.dma_start(out=out, in_=in_, allow_slow_non_contiguous=slow), (sem, 16)))
        self.ninst += 1

    def barrier(self):
        toks = {}
        for e in self.ENG:
            if self.cnt[e] > 0:
                toks[self.esem[e]] = self.cnt[e]
        for q in self.ring:
            for k, s in enumerate(self.ring[q]):
                if self.ringcnt[q][k] > 0:
                    toks[s] = 16 * self.ringcnt[q][k]
        for e in self.ENG:
            waits = self._waits(e, dict(toks))
            if waits:
                self.lists[e].append((waits, None, None))

    def emit(self, e, eng):
        for waits, fn, inc in self.lists[e]:
            for s, v in waits:
                eng.wait_ge(s, v)
            if fn is not None:
                ins = fn(eng)
                if inc is not None:
                    ins.then_inc(inc[0], inc[1])


class SB:
    def __init__(self, nc):
        self.nc = nc
        self.off = SBUF_BASE
        self.n = 0

    def alloc(self, shape, dt, name="t"):
        esz = 2 if dt == BF16 else 4
        nb = esz
        for s in shape[1:]:
            nb *= s
        nb = (nb + 31) // 32 * 32
        assert self.off + nb <= SBUF_LIMIT, f"SBUF overflow {name}: {self.off}+{nb}"
        self.n += 1
        t = self.nc.alloc_sbuf_tensor_at(f"{name}{self.n}", list(shape), dt, offset=self.off)
        self.off += nb
        return t


def _rr(n):
    return [Res() for _ in range(n)]


def pipeline(n, stages, skew=1):
    ns = len(stages)
    for step in range(n + (ns - 1) * skew):
        for s_ in range(ns - 1, -1, -1):
            i = step - s_ * skew
            if 0 <= i < n:
                stages[s_](i)


WEIGHT_SHAPES = {
    "attn_norm_g": (DEPTH, D), "w_in": (DEPTH, D, D_IN), "w_fourier": (DEPTH, GW, GW),
    "diff_lambda": (DEPTH, 4, 32), "diff_subln_g": (DEPTH, 64), "na_rpb": (DEPTH, 4, 15, 31),
    "ssm_conv_w": (DEPTH, 5, 768), "ssm_conv_b": (DEPTH, 768), "ssm_dt_bias": (DEPTH, 2, 4),
    "ssm_A_log": (DEPTH, 2, 4), "ssm_D": (DEPTH, 4), "ssm_norm_g": (DEPTH, GW),
    "w_out": (DEPTH, D, D), "ffn_norm_g": (DEPTH, D), "w_gate": (DEPTH, D, D_FF),
    "w_up": (DEPTH, D, D_FF), "w_down": (DEPTH, D_FF, D), "final_norm_g": (D,),
}


class Ctx:
    pass


def build(seq_lens, depth=DEPTH, mix=("fn", "da", "na", "ssm"), do_ffn=True):
    from contextlib import ExitStack
    T = sum(seq_lens)
    nc = bass.Bass("TRN2", target_bir_lowering=False)
    c = Ctx()
    c.nc, c.T, c.seq_lens, c.depth, c.mix = nc, T, seq_lens, depth, mix

    def din(name, shape, dt=F32):
        return nc.dram_tensor(name, list(shape), dt, kind="ExternalInput").ap()

    c.xin = din("xin", [T, D])
    c.W = {name: din(name, shape) for name, shape in WEIGHT_SHAPES.items()}
    c.C = {name: din(name, shape, dt) for name, (shape, dt) in const_specs(seq_lens).items()}
    c.y = nc.dram_tensor("y", [T, D], F32, kind="ExternalOutput").ap()
    c.rpad = nc.dram_tensor("rpad_scr", [128 + 1860 + 128], F32)
    c.winb = nc.dram_tensor("winb_scr", [depth, D, D_IN], BF16).ap()
    c.woutb = nc.dram_tensor("woutb_scr", [depth, D, D], BF16).ap()
    c.winb_r = [Res() for _ in range(depth)]
    import os as _os
    c.Xs = nc.dram_tensor("xs_scr", [T, D], F32, kind=("ExternalOutput" if _os.environ.get("KDEBUG") else "Internal")).ap()
    import os as _os
    c.Os = nc.dram_tensor("os_scr", [T, D], BF16, kind=("ExternalOutput" if _os.environ.get("KDEBUG") else "Internal")).ap()

    with ExitStack() as es:
        esems = {e: es.enter_context(nc.semaphore(f"s_{e}")) for e in Prog.ENG}
        rings = {q: [es.enter_context(nc.semaphore(f"r_{q}{i}")) for i in range(NRING)] for q in ("sp", "pool")}
        P = Prog(nc, esems, rings)
        c.P = P
        c.sb = SB(nc)
        c.pd = [nc.alloc_psum_tensor(f"pd{i}", [128, 1024], F32) for i in range(4)]
        c.pdr = [[Res(), Res()] for _ in range(4)]
        record(c, do_ffn)
        blk = es.enter_context(nc.Block())

        @blk.tensor
        def _(e):
            P.emit("pe", e)

        @blk.scalar
        def _(e):
            P.emit("act", e)

        @blk.vector
        def _(e):
            P.emit("dve", e)

        @blk.gpsimd
        def _(e):
            P.emit("pool", e)

        @blk.sync
        def _(e):
            P.emit("sp", e)
    return nc


def const_specs(seq_lens):
    s = {"ident": ([128, 128], BF16), "cs64": ([256, 512], BF16), "dlo": ([128, 128], BF16),
         "dhi": ([128, 128], BF16), "negt": ([128, 128], BF16), "na_wm": ([128, 64], F32), "identf": ([128, 128], F32),
         "ropec": ([128, max(seq_lens)], BF16), "ropes": ([128, max(seq_lens)], BF16),
         "mask4": ([128, 4], F32), "mask2": ([128, 2], F32), "tri_f": ([128, 128], F32), "tri_b": ([128, 128], F32), "onesf": ([128, 128], F32),
         "mask_f": ([128, 512], BF16), "mask_b": ([128, 512], BF16)}
    for L in sorted(set(seq_lens)):
        s[f"dftc{L}"] = ([L, L], BF16)
        s[f"dfts{L}"] = ([L, L], BF16)
        s[f"alt{L}"] = ([128, 1], BF16)
    return s


def make_consts(seq_lens):
    bf = ml_dtypes.bfloat16
    out = {"ident": np.eye(128, dtype=np.float32).astype(bf)}
    dlo = np.zeros((128, 128), np.float32)
    dlo[np.arange(64), np.arange(64)] = 1.0
    dhi = np.zeros((128, 128), np.float32)
    dhi[np.arange(64, 128), np.arange(64, 128)] = 1.0
    out["dlo"], out["dhi"] = dlo.astype(bf), dhi.astype(bf)
    out["negt"] = np.full((128, 128), NEG, np.float32).astype(bf)
    wm = np.zeros((128, 64), np.float32)
    for p in range(128):
        col = p % 64
        for qp in range(64):
            q = 63 - qp
            sc = min(max(q - 8, 0), 48)
            wm[p, qp] = 0.0 if sc <= col < sc + 16 else NEG / 8.0
    out["na_wm"] = wm
    out["identf"] = np.eye(128, dtype=np.float32)
    Lm = max(seq_lens)
    fi = (np.arange(128) % 16).astype(np.float64)
    inv = np.power(10000.0, -fi / 16.0)
    ra = np.arange(Lm, dtype=np.float64)[None, :] * inv[:, None]
    out["ropec"] = np.cos(ra).astype(np.float32).astype(bf)
    out["ropes"] = np.sin(ra).astype(np.float32).astype(bf)
    out["mask2"] = (np.arange(128)[:, None] // 64 == np.arange(2)[None, :]).astype(np.float32)
    out["mask4"] = (np.arange(128)[:, None] // 32 == np.arange(4)[None, :]).astype(np.float32)
    jj, ll = np.arange(128)[:, None], np.arange(128)[None, :]
    out["tri_f"] = (jj <= ll).astype(np.float32)
    out["tri_b"] = (jj >= ll).astype(np.float32)
    out["onesf"] = np.ones((128, 128), np.float32)
    out["mask_f"] = np.tile(np.where(ll < jj, NEG, 0.0).astype(np.float32), (1, 4)).astype(bf)
    out["mask_b"] = np.tile(np.where(ll > jj, NEG, 0.0).astype(np.float32), (1, 4)).astype(bf)
    i64 = np.arange(64)
    ang = 2.0 * np.pi * ((i64[:, None] * i64[None, :]) % 64) / 64.0
    c64, s64 = np.cos(ang) / 8.0, np.sin(ang) / 8.0
    cs = np.zeros((256, 512), np.float64)
    for g in range(4):
        cs[g * 64:(g + 1) * 64, g * 64:(g + 1) * 64] = c64
        cs[g * 64:(g + 1) * 64, 256 + g * 64:256 + (g + 1) * 64] = -s64
    out["cs64"] = cs.astype(np.float32).astype(bf)
    for L in sorted(set(seq_lens)):
        il = np.arange(L, dtype=np.int64)
        a = 2.0 * np.pi * ((il[:, None] * il[None, :]) % L).astype(np.float64) / L
        out[f"dftc{L}"] = (np.cos(a) / np.sqrt(L)).astype(np.float32).astype(bf)
        out[f"dfts{L}"] = (np.sin(a) / np.sqrt(L)).astype(np.float32).astype(bf)
        out[f"alt{L}"] = (((-1.0) ** np.arange(128)) / np.sqrt(L)).astype(np.float32).reshape(128, 1).astype(bf)
    return out


def wload(c, q, dst, src, res):
    c.P.dma(q, dst, src, writes=[res])


def rms_rstd(c, ss, rstd, n, res_ss, res_rstd):
    P = c.P
    P.op("dve", lambda e: e.tensor_scalar(out=rstd[:, 0:n], in0=ss[:, 0:n], scalar1=1.0 / D, scalar2=EPS,
                                           op0=ALU.mult, op1=ALU.add), reads=[res_ss], writes=[res_rstd])
    P.op("act", lambda e: e.activation(out=rstd[:, 0:n], in_=rstd[:, 0:n], func=AF.Sqrt),
         reads=[res_rstd], writes=[res_rstd])
    P.op("dve", lambda e: e.reciprocal(out=rstd[:, 0:n], in_=rstd[:, 0:n]), reads=[res_rstd], writes=[res_rstd])


def transpose8(c, src_tile, src_res, dst_ap, dst_res, bank, evac="act", split=False, half=0):
    P = c.P
    pt = c.pd[bank][:, half * 512:(half + 1) * 512].bitcast(BF16).rearrange("p (k t) -> p k t", k=8)
    pr = c.pdr[bank][half]

    def f(e):
        ins = None
        for k in range(8):
            ins = e.transpose(out=pt[:, k, :], in_=src_tile[:, k * 128:(k + 1) * 128], identity=c.ident[:])
        return ins

    def part(which):
        if which == 0:
            P.op("pe", f, reads=[src_res, c.ident_r], writes=[pr])
        elif evac == "act":
            P.op("act", lambda e: e.copy(out=dst_ap, in_=pt), reads=[pr], writes=[dst_res])
        else:
            P.op("dve", lambda e: e.tensor_copy(out=dst_ap, in_=pt), reads=[pr], writes=[dst_res])
    if split:
        return part
    part(0)
    part(1)


def precast_weights(c, li):
    for k in range(8):
        c.P.dma("pool", c.winb[li, k * 128:(k + 1) * 128, :], c.W["w_in"][li, k * 128:(k + 1) * 128, :],
                writes=[c.winb_r[li]])
    for k in range(0, 8, 2):
        c.P.dma("pool", c.woutb[li, k * 128:(k + 2) * 128, :], c.W["w_out"][li, k * 128:(k + 2) * 128, :],
                writes=[c.winb_r[li]])


def record(c, do_ffn):
    P, sb, nc = c.P, c.sb, c.nc
    c.ident = sb.alloc([128, 128], BF16, "ident")
    c.ident_r = Res()
    P.dma("sp", c.ident[:], c.C["ident"], writes=[c.ident_r])
    c.epst = sb.alloc([128, 1], F32, "epst")
    c.epst_r = Res()
    P.op("pool", lambda e: e.memset(c.epst[:], EPS), writes=[c.epst_r])
    c.mhalf = sb.alloc([128, 8], F32, "mhalf")
    P.op("pool", lambda e: e.memset(c.mhalf[:], -0.5), writes=[c.epst_r])
    c.gbc = sb.alloc([128, D], F32, "gbc")
    c.gbc_r = Res()
    c.base = sb.off
    tok0 = 0
    precast_weights(c, 0)
    for li in range(c.depth):
        src = c.xin if li == 0 else c.Xs
        sb.off = c.base
        P.barrier()
        P.dma("sp", c.gbc[:], c.W["attn_norm_g"][li:li + 1, :].partition_broadcast(128), writes=[c.gbc_r])
        t0 = 0
        wg_pre = None
        for si, L in enumerate(c.seq_lens):
            hook = None
            sb.off = c.base
            if do_ffn and len(c.seq_lens) > 1 and si == len(c.seq_lens) - 1 and L <= 2048:
                WgP, WuP, _ = ffn_weight_tiles(c)
                sb.off = c.base + 8 * D_FF * 2
                wg_pre = (Res(), Res())

                def hook(name, WgP=WgP, WuP=WuP, r=wg_pre, li=li):
                    if name == "da":
                        load_wg(c, li, WgP, r[0])

            phase_m(c, li, src, t0, L, hook)
            t0 += L
        sb.off = c.base
        P.barrier()
        phase_f(c, li, do_ffn, wg_pre)
    P.barrier()


def phase_m(c, li, src, t0, L, hook=None):
    P, sb = c.P, c.sb
    nt = L // 128
    m0 = sb.off
    P.barrier()
    hT = sb.alloc([128, 8, L], BF16, "hT")
    hT_r = _rr(nt)
    ss = sb.alloc([128, nt], F32, "ss")
    rstd = sb.alloc([128, nt], F32, "rstd")
    ss_r, rstd_r = _rr(nt), _rr(nt)
    mm = sb.off
    NXA, NHA = 6, 3
    xts = [sb.alloc([128, D], F32, "xt") for _ in range(NXA)]
    xts_r = _rr(NXA)
    hts = [sb.alloc([128, D], BF16, "ht") for _ in range(NHA)]
    hts_r = _rr(NHA)
    junk = sb.alloc([128, D], BF16, "junk")
    junk_r = Res()

    def a0(j):
        xt, xr = xts[j % NXA], xts_r[j % NXA]
        P.dma("sp", xt[:], src[t0 + j * 128:t0 + (j + 1) * 128, :], writes=[xr])
        P.op("act", lambda e: e.activation(out=junk[:], in_=xt[:], func=AF.Square, accum_out=ss[:, j:j + 1]),
             reads=[xr], writes=[junk_r, ss_r[j]])

    def a1(j):
        xt, xr = xts[j % NXA], xts_r[j % NXA]
        ht, hr = hts[j % NHA], hts_r[j % NHA]
        P.op("act", lambda e: e.activation(out=rstd[:, j:j + 1], in_=ss[:, j:j + 1], func=AF.Sqrt,
                                           bias=c.epst[:, 0:1], scale=1.0 / D),
             reads=[ss_r[j], c.epst_r], writes=[rstd_r[j]])
        P.op("dve", lambda e: e.reciprocal(out=rstd[:, j:j + 1], in_=rstd[:, j:j + 1]),
             reads=[rstd_r[j]], writes=[rstd_r[j]])
        P.op("dve", lambda e: e.scalar_tensor_tensor(
            out=ht[:], in0=xt[:], scalar=rstd[:, j:j + 1], in1=c.gbc[:], op0=ALU.mult, op1=ALU.mult),
            reads=[xr, rstd_r[j], c.gbc_r], writes=[hr])

    def a2(j):
        ht, hr = hts[j % NHA], hts_r[j % NHA]
        transpose8(c, ht, hr, hT[:, :, j * 128:(j + 1) * 128], hT_r[j], bank=j % 2, evac="act", split=True)(0)

    def a3(j):
        ht, hr = hts[j % NHA], hts_r[j % NHA]
        transpose8(c, ht, hr, hT[:, :, j * 128:(j + 1) * 128], hT_r[j], bank=j % 2, evac="act", split=True)(1)
    pipeline(nt, [a0, a1, a2, a3])
    for name in ("fn", "da", "na", "ssm"):
        sb.off = mm
        P.barrier()
        if hook is not None:
            hook(name)
        if name in c.mix:
            MIXERS[name](c, li, t0, L, hT, hT_r)
        else:
            zero_mixer(c, name, t0, L)
    sb.off = mm
    P.barrier()
    mixer_out_proj(c, li, src, t0, L)
    sb.off = m0


def zero_mixer(c, name, t0, L):
    P, sb = c.P, c.sb
    col = {"fn": 0, "da": 256, "na": 512, "ssm": 768}[name]
    z = sb.alloc([128, 4, 256], BF16, "zz")
    zr = Res()
    P.op("pool", lambda e: e.memset(z[:], 0.0), writes=[zr])
    for b in range(L // 512):
        dst = c.Os[t0 + b * 512:t0 + (b + 1) * 512, col:col + 256].rearrange("(j p) c -> p j c", p=128)
        P.dma("sp", dst, z[:], reads=[zr])


def mixer_out_proj(c, li, src, t0, L):
    P, sb = c.P, c.sb
    nt = L // 128
    Wo = sb.alloc([128, 8, D], BF16, "Wo")
    Wo_r = Res()
    wv = c.woutb[li].rearrange("(k p) c -> p k c", p=128)
    for k in range(0, 8, 4):
        P.dma("sp", Wo[:, k:k + 4, :], wv[:, k:k + 4, :], reads=[c.winb_r[li]], writes=[Wo_r])
    NX, NO = 6, 5
    xts = [sb.alloc([128, D], F32, "xo") for _ in range(NX)]
    xts_r = _rr(NX)
    ots = [sb.alloc([128, D], BF16, "ot") for _ in range(NO)]
    ots_r = _rr(NO)
    oTs = [sb.alloc([128, 8, 128], BF16, "oT") for _ in range(2)]
    oTs_r = _rr(2)

    def s0(j):
        r0 = t0 + j * 128
        P.dma("sp", xts[j % NX][:], src[r0:r0 + 128, :], writes=[xts_r[j % NX]])
        P.dma("sp", ots[j % NO][:], c.Os[r0:r0 + 128, :], writes=[ots_r[j % NO]])

    def s1(j):
        transpose8(c, ots[j % NO], ots_r[j % NO], oTs[j % 2][:], oTs_r[j % 2], bank=0, half=j % 2, evac="act",
                   split=True)(0)

    def s2(j):
        transpose8(c, ots[j % NO], ots_r[j % NO], oTs[j % 2][:], oTs_r[j % 2], bank=0, half=j % 2, evac="act",
                   split=True)(1)

    def s3(j):
        xt, xr = xts[j % NX], xts_r[j % NX]
        oT, oTr = oTs[j % 2], oTs_r[j % 2]
        for half in range(2):
            bank = 1 + (j * 2 + half) % 3
            ps = c.pd[bank][:, 0:512]
            pr = c.pdr[bank][0]

            def f(e, half=half, ps=ps):
                ins = None
                for k in range(8):
                    ins = e.matmul(ps, lhsT=oT[:, k, :], rhs=Wo[:, k, half * 512:(half + 1) * 512],
                                   start=(k == 0), stop=(k == 7))
                return ins
            P.op("pe", f, reads=[oTr, Wo_r], writes=[pr])

    def s4(j):
        xt, xr = xts[j % NX], xts_r[j % NX]
        r0 = t0 + j * 128
        for half in range(2):
            bank = 1 + (j * 2 + half) % 3
            ps = c.pd[bank][:, 0:512]
            pr = c.pdr[bank][0]
            P.op("dve", lambda e, half=half, ps=ps: e.tensor_tensor(
                out=xt[:, half * 512:(half + 1) * 512], in0=ps, in1=xt[:, half * 512:(half + 1) * 512], op=ALU.add),
                reads=[pr, xr], writes=[xr])
        P.dma("sp", c.Xs[r0:r0 + 128, :], xt[:], reads=[xr])
    pipeline(nt, [s0, s1, s2, s3, s4])


def ffn_weight_tiles(c):
    sb = c.sb
    Wg = sb.alloc([128, 8, D_FF], BF16, "Wg")
    Wu = sb.alloc([128, 8, D_FF], BF16, "Wu")
    Wdn = sb.alloc([128, NFF, D], BF16, "Wdn")
    return Wg, Wu, Wdn


def load_wg(c, li, Wg, Wg_r):
    gv = c.W["w_gate"][li].rearrange("(k p) c -> p k c", p=128)
    for k in range(8):
        c.P.dma("pool", Wg[:, k, :], gv[:, k, :], writes=[Wg_r])


def phase_f(c, li, do_ffn, wg_pre=None):
    P, sb = c.P, c.sb
    last = (li == c.depth - 1)
    T = c.T
    Wg, Wu, Wdn = ffn_weight_tiles(c)
    Wg_r, Wu_r, Wdn_r = Res(), Res(), Res()
    if wg_pre is not None:
        Wg_r = wg_pre[0]
    if do_ffn:
        uv = c.W["w_up"][li].rearrange("(k p) c -> p k c", p=128)
        dv = c.W["w_down"][li].rearrange("(k p) c -> p k c", p=128)
        if wg_pre is None:
            load_wg(c, li, Wg, Wg_r)
        for k in range(8):
            P.dma("pool", Wu[:, k, :], uv[:, k, :], writes=[Wu_r])
        for k in range(0, NFF, 2):
            P.dma("pool", Wdn[:, k:k + 2, :], dv[:, k:k + 2, :], writes=[Wdn_r])
    if li + 1 < c.depth:
        precast_weights(c, li + 1)
    P.dma("sp", c.gbc[:], c.W["ffn_norm_g"][li:li + 1, :].partition_broadcast(128), writes=[c.gbc_r])
    gfin, gfin_r = None, Res()
    if last:
        gfin = sb.alloc([128, D], F32, "gfin")
        P.dma("sp", gfin[:], c.W["final_norm_g"][None, :].partition_broadcast(128), writes=[gfin_r])
    NXN, NXR = 2, 2
    xns = [sb.alloc([128, D], F32, "xn") for _ in range(NXN)]
    xns_r = _rr(NXN)
    xrs = [sb.alloc([128, D], F32, "xr") for _ in range(NXR)]
    xrs_r = _rr(NXR)
    hts = [sb.alloc([128, D], BF16, "hf") for _ in range(4)]
    hts_r = _rr(4)
    h2T = sb.alloc([128, 8, 512], BF16, "h2T")
    h2T_r = _rr(4)
    aT = sb.alloc([128, NFF, 512], BF16, "aT")
    aT_r = _rr(NFF)
    sgs = [sb.alloc([128, 512], F32, "sg") for _ in range(2)]
    sgs_r = _rr(2)
    junk = sb.alloc([128, D], BF16, "junkf")
    junk_r = Res()
    nblk = T // 512
    ss = sb.alloc([128, 2 * nblk * 4], F32, "ssf")
    rstd = sb.alloc([128, 2 * nblk * 4], F32, "rstdf")
    cnt = {"xn": 0, "xr": 0}

    def norm_chain(xt, xr, sc, gain, gain_r, out_ap, out_r):
        ssr, rsr = Res(), Res()
        P.op("act", lambda e: e.activation(out=junk[:], in_=xt[:], func=AF.Square, accum_out=ss[:, sc:sc + 1]),
             reads=[xr], writes=[junk_r, ssr])
        P.op("act", lambda e: e.activation(out=rstd[:, sc:sc + 1], in_=ss[:, sc:sc + 1], func=AF.Sqrt,
                                           bias=c.epst[:, 0:1], scale=1.0 / D), reads=[ssr, c.epst_r], writes=[rsr])
        P.op("dve", lambda e: e.reciprocal(out=rstd[:, sc:sc + 1], in_=rstd[:, sc:sc + 1]), reads=[rsr], writes=[rsr])
        P.op("dve", lambda e: e.scalar_tensor_tensor(out=out_ap, in0=xt[:], scalar=rstd[:, sc:sc + 1], in1=gain[:],
                                                     op0=ALU.mult, op1=ALU.mult), reads=[xr, rsr, gain_r],
             writes=[out_r])

    def pro_norm(b):
        for j in range(4):
            xt, xr = xns[cnt["xn"] % NXN], xns_r[cnt["xn"] % NXN]
            cnt["xn"] += 1
            r0 = b * 512 + j * 128
            P.dma("sp", xt[:], c.Xs[r0:r0 + 128, :], writes=[xr])
            norm_chain(xt, xr, b * 4 + j, c.gbc, c.gbc_r, hts[j][:], hts_r[j])

    def pro_T(b):
        for j in range(4):
            transpose8(c, hts[j], hts_r[j], h2T[:, :, j * 128:(j + 1) * 128], h2T_r[j], bank=0, half=j % 2,
                       evac="dve")

    def gateup(b):
        for cc in range(NFF):
            bg, bu = (1, 2) if cc % 2 == 0 else (3, 1)
            hg, hu = (0, 0) if cc % 2 == 0 else (0, 1)
            if cc % 2 == 1:
                bu, hu = 2, 1
            psg, psu = c.pd[bg][:, hg * 512:(hg + 1) * 512], c.pd[bu][:, hu * 512:(hu + 1) * 512]
            prg, pru = c.pdr[bg][hg], c.pdr[bu][hu]
            sg, sgr = sgs[cc % 2], sgs_r[cc % 2]

            def fg(e, cc=cc, ps=psg, Wx=Wg):
                ins = None
                for k in range(8):
                    ins = e.matmul(ps, lhsT=Wx[:, k, cc * 128:(cc + 1) * 128], rhs=h2T[:, k, :],
                                   start=(k == 0), stop=(k == 7))
                return ins

            def fu(e, cc=cc, ps=psu, Wx=Wu):
                ins = None
                for k in range(8):
                    ins = e.matmul(ps, lhsT=Wx[:, k, cc * 128:(cc + 1) * 128], rhs=h2T[:, k, :],
                                   start=(k == 0), stop=(k == 7))
                return ins
            P.op("pe", fg, reads=h2T_r + [Wg_r], writes=[prg])
            P.op("pe", fu, reads=h2T_r + [Wu_r], writes=[pru])
            P.op("act", lambda e, sg=sg, ps=psg: e.activation(out=sg[:], in_=ps, func=AF.Silu),
                 reads=[prg], writes=[sgr])
            P.op("dve", lambda e, sg=sg, ps=psu, cc=cc: e.tensor_tensor(out=aT[:, cc, :], in0=ps, in1=sg[:],
                                                                        op=ALU.mult),
                 reads=[pru, sgr], writes=[aT_r[cc]])

    def down(b):
        for j in range(4):
            xt, xr = xrs[cnt["xr"] % NXR], xrs_r[cnt["xr"] % NXR]
            cnt["xr"] += 1
            r0 = b * 512 + j * 128
            P.dma("sp", xt[:], c.Xs[r0:r0 + 128, :], writes=[xr])
            if do_ffn:
                for half in range(2):
                    bank, hs = ((1, 0), (2, 0)) [half] if j % 2 == 0 else ((3, 0), (3, 1))[half]
                    ps = c.pd[bank][:, hs * 512:(hs + 1) * 512]
                    pr = c.pdr[bank][hs]

                    def fd(e, j=j, half=half, ps=ps):
                        ins = None
                        for cc in range(NFF):
                            ins = e.matmul(ps, lhsT=aT[:, cc, j * 128:(j + 1) * 128],
                                           rhs=Wdn[:, cc, half * 512:(half + 1) * 512],
                                           start=(cc == 0), stop=(cc == NFF - 1))
                        return ins
                    P.op("pe", fd, reads=aT_r + [Wdn_r], writes=[pr])
                    P.op("dve", lambda e, xt=xt, half=half, ps=ps: e.tensor_tensor(
                        out=xt[:, half * 512:(half + 1) * 512], in0=ps, in1=xt[:, half * 512:(half + 1) * 512],
                        op=ALU.add), reads=[pr, xr], writes=[xr])
            if last:
                norm_chain(xt, xr, nblk * 4 + b * 4 + j, gfin, gfin_r, xt[:], xr)
                P.dma("sp", c.y[r0:r0 + 128, :], xt[:], reads=[xr])
            else:
                P.dma("sp", c.Xs[r0:r0 + 128, :], xt[:], reads=[xr])

    if do_ffn:
        pro_norm(0)
        pro_T(0)
    for b in range(nblk):
        if do_ffn:
            gateup(b)
            if b + 1 < nblk:
                pro_norm(b + 1)
        down(b)
        if do_ffn and b + 1 < nblk:
            pro_T(b + 1)


def evac(c, i, out, in_, reads, writes):
    if i % 2 == 0:
        c.P.op("act", lambda e: e.copy(out=out, in_=in_), reads=reads, writes=writes)
    else:
        c.P.op("dve", lambda e: e.tensor_copy(out=out, in_=in_), reads=reads, writes=writes)


def mixer_fn(c, li, t0, L, hT, hT_r):
    P, sb = c.P, c.sb
    nt, nb = L // 128, L // 512
    win = c.winb[li].rearrange("(k p) c -> p k c", p=128)
    Wi = sb.alloc([128, 8, 256], BF16, "fnWi")
    Wi_r = Res()
    P.dma("sp", Wi[:], win[:, :, OFF_FN:OFF_FN + 256], reads=[c.winb_r[li]], writes=[Wi_r])
    CS = sb.alloc([128, 2, 512], BF16, "fnCS")
    CS_r = Res()
    P.dma("sp", CS[:], c.C["cs64"].rearrange("(c p) n -> p c n", p=128), writes=[CS_r])
    Wf = sb.alloc([128, 2, 256], BF16, "fnWf")
    Wf_r = Res()
    P.dma("pool", Wf[:], c.W["w_fourier"][li].rearrange("(c p) n -> p c n", p=128), writes=[Wf_r])
    WT = sb.alloc([128, 2, D], BF16, "fnWT")
    WT_r = Res()
    for cc in range(2):
        pt = c.pd[0][:, 0:512].bitcast(BF16).rearrange("p (k t) -> p k t", k=8)
        pr = c.pdr[0][0]

        def f(e, cc=cc, pt=pt):
            ins = None
            for k in range(8):
                ins = e.transpose(out=pt[:, k, :], in_=Wi[:, k, cc * 128:(cc + 1) * 128], identity=c.ident[:])
            return ins
        P.op("pe", f, reads=[Wi_r, c.ident_r], writes=[pr])
        P.op("act", lambda e, cc=cc, pt=pt: e.copy(out=WT[:, cc, :].rearrange("p (k t) -> p k t", k=8), in_=pt),
             reads=[pr], writes=[WT_r])
    Wcs = sb.alloc([128, 8, 512], BF16, "fnWcs")
    Wcs_r = Res()
    for k in range(8):
        bank = 1 + k % 2
        ps, pr = c.pd[bank][:, 0:512], c.pdr[bank][0]

        def f(e, k=k, ps=ps):
            ins = None
            for cc in range(2):
                ins = e.matmul(ps, lhsT=WT[:, cc, k * 128:(k + 1) * 128], rhs=CS[:, cc, :], start=(cc == 0),
                               stop=(cc == 1))
            return ins
        P.op("pe", f, reads=[WT_r, CS_r], writes=[pr])
        evac(c, k, Wcs[:, k, :], ps, [pr], [Wcs_r])
    Acs = sb.alloc([128, nt, 512], BF16, "fnA")
    Acs_r = Res()
    for j in range(nt):
        bank = 1 + j % 3
        ps, pr = c.pd[bank][:, 0:512], c.pdr[bank][0]

        def f(e, j=j, ps=ps):
            ins = None
            for k in range(8):
                ins = e.matmul(ps, lhsT=hT[:, k, j * 128:(j + 1) * 128], rhs=Wcs[:, k, :], start=(k == 0),
                               stop=(k == 7))
            return ins
        P.op("pe", f, reads=[hT_r[j], Wcs_r], writes=[pr])
        evac(c, j, Acs[:, j, :], ps, [pr], [Acs_r])
    G = min(8, nt)
    ng = nt // G
    stripes = [(sb.alloc([128, G, 512], BF16, "fnSc"), sb.alloc([128, G, 512], BF16, "fnSs"), Res()) for _ in range(3)]
    cv = c.C[f"dftc{L}"].rearrange("(a p) n -> p a n", p=128)
    sv = c.C[f"dfts{L}"].rearrange("(a p) n -> p a n", p=128)
    FT = sb.alloc([128, 2, L], BF16, "fnFT")
    FT_r = _rr(nb)
    alt = sb.alloc([128, 1], BF16, "fnalt")
    alt_r = Res()
    P.dma("sp", alt[:], c.C[f"alt{L}"], writes=[alt_r])
    for cc in range(2):
        ps, pr = c.pd[0][:, cc * 512:cc * 512 + 1], c.pdr[0][cc]

        def fh(e, cc=cc, ps=ps):
            ins = None
            for ja in range(nt):
                ins = e.matmul(ps, lhsT=Acs[:, ja, cc * 128:(cc + 1) * 128], rhs=alt[:], start=(ja == 0),
                               stop=(ja == nt - 1))
            return ins
        P.op("pe", fh, reads=[Acs_r, alt_r], writes=[pr])
        P.op("dve", lambda e, cc=cc, ps=ps: e.tensor_copy(out=FT[:, cc, L // 2:L // 2 + 1], in_=ps), reads=[pr],
             writes=[FT_r[(L // 2) // 512]])
    tmps = [(sb.alloc([128, 512], F32, "fntmp"), Res()) for _ in range(2)]
    si = 0
    ti = 0
    for b in range(nb // 2):
        pb = (0, 1) if b % 2 == 0 else (2, 3)
        Fc = [c.pd[pb[0]][:, 0:512], c.pd[pb[1]][:, 0:512]]
        Fs = [c.pd[pb[0]][:, 512:1024], c.pd[pb[1]][:, 512:1024]]
        Fc_r = [c.pdr[pb[0]][0], c.pdr[pb[1]][0]]
        Fs_r = [c.pdr[pb[0]][1], c.pdr[pb[1]][1]]
        for g in range(ng):
            Sc, Ss, Sr = stripes[si % 3]
            si += 1
            P.dma("sp", Sc[:], cv[:, g * G:(g + 1) * G, b * 512:(b + 1) * 512], writes=[Sr])
            P.dma("sp", Ss[:], sv[:, g * G:(g + 1) * G, b * 512:(b + 1) * 512], writes=[Sr])

            def f(e, g=g, Sc=Sc, Ss=Ss, Fc=Fc, Fs=Fs):
                ins = None
                for a in range(G):
                    ja = g * G + a
                    for cc in range(2):
                        e.matmul(Fc[cc], lhsT=Acs[:, ja, cc * 128:(cc + 1) * 128], rhs=Sc[:, a, :],
                                 start=(ja == 0), stop=(ja == nt - 1))
                        ins = e.matmul(Fs[cc], lhsT=Acs[:, ja, 256 + cc * 128:256 + (cc + 1) * 128], rhs=Ss[:, a, :],
                                       start=(ja == 0), stop=(ja == nt - 1))
                return ins
            P.op("pe", f, reads=[Acs_r, Sr], writes=Fc_r + Fs_r)
        hi = L - b * 512
        mb = (hi - 1) // 512
        for cc in range(2):
            tmp, tmp_r = tmps[ti % 2]
            ti += 1
            P.op("act", lambda e, cc=cc, tmp=tmp, Fs=Fs: e.copy(out=tmp[:], in_=Fs[cc]), reads=[Fs_r[cc]],
                 writes=[tmp_r])
            P.op("dve", lambda e, cc=cc, tmp=tmp, Fc=Fc, b=b: e.tensor_tensor(
                out=FT[:, cc, b * 512:(b + 1) * 512], in0=Fc[cc], in1=tmp[:], op=ALU.add),
                reads=[Fc_r[cc], tmp_r], writes=[FT_r[b]])
            i0 = 1 if b == 0 else 0
            n = 512 - i0
            dstv = FT[:, cc, hi - 511:hi - 511 + n]
            P.op("dve", lambda e, cc=cc, tmp=tmp, Fc=Fc, dstv=dstv, i0=i0: e.tensor_tensor(
                out=dstv[:, ::-1], in0=Fc[cc][:, i0:512], in1=tmp[:, i0:512], op=ALU.subtract),
                reads=[Fc_r[cc], tmp_r], writes=[FT_r[mb], FT_r[min(mb + 1, nb - 1)], FT_r[max(mb - 1, 0)]])
    obuf = [(sb.alloc([128, 4, 256], BF16, "fnO"), Res()) for _ in range(2)]
    for j in range(nt):
        ob, obr = obuf[(j // 4) % 2]
        bank = 1 + j % 3
        ps, pr = c.pd[bank][:, 0:256], c.pdr[bank][0]

        def f(e, j=j, ps=ps):
            ins = None
            for cc in range(2):
                ins = e.matmul(ps, lhsT=FT[:, cc, j * 128:(j + 1) * 128], rhs=Wf[:, cc, :], start=(cc == 0),
                               stop=(cc == 1))
            return ins
        P.op("pe", f, reads=[FT_r[j // 4], Wf_r], writes=[pr])
        evac(c, j, ob[:, j % 4, :], ps, [pr], [obr])
        if j % 4 == 3:
            b = j // 4
            dst = c.Os[t0 + b * 512:t0 + (b + 1) * 512, 0:256].rearrange("(j p) c -> p j c", p=128)
            P.dma("sp", dst, ob[:], reads=[obr])


def mixer_na(c, li, t0, L, hT, hT_r):
    P, sb, nc = c.P, c.sb, c.nc
    nt, nb, rows = L // 128, L // 512, L // 64
    win = c.winb[li].rearrange("(k p) c -> p k c", p=128)
    Wn = sb.alloc([128, 8, 768], BF16, "naW")
    Wn_r = Res()
    for k in range(0, 8, 4):
        P.dma("sp", Wn[:, k:k + 4, :], win[:, k:k + 4, OFF_NA:OFF_NA + 768], reads=[c.winb_r[li]], writes=[Wn_r])
    dlo = sb.alloc([128, 128], BF16, "dlo")
    dhi = sb.alloc([128, 128], BF16, "dhi")
    negt = sb.alloc([128, 128], BF16, "negt")
    wm = sb.alloc([128, 64], F32, "nawm")
    cst_r = Res()
    P.dma("sp", dlo[:], c.C["dlo"], writes=[cst_r])
    P.dma("sp", dhi[:], c.C["dhi"], writes=[cst_r])
    P.dma("sp", negt[:], c.C["negt"], writes=[cst_r])
    P.dma("sp", wm[:], c.C["na_wm"], writes=[cst_r])
    NP = 128 + 1860 + 128
    zt = sb.alloc([1, NP], F32, "naz")
    zt_r, rp_r = Res(), Res()
    P.op("pool", lambda e: e.memset(zt[:], 0.0), writes=[zt_r])
    P.dma("sp", c.rpad.ap()[None, :], zt[:], reads=[zt_r], writes=[rp_r])
    P.dma("sp", c.rpad.ap()[128:128 + 1860], c.W["na_rpb"][li].rearrange("h r d -> (h r d)"), writes=[rp_r])
    TMr = sb.alloc([128, 4, 16, 64], F32, "naTMr")
    TMr_r = Res()
    off_lo = 128 - 48 - 31
    for h in range(4):
        P.dma("sp", TMr[0:64, h], bass.AP(c.rpad, off_lo + 465 * h, [[1, 64], [31, 16], [1, 64]]), reads=[rp_r],
              writes=[TMr_r])
        P.dma("sp", TMr[64:128, h], bass.AP(c.rpad, off_lo + 31 + 465 * h, [[1, 64], [31, 16], [1, 64]]),
              reads=[rp_r], writes=[TMr_r])
    TMv = TMr[:].rearrange("p h r q -> p (h r) q")
    P.op("dve", lambda e: e.tensor_tensor(out=TMv, in0=TMv, in1=wm[:].unsqueeze(1).broadcast_to([128, 64, 64]),
                                          op=ALU.add), reads=[TMr_r, cst_r], writes=[TMr_r])
    TM2 = sb.alloc([128, 4, 16, 64], BF16, "naTM2")
    TM2_r = Res()
    P.op("dve", lambda e: e.tensor_scalar(out=TM2[:].rearrange("p h r q -> p (h r) q"), in0=TMv[:, :, ::-1],
                                          scalar1=8.0, scalar2=None, op0=ALU.mult), reads=[TMr_r], writes=[TM2_r])
    qT = sb.alloc([128, 2, L], BF16, "naq")
    kT = sb.alloc([128, 2, L], BF16, "nak")
    qk_r = _rr(nb)
    V = sb.alloc([128, nt, 4, 65], BF16, "nav")
    V_r = Res()
    P.op("pool", lambda e: e.memset(V[:, :, :, 64:65], 1.0), writes=[V_r])
    ei = 0
    for b in range(nb):
        for which, dstT in ((0, qT), (1, kT)):
            for cc in range(2):
                bank = ei % 4
                ps, pr = c.pd[bank][:, 0:512], c.pdr[bank][0]
                col = which * 256 + cc * 128

                def f(e, b=b, col=col, ps=ps):
                    ins = None
                    for k in range(8):
                        ins = e.matmul(ps, lhsT=Wn[:, k, col:col + 128], rhs=hT[:, k, b * 512:(b + 1) * 512],
                                       start=(k == 0), stop=(k == 7))
                    return ins
                P.op("pe", f, reads=hT_r[b * 4:b * 4 + 4] + [Wn_r], writes=[pr])
                evac(c, ei, dstT[:, cc, b * 512:(b + 1) * 512], ps, [pr], [qk_r[b]])
                ei += 1
    for j in range(nt):
        bank = ei % 4
        ps, pr = c.pd[bank][:, 0:256], c.pdr[bank][0]

        def f(e, j=j, ps=ps):
            ins = None
            for k in range(8):
                ins = e.matmul(ps, lhsT=hT[:, k, j * 128:(j + 1) * 128], rhs=Wn[:, k, 512:768], start=(k == 0),
                               stop=(k == 7))
            return ins
        P.op("pe", f, reads=[hT_r[j], Wn_r], writes=[pr])
        evac(c, ei, V[:, j, :, 0:64], ps.rearrange("p (h d) -> p h d", h=4), [pr], [V_r])
        ei += 1
    Pts = [(sb.alloc([128, 640], BF16, "naP"), Res()) for _ in range(2)]
    obuf = [(sb.alloc([128, 4, 256], BF16, "naO"), Res()) for _ in range(2)]
    rec = sb.alloc([128, 2, 4], F32, "narec")
    rec_r = Res()
    allqk = qk_r
    units = [(j, h) for j in range(nt) for h in range(4)]
    Ob = [c.pd[3][:, 0:260].rearrange("p (h d) -> p h d", h=4), c.pd[3][:, 512:772].rearrange("p (h d) -> p h d", h=4)]
    Ob_r = c.pdr[3]

    def geom(j):
        r0 = 2 * j
        sts = [min(max(r0 + jq - 4, 0), rows - 8) for jq in range(2)]
        Rs, Re = sts[0], sts[1] + 8
        return r0, sts, Rs, (Re - Rs + 1) // 2

    mask2 = sb.alloc([128, 2], F32, "nam2")
    P.dma("sp", mask2[:], c.C["mask2"], writes=[cst_r])
    qms = [(sb.alloc([128, 4, 128], BF16, "naqm"), Res()) for _ in range(2)]

    def expand(j2):
        qm2, qm2_r = qms[j2 % 2]
        for hh in range(4):
            P.op("dve", lambda e, hh=hh: e.tensor_scalar(
                out=qm2[:, hh, :], in0=qT[:, hh // 2, j2 * 128:(j2 + 1) * 128], scalar1=mask2[:, hh % 2:hh % 2 + 1],
                scalar2=None, op0=ALU.mult), reads=[qk_r[j2 // 4], cst_r], writes=[qm2_r])

    def n_s(i):
        j, h = units[i]
        r0, sts, Rs, nkt = geom(j)
        cc, pb = h // 2, (h % 2) * 64
        S, S_r = c.pd[1 + i % 2], c.pdr[1 + i % 2]
        qm, qm_r = qms[j % 2]
        if i == 0:
            expand(0)
        if h == 1 and j + 1 < nt:
            expand(j + 1)

        def fs(e):
            ins = None
            for t in range(nkt):
                kt = Rs // 2 + t
                blk = S[:, t * 128:(t + 1) * 128]
                e.matmul(blk, lhsT=kT[:, cc, kt * 128:(kt + 1) * 128], rhs=qm[:, h, :], start=True, stop=False,
                         skip_group_check=True)
                for jq in range(2):
                    rq = r0 + jq
                    R = Rs + 2 * t
                    v_lo = sts[jq] <= R < sts[jq] + 8
                    v_hi = sts[jq] <= R + 1 < sts[jq] + 8
                    dr = R - rq + 7
                    sub = S[:, t * 128 + jq * 64:t * 128 + (jq + 1) * 64]
                    if not v_lo and not v_hi:
                        ins = e.matmul(sub, lhsT=c.ident[:], rhs=negt[:, 0:64], start=False, stop=True,
                                       skip_group_check=True)
                        continue
                    assert -1 <= dr <= 14
                    ins = e.matmul(sub, lhsT=c.ident[:], rhs=TM2[:, h, dr + 1, :], start=False, stop=True,
                                   skip_group_check=True)
                    if not v_lo:
                        ins = e.matmul(sub, lhsT=dlo[:], rhs=negt[:, 0:64], start=False, stop=True,
                                       skip_group_check=True)
                    if not v_hi:
                        ins = e.matmul(sub, lhsT=dhi[:], rhs=negt[:, 0:64], start=False, stop=True,
                                       skip_group_check=True)
            return ins
        P.op("pe", fs, reads=allqk + [TM2_r, cst_r, c.ident_r, qm_r], writes=S_r)

    def n_e(i):
        j, h = units[i]
        nkt = geom(j)[3]
        S, S_r = c.pd[1 + i % 2], c.pdr[1 + i % 2]
        Pt, Pt_r = Pts[i % 2]
        P.op("act", lambda e: e.activation(out=Pt[:, 0:nkt * 128], in_=S[:, 0:nkt * 128], func=AF.Exp, scale=0.125),
             reads=S_r, writes=[Pt_r])

    def n_o(i):
        j, h = units[i]
        r0, sts, Rs, nkt = geom(j)
        Pt, Pt_r = Pts[i % 2]
        O, O_r = Ob[j % 2], Ob_r[j % 2]

        def fo(e):
            ins = None
            for t in range(nkt):
                kt = Rs // 2 + t
                ins = e.matmul(O[:, h, :], lhsT=Pt[:, t * 128:(t + 1) * 128], rhs=V[:, kt, h, :],
                               start=(t == 0), stop=(t == nkt - 1), skip_group_check=True)
            return ins
        P.op("pe", fo, reads=[Pt_r, V_r], writes=[O_r])
        if h == 3:
            ob, obr = obuf[(j // 4) % 2]
            P.op("dve", lambda e: e.reciprocal(out=rec[:, j % 2, :], in_=O[:, :, 64]), reads=[O_r], writes=[rec_r])
            P.op("dve", lambda e: e.tensor_tensor(
                out=ob[:, j % 4, :].rearrange("p (h d) -> p h d", h=4), in0=O[:, :, 0:64],
                in1=rec[:, j % 2, :].unsqueeze(2).broadcast_to([128, 4, 64]), op=ALU.mult), reads=[O_r, rec_r],
                writes=[obr])
            if j % 4 == 3:
                b = j // 4
                dst = c.Os[t0 + b * 512:t0 + (b + 1) * 512, 512:768].rearrange("(j p) c -> p j c", p=128)
                P.dma("sp", dst, ob[:], reads=[obr])
    pipeline(len(units), [n_s, n_e, n_o])


def mixer_da(c, li, t0, L, hT, hT_r):
    P, sb = c.P, c.sb
    nt, nb = L // 128, L // 512
    lam0 = 0.8 - 0.6 * math.exp(-0.3 * li)
    win = c.winb[li].rearrange("(k p) c -> p k c", p=128)
    Wd = sb.alloc([128, 8, 768], BF16, "daW")
    Wd_r = Res()
    for k in range(0, 8, 4):
        P.dma("sp", Wd[:, k:k + 4, :], win[:, k:k + 4, OFF_DA:OFF_DA + 768], reads=[c.winb_r[li]], writes=[Wd_r])
    Wsw = sb.alloc([128, 8, 512], BF16, "daWs")
    Wsw_r = Res()
    for k in range(8):
        src = Wd[:, k, 0:512].rearrange("p (g t i) -> p g t i", t=2, i=16)
        dst = Wsw[:, k, :].rearrange("p (g t i) -> p g t i", t=2, i=16)
        P.op("dve", lambda e, src=src, dst=dst: e.tensor_scalar(out=dst[:, :, 0, :], in0=src[:, :, 1, :], scalar1=-1.0,
                                                                scalar2=None, op0=ALU.mult),
             reads=[Wd_r], writes=[Wsw_r])
        P.op("dve", lambda e, src=src, dst=dst: e.tensor_copy(out=dst[:, :, 1, :], in_=src[:, :, 0, :]),
             reads=[Wd_r], writes=[Wsw_r])
    rc = sb.alloc([128, L], BF16, "darc")
    rs = sb.alloc([128, L], BF16, "dars")
    idf = sb.alloc([128, 128], F32, "daidf")
    gsub = sb.alloc([128, 64], F32, "dagsub")
    lvt = sb.alloc([128, 128], F32, "dalv")
    cst_r, lv_r = Res(), Res()
    P.dma("sp", rc[:], c.C["ropec"][:, 0:L], writes=[cst_r])
    P.dma("sp", rs[:], c.C["ropes"][:, 0:L], writes=[cst_r])
    P.dma("sp", idf[:], c.C["identf"], writes=[cst_r])
    P.dma("sp", gsub[:], c.W["diff_subln_g"][li:li + 1, :].partition_broadcast(128), writes=[cst_r])
    P.dma("sp", lvt[:], c.W["diff_lambda"][li:li + 1].rearrange("o a b -> o (a b)").partition_broadcast(128),
          writes=[lv_r])
    P.op("dve", lambda e: e.tensor_scalar(out=gsub[:], in0=gsub[:], scalar1=1.0 - lam0, scalar2=None, op0=ALU.mult),
         reads=[cst_r], writes=[cst_r])
    lp = sb.alloc([128, 2, 32], F32, "dalp")
    lsum = sb.alloc([128, 2], F32, "dals")
    lam = sb.alloc([128, 1], F32, "dalam")
    lv4 = lvt[:].rearrange("p (a t i) -> p a t i", t=2, i=32)
    P.op("dve", lambda e: e.tensor_tensor(out=lp[:], in0=lv4[:, :, 0, :], in1=lv4[:, :, 1, :], op=ALU.mult),
         reads=[lv_r], writes=[lv_r])
    P.op("dve", lambda e: e.reduce_sum(out=lsum[:], in_=lp[:], axis=AX.X), reads=[lv_r], writes=[lv_r])
    P.op("act", lambda e: e.activation(out=lsum[:], in_=lsum[:], func=AF.Exp), reads=[lv_r], writes=[lv_r])
    P.op("dve", lambda e: e.tensor_tensor(out=lam[:], in0=lsum[:, 0:1], in1=lsum[:, 1:2], op=ALU.subtract),
         reads=[lv_r], writes=[lv_r])
    P.op("dve", lambda e: e.tensor_scalar(out=lam[:], in0=lam[:], scalar1=lam0, scalar2=None, op0=ALU.add),
         reads=[lv_r], writes=[lv_r])
    qT = sb.alloc([128, 2, L], BF16, "daq")
    kT = sb.alloc([128, 2, L], BF16, "dak")
    mask4 = sb.alloc([128, 4], F32, "dam4")
    P.dma("sp", mask4[:], c.C["mask4"], writes=[cst_r])
    qms = [(sb.alloc([128, 8, 512], BF16, "daqm"), Res()) for _ in range(2)]
    qk_r = _rr(nb)
    Vf = sb.alloc([128, nt, 4 * 65 + 64], BF16, "dav")
    V = Vf[:, :, 0:260].rearrange("p t (h d) -> p t h d", h=4)
    V_r = Res()
    P.op("pool", lambda e: e.memset(Vf[:, :, 260:324], 0.0), writes=[V_r])
    P.op("pool", lambda e: e.memset(V[:, :, :, 64:65], 1.0), writes=[V_r])
    tmps = [(sb.alloc([128, 512], F32, "dat1"), sb.alloc([128, 512], F32, "dat2"), Res()) for _ in range(2)]
    ei = 0
    for b in range(nb):
        for which, dstT in ((0, qT), (1, kT)):
            for ch in range(2):
                M = 128
                col = which * 256 + ch * 128
                bank = ei % 2
                ps1, ps2 = c.pd[bank][0:M, 0:512], c.pd[bank][0:M, 512:1024]
                pr = c.pdr[bank]
                t1, t2, tr = tmps[ei % 2]
                ei += 1

                def f(e, b=b, col=col, M=M, ps1=ps1, ps2=ps2):
                    ins = None
                    for k in range(8):
                        e.matmul(ps1, lhsT=Wd[:, k, col:col + M], rhs=hT[:, k, b * 512:(b + 1) * 512],
                                 start=(k == 0), stop=(k == 7))
                    for k in range(8):
                        ins = e.matmul(ps2, lhsT=Wsw[:, k, col:col + M], rhs=hT[:, k, b * 512:(b + 1) * 512],
                                       start=(k == 0), stop=(k == 7))
                    return ins
                P.op("pe", f, reads=hT_r[b * 4:b * 4 + 4] + [Wd_r, Wsw_r], writes=pr)
                P.op("dve", lambda e, t1=t1, ps1=ps1, M=M, b=b: e.tensor_tensor(
                    out=t1[0:M, :], in0=ps1, in1=rc[0:M, b * 512:(b + 1) * 512], op=ALU.mult),
                    reads=[pr[0], cst_r], writes=[tr])
                P.op("dve", lambda e, t2=t2, ps2=ps2, M=M, b=b: e.tensor_tensor(
                    out=t2[0:M, :], in0=ps2, in1=rs[0:M, b * 512:(b + 1) * 512], op=ALU.mult),
                    reads=[pr[1], cst_r], writes=[tr])
                P.op("dve", lambda e, t1=t1, t2=t2, M=M, dstT=dstT, ch=ch, b=b: e.tensor_tensor(
                    out=dstT[0:M, ch, b * 512:(b + 1) * 512], in0=t1[0:M, :], in1=t2[0:M, :], op=ALU.add),
                    reads=[tr], writes=[qk_r[b]])
    for j in range(nt):
        bank = 2 + j % 2
        ps, pr = c.pd[bank][:, 0:256], c.pdr[bank][0]

        def f(e, j=j, ps=ps):
            ins = None
            for k in range(8):
                ins = e.matmul(ps, lhsT=hT[:, k, j * 128:(j + 1) * 128], rhs=Wd[:, k, 512:768], start=(k == 0),
                               stop=(k == 7))
            return ins
        P.op("pe", f, reads=[hT_r[j], Wd_r], writes=[pr])
        evac(c, j, V[:, j, :, 0:64], ps.rearrange("p (h d) -> p h d", h=4), [pr], [V_r])
    Pts = [(sb.alloc([128, 1024], BF16, "daP"), Res()) for _ in range(2)]
    OT = [sb.alloc([65, 512], F32, "daOT") for _ in range(2)]
    OT_r = _rr(2)
    obuf = [(sb.alloc([128, 4, 256], BF16, "daO"), Res()) for _ in range(2)]
    r12 = sb.alloc([128, 2, 4], F32, "dar")
    ab = sb.alloc([128, 2, 4, 64], F32, "daab")
    dd = sb.alloc([128, 4, 64], F32, "dad")
    sq = sb.alloc([128, 4, 64], F32, "dasq")
    ssq = sb.alloc([128, 4], F32, "dassq")
    w_r = Res()
    scale = 32 ** -0.5
    units = [(qb, h, kt) for qb in range(nb) for h in range(4) for kt in range(nt)]
    acc = [c.pd[2][0:65, 0:512], c.pd[3][0:65, 0:512]]
    accf = [c.pd[2][:, 0:512], c.pd[3][:, 0:512]]
    acc_r = [c.pdr[2][0], c.pdr[3][0]]
    tp = [c.pd[2][:, 512:772].rearrange("p (t d) -> p t d", t=4),
          c.pd[3][:, 512:772].rearrange("p (t d) -> p t d", t=4)]
    tp_r = [c.pdr[2][1], c.pdr[3][1]]
    fill = [c.pd[2][:, 772:1024], c.pd[3][:, 772:1024]]
    fill_r = Res()
    NFILL = DA_NFILL
    deferred = []

    def u_s(i):
        qb, h, kt = units[i]
        S, S_r = c.pd[i % 2], c.pdr[i % 2]

        qm, qm_r = qms[qb % 2]

        def expand(qb2):
            qm2, qm2_r = qms[qb2 % 2]
            for hb in range(8):
                P.op("dve", lambda e, hb=hb: e.tensor_scalar(
                    out=qm2[:, hb, :], in0=qT[:, hb // 4, qb2 * 512:(qb2 + 1) * 512],
                    scalar1=mask4[:, hb % 4:hb % 4 + 1], scalar2=None, op0=ALU.mult),
                    reads=[qk_r[qb2], cst_r], writes=[qm2_r])
        if i == 0:
            expand(0)
        if h == 1 and kt == 0 and qb + 1 < nb:
            expand(qb + 1)

        def fs(e):
            ins = None
            for br in range(2):
                hb = 2 * h + br
                ins = e.matmul(S[:, br * 512:(br + 1) * 512], lhsT=kT[:, hb // 4, kt * 128:(kt + 1) * 128],
                               rhs=qm[:, hb, :], start=True, stop=True)
            return ins
        P.op("pe", fs, reads=[qm_r, qk_r[kt // 4]], writes=S_r)
        if NFILL:
            def ff(e):
                ins = None
                for k in range(NFILL):
                    ins = e.matmul(fill[k % 2], lhsT=Vf[:, kt, 0:128], rhs=qT[:, 0, 0:252], start=True, stop=True,
                                   skip_group_check=True)
                return ins
            P.op("pe", ff, reads=[V_r, qk_r[0]], writes=[fill_r], inc=False)
        for dfn in [d for d in deferred if d[0] <= i]:
            deferred.remove(dfn)
            dfn[1]()

    def u_e(i):
        S, S_r = c.pd[i % 2], c.pdr[i % 2]
        Pt, Pt_r = Pts[i % 2]
        P.op("act", lambda e: e.activation(out=Pt[:], in_=S[:], func=AF.Exp, scale=scale), reads=S_r, writes=[Pt_r])

    def u_o(i):
        qb, h, kt = units[i]
        Pt, Pt_r = Pts[i % 2]

        def fo(e):
            ins = None
            for br in range(2):
                ins = e.matmul(accf[br], lhsT=Vf[:, kt, h * 65:h * 65 + 128], rhs=Pt[:, br * 512:(br + 1) * 512],
                               start=(kt == 0), stop=(kt == nt - 1))
            return ins
        P.op("pe", fo, reads=[Pt_r, V_r], writes=acc_r)
        if kt == nt - 1:
            epilogue(i, qb, h)

    def epilogue(i, qb, h):
        ob, obr = obuf[qb % 2]
        for br in range(2):
            P.op("dve", lambda e, br=br: e.tensor_copy(out=OT[br][:], in_=acc[br]), reads=[acc_r[br]],
                 writes=[OT_r[br]])

        def rest():
            def ft(e):
                ins = None
                for br in range(2):
                    for qt in range(4):
                        ins = e.transpose(out=tp[br][:, qt, :], in_=OT[br][0:65, qt * 128:(qt + 1) * 128],
                                          identity=idf[0:65, 0:65])
                return ins
            P.op("pe", ft, reads=OT_r + [cst_r], writes=tp_r)
            for br in range(2):
                P.op("dve", lambda e, br=br: e.reciprocal(out=r12[:, br, :], in_=tp[br][:, :, 64]),
                     reads=[tp_r[br]], writes=[w_r])
            P.op("dve", lambda e: e.tensor_scalar(out=r12[:, 1, :], in0=r12[:, 1, :], scalar1=lam[:, 0:1], scalar2=None,
                                                  op0=ALU.mult), reads=[w_r, lv_r], writes=[w_r])
            for br in range(2):
                P.op("dve", lambda e, br=br: e.tensor_tensor(
                    out=ab[:, br], in0=tp[br][:, :, 0:64], in1=r12[:, br, :].unsqueeze(2).broadcast_to([128, 4, 64]),
                    op=ALU.mult), reads=[tp_r[br], w_r], writes=[w_r])
            P.op("dve", lambda e: e.tensor_tensor(out=dd[:], in0=ab[:, 0], in1=ab[:, 1], op=ALU.subtract),
                 reads=[w_r], writes=[w_r])
            P.op("dve", lambda e: e.tensor_tensor(out=sq[:], in0=dd[:], in1=dd[:], op=ALU.mult),
                 reads=[w_r], writes=[w_r])
            P.op("dve", lambda e: e.reduce_sum(out=ssq[:], in_=sq[:], axis=AX.X), reads=[w_r], writes=[w_r])
            P.op("dve", lambda e: e.tensor_scalar(out=ssq[:], in0=ssq[:], scalar1=1.0 / 64, scalar2=EPS, op0=ALU.mult,
                                                  op1=ALU.add), reads=[w_r], writes=[w_r])
            P.op("pool", lambda e: e.tensor_tensor(out=ssq[:], in0=ssq[:], in1=c.mhalf[:, 0:4], op=ALU.pow),
                 reads=[w_r, c.epst_r], writes=[w_r])
            P.op("dve", lambda e: e.tensor_tensor(out=dd[:], in0=dd[:],
                                                  in1=ssq[:].unsqueeze(2).broadcast_to([128, 4, 64]), op=ALU.mult),
                 reads=[w_r], writes=[w_r])
            P.op("dve", lambda e: e.tensor_tensor(
                out=ob[:, :, h * 64:(h + 1) * 64], in0=dd[:], in1=gsub[:].unsqueeze(1).broadcast_to([128, 4, 64]),
                op=ALU.mult), reads=[w_r, cst_r], writes=[obr])
            if h == 3:
                dst = c.Os[t0 + qb * 512:t0 + (qb + 1) * 512, 256:512].rearrange("(j p) c -> p j c", p=128)
                P.dma("sp", dst, ob[:], reads=[obr])
        deferred.append((i + 4, rest))
    pipeline(len(units), [u_s, u_e, u_o])
    for dfn in list(deferred):
        dfn[1]()


def mixer_ssm(c, li, t0, L, hT, hT_r):
    P, sb, nc = c.P, c.sb, c.nc
    nt, nb = L // 128, L // 512
    win = c.winb[li].rearrange("(k p) c -> p k c", p=128)
    Ws = sb.alloc([128, 8, 1032], BF16, "smW")
    Ws_r = Res()
    for k in range(0, 8, 4):
        P.dma("sp", Ws[:, k:k + 4, :], win[:, k:k + 4, OFF_SSM:OFF_SSM + 1032], reads=[c.winb_r[li]], writes=[Ws_r])
    cst_r = Res()
    tri = [sb.alloc([128, 128], F32, "smtri") for _ in range(2)]
    msk = [sb.alloc([128, 512], BF16, "smmsk") for _ in range(2)]
    onesf = sb.alloc([128, 128], F32, "smones")
    for d, nm in enumerate(("f", "b")):
        P.dma("sp", tri[d][:], c.C["tri_" + nm], writes=[cst_r])
        P.dma("sp", msk[d][:], c.C["mask_" + nm], writes=[cst_r])
    P.dma("sp", onesf[:], c.C["onesf"], writes=[cst_r])
    cw = sb.alloc([128, 6, 5], F32, "smcw")
    cb = sb.alloc([128, 6], F32, "smcb")
    dtb = sb.alloc([128, 8], F32, "smdtb")
    alog = sb.alloc([128, 8], F32, "smalog")
    Dv = sb.alloc([128, 4], F32, "smD")
    ng = sb.alloc([128, 256], F32, "smng")
    for j in range(5):
        P.dma("sp", cw[:, :, j], c.W["ssm_conv_w"][li, j].rearrange("(c p) -> p c", p=128), writes=[cst_r], slow=True)
    P.dma("sp", cb[:], c.W["ssm_conv_b"][li].rearrange("(c p) -> p c", p=128), writes=[cst_r], slow=True)
    P.dma("sp", dtb[:], c.W["ssm_dt_bias"][li:li + 1].rearrange("o a b -> o (a b)").partition_broadcast(128),
          writes=[cst_r])
    P.dma("sp", alog[:], c.W["ssm_A_log"][li:li + 1].rearrange("o a b -> o (a b)").partition_broadcast(128),
          writes=[cst_r])
    P.dma("sp", Dv[:], c.W["ssm_D"][li:li + 1, :].partition_broadcast(128), writes=[cst_r])
    P.dma("sp", ng[:], c.W["ssm_norm_g"][li:li + 1, :].partition_broadcast(128), writes=[cst_r])
    P.op("act", lambda e: e.activation(out=alog[:], in_=alog[:], func=AF.Exp), reads=[cst_r], writes=[cst_r])
    P.op("dve", lambda e: e.tensor_scalar(out=alog[:], in0=alog[:], scalar1=-1.0, scalar2=None, op0=ALU.mult),
         reads=[cst_r], writes=[cst_r])
    xcT = sb.alloc([128, 6, L], BF16, "smxc")
    xc_r = Res()
    dtv = sb.alloc([128, nt, 8], F32, "smdt")
    adt = sb.alloc([128, nt, 8], F32, "smadt")
    dt_r = Res()
    yf = sb.alloc([128, nt, 256], BF16, "smyf")
    yf_r = _rr(nt)
    psd = c.pd[3][:, 512:512 + nt * 8].rearrange("p (t e) -> p t e", e=8)

    def fdt(e):
        ins = None
        for j in range(nt):
            for k in range(8):
                ins = e.matmul(psd[:, j, :], lhsT=hT[:, k, j * 128:(j + 1) * 128], rhs=Ws[:, k, 1024:1032],
                               start=(k == 0), stop=(k == 7), skip_group_check=True)
        return ins
    P.op("pe", fdt, reads=hT_r + [Ws_r], writes=[c.pdr[3][1]])
    P.op("dve", lambda e: e.tensor_tensor(out=dtv[:], in0=psd, in1=dtb[:].unsqueeze(1).broadcast_to([128, nt, 8]),
                                          op=ALU.add), reads=[c.pdr[3][1], cst_r], writes=[dt_r])
    P.op("act", lambda e: e.activation(out=dtv[:], in_=dtv[:], func=AF.Exp), reads=[dt_r], writes=[dt_r])
    P.op("act", lambda e: e.activation(out=dtv[:], in_=dtv[:], func=AF.Ln, bias=1.0), reads=[dt_r], writes=[dt_r])
    P.op("dve", lambda e: e.tensor_tensor(out=adt[:], in0=dtv[:], in1=alog[:].unsqueeze(1).broadcast_to([128, nt, 8]),
                                          op=ALU.mult), reads=[dt_r, cst_r], writes=[dt_r])
    szall = sb.alloc([128, nt, 256], BF16, "smsz")
    sz_r = Res()
    for j in range(nt):
        bank = j % 3
        ps, pr = c.pd[bank][:, 512:768], c.pdr[bank][1]

        def fz(e, j=j, ps=ps):
            ins = None
            for k in range(8):
                ins = e.matmul(ps, lhsT=hT[:, k, j * 128:(j + 1) * 128], rhs=Ws[:, k, 0:256], start=(k == 0),
                               stop=(k == 7))
            return ins
        P.op("pe", fz, reads=[hT_r[j], Ws_r], writes=[pr])
        P.op("act", lambda e, j=j, ps=ps: e.activation(out=szall[:, j, :], in_=ps, func=AF.Silu), reads=[pr],
             writes=[sz_r])
    ov = sb.off
    idf = sb.alloc([128, 128], F32, "smidf")
    idf_r = Res()
    P.dma("sp", idf[:], c.C["identf"], writes=[idf_r])
    Dg = sb.alloc([128, 6, 5, 128], BF16, "smDg")
    Dg_r = Res()
    for ch in range(6):
        for j in range(5):
            P.op("dve", lambda e, ch=ch, j=j: e.tensor_scalar(out=Dg[:, ch, j, :], in0=idf[:],
                                                               scalar1=cw[:, ch, j:j + 1], scalar2=None, op0=ALU.mult),
                 reads=[idf_r, cst_r], writes=[Dg_r])
    pres = [(sb.alloc([128, L + 4], BF16, "smpre"), Res()) for _ in range(2)]
    for pre, pre_r in pres:
        P.op("pool", lambda e, pre=pre: e.memset(pre[:, 0:2], 0.0), writes=[pre_r])
        P.op("pool", lambda e, pre=pre: e.memset(pre[:, L + 2:L + 4], 0.0), writes=[pre_r])
    for ch in range(6):
        pre, pre_r = pres[ch % 2]
        for b in range(nb):
            bank = b % 3
            ps, pr = c.pd[bank][:, 0:512], c.pdr[bank][0]

            def f(e, ch=ch, b=b, ps=ps):
                ins = None
                for k in range(8):
                    ins = e.matmul(ps, lhsT=Ws[:, k, 256 + ch * 128:256 + (ch + 1) * 128],
                                   rhs=hT[:, k, b * 512:(b + 1) * 512], start=(k == 0), stop=(k == 7))
                return ins
            P.op("pe", f, reads=hT_r[b * 4:b * 4 + 4] + [Ws_r], writes=[pr])
            evac(c, b, pre[:, 2 + b * 512:2 + (b + 1) * 512], ps, [pr], [pre_r])
        for b in range(nb):
            ps, pr = c.pd[3][:, (b % 2) * 512:(b % 2 + 1) * 512], c.pdr[3][b % 2]

            def fc(e, ch=ch, b=b, ps=ps, pre=pre):
                ins = None
                for j in range(5):
                    ins = e.matmul(ps, lhsT=Dg[:, ch, j, :], rhs=pre[:, b * 512 + j:b * 512 + j + 512],
                                   start=(j == 0), stop=(j == 4))
                return ins
            P.op("pe", fc, reads=[pre_r, Dg_r], writes=[pr])
            P.op("act", lambda e, ch=ch, b=b, ps=ps: e.activation(out=xcT[:, ch, b * 512:(b + 1) * 512], in_=ps,
                                                                   func=AF.Silu, bias=cb[:, ch:ch + 1]),
                 reads=[pr, cst_r], writes=[xc_r])
    P.barrier()
    sb.off = ov
    hst = sb.alloc([128, 2, 256], F32, "smh")
    hbf = sb.alloc([128, 2, 256], BF16, "smhb")
    h_r = [Res(), Res()]
    obuf = [(sb.alloc([128, 4, 256], BF16, "smO"), Res()) for _ in range(2)]
    ocnt = [0] * (nt // 4)
    for d in range(2):
        P.op("pool", lambda e, d=d: e.memset(hst[:, d, :], 0.0), writes=[h_r[d]])
        P.op("pool", lambda e, d=d: e.memset(hbf[:, d, :], 0.0), writes=[h_r[d]])

    class B_:
        pass
    Bs = []
    for d in range(2):
        b_ = B_()
        b_.xb = sb.alloc([128, 4, 128], BF16, "smxb")
        b_.xtm = b_.xb[:, 0:2, :].rearrange("p k t -> p (k t)")
        b_.btm = b_.xb[:, 2:4, :].rearrange("p k t -> p (k t)")
        b_.rhsA = sb.alloc([128, 4, 128], F32, "smrA")
        b_.dec = sb.alloc([128, 4, 128], F32, "smdec")
        b_.MT = sb.alloc([128, 4, 128], BF16, "smMT")
        b_.sm4 = sb.alloc([128, 6, 4], F32, "sm4")
        b_.xdt = sb.alloc([128, 4, 64], BF16, "smxdt")
        b_.xdd = sb.alloc([128, 4, 64], BF16, "smxdd")
        b_.yt = sb.alloc([128, 4, 64], F32, "smyt")
        b_.y2 = sb.alloc([128, 256], F32, "smy2")
        b_.sq = sb.alloc([128, 256], F32, "smsq")
        b_.s2 = sb.alloc([128, 2], F32, "sms2")
        b_.tm_r, b_.w_r, b_.a_r, b_.m_r, b_.x_r, b_.c_r = Res(), Res(), Res(), Res(), Res(), Res()
        P0, P1 = c.pd[2 * d], c.pd[2 * d + 1]
        b_.accBf = P0[:, 0:512]
        b_.accB = P0[:, 0:512].rearrange("p (h l) -> p h l", h=4)
        b_.accB_r = c.pdr[2 * d][0]
        b_.cbT = P0[:, 512:768].rearrange("p (g l) -> p g l", g=2)
        b_.tpp = P0[:, 768:1024].bitcast(BF16).rearrange("p (k t) -> p k t", k=4)
        b_.p0b_r = c.pdr[2 * d][1]
        b_.yd = P1[:, 0:256].rearrange("p (h d) -> p h d", h=4)
        b_.yo = P1[:, 256:512].rearrange("p (h d) -> p h d", h=4)
        b_.y_r = c.pdr[2 * d + 1][0]
        b_.stp = P1[:, 512:768].rearrange("p (h d) -> p h d", h=4)
        b_.acol = P1[:, 768:772]
        b_.st_r = c.pdr[2 * d + 1][1]
        Bs.append(b_)

    def chunk(d, i):
        b_ = Bs[d]
        cidx = i if d == 0 else nt - 1 - i
        first = i < nt // 2
        last = 127 if d == 0 else 0
        cs = slice(cidx * 128, (cidx + 1) * 128)
        xtm, btm, rhsA, dec, MT, sm4, xdt, xdd, yt, y2, sq, s2 = (b_.xtm, b_.btm, b_.rhsA, b_.dec, b_.MT, b_.sm4, b_.xdt,
                                                                  b_.xdd, b_.yt, b_.y2, b_.sq, b_.s2)

        def ftp(e):
            ins = None
            for k in range(4):
                ins = e.transpose(out=b_.tpp[:, k, :], in_=xcT[:, k, cs], identity=c.ident[:])
            return ins
        P.op("pe", ftp, reads=[xc_r, c.ident_r], writes=[b_.p0b_r])
        P.op("act", lambda e: e.copy(out=b_.xb[:], in_=b_.tpp), reads=[b_.p0b_r], writes=[b_.tm_r])
        av = adt[:, cidx, d * 4:(d + 1) * 4]
        P.op("pool", lambda e: e.tensor_tensor(
            out=rhsA[:], in0=tri[d][:].unsqueeze(1).broadcast_to([128, 4, 128]),
            in1=av.unsqueeze(2).broadcast_to([128, 4, 128]), op=ALU.mult), reads=[dt_r, cst_r], writes=[b_.a_r])

        def facc(e):
            e.matmul(b_.accBf, lhsT=onesf[:], rhs=rhsA[:].rearrange("p h l -> p (h l)"), start=True,
                     stop=False, skip_group_check=True)
            e.matmul(b_.accBf, lhsT=c.ident[:], rhs=msk[d][:], start=False, stop=True, skip_group_check=True)
            return e.matmul(b_.acol, lhsT=tri[d][:], rhs=av, start=True, stop=True, skip_group_check=True)
        P.op("pe", facc, reads=[b_.a_r, cst_r, dt_r, c.ident_r], writes=[b_.accB_r, b_.st_r])

        def fcb(e):
            ins = None
            for g in range(2):
                ins = e.matmul(b_.cbT[:, g, :], lhsT=xcT[:, 2 + g, cs], rhs=xcT[:, 4 + g, cs], start=True, stop=True,
                               skip_group_check=True)
            return ins
        P.op("pe", fcb, reads=[xc_r], writes=[b_.p0b_r])
        P.op("dve", lambda e: e.tensor_scalar(out=sm4[:, 0, :], in0=b_.acol, scalar1=-1.0, scalar2=None, op0=ALU.mult),
             reads=[b_.st_r], writes=[b_.w_r])
        P.op("act", lambda e: e.activation(out=sm4[:, 1, :], in_=b_.acol, func=AF.Exp), reads=[b_.st_r],
             writes=[b_.w_r])
        P.op("dve", lambda e: e.tensor_tensor(out=sm4[:, 4, :], in0=b_.accB[:, :, last], in1=sm4[:, 0, :], op=ALU.add),
             reads=[b_.accB_r, b_.w_r], writes=[b_.w_r])
        P.op("act", lambda e: e.activation(out=sm4[:, 2, :], in_=sm4[:, 4, :], func=AF.Exp), reads=[b_.w_r],
             writes=[b_.w_r])
        P.op("act", lambda e: e.activation(out=sm4[:, 3, :], in_=b_.accB[:, :, last], func=AF.Exp),
             reads=[b_.accB_r], writes=[b_.w_r])
        P.op("dve", lambda e: e.tensor_tensor(out=dec[:], in0=b_.accB,
                                              in1=sm4[:, 0, :].unsqueeze(2).broadcast_to([128, 4, 128]), op=ALU.add),
             reads=[b_.accB_r, b_.w_r], writes=[b_.m_r])
        P.op("act", lambda e: e.activation(out=dec[:], in_=dec[:], func=AF.Exp), reads=[b_.m_r], writes=[b_.m_r])
        P.op("dve", lambda e: e.tensor_tensor(
            out=MT[:].rearrange("p (g k) l -> p g k l", g=2),
            in0=b_.cbT.unsqueeze(2).broadcast_to([128, 2, 2, 128]),
            in1=dec[:].rearrange("p (g k) l -> p g k l", g=2), op=ALU.mult), reads=[b_.p0b_r, b_.m_r],
            writes=[b_.m_r])
        dv = dtv[:, cidx, d * 4:(d + 1) * 4]
        P.op("pool", lambda e: e.tensor_tensor(
            out=xdt[:], in0=xtm.rearrange("p (h d) -> p h d", h=4),
            in1=dv.unsqueeze(2).broadcast_to([128, 4, 64]), op=ALU.mult), reads=[b_.tm_r, dt_r], writes=[b_.x_r])
        P.op("pool", lambda e: e.tensor_tensor(
            out=xdd[:], in0=xdt[:], in1=sm4[:, 2, :].unsqueeze(2).broadcast_to([128, 4, 64]), op=ALU.mult),
            reads=[b_.x_r, b_.w_r], writes=[b_.x_r])

        def fy(e):
            ins = None
            for h in range(4):
                ins = e.matmul(b_.yd[:, h, :], lhsT=MT[:, h, :], rhs=xdt[:, h, :], start=True, stop=True,
                               skip_group_check=True)
            for g in range(2):
                ins = e.matmul(b_.yo[:, 2 * g:2 * g + 2, :], lhsT=xcT[:, 4 + g, cs],
                               rhs=hbf[:, d, g * 128:(g + 1) * 128], start=True, stop=True, skip_group_check=True)
            return ins
        P.op("pe", fy, reads=[b_.m_r, b_.x_r, xc_r, h_r[d]], writes=[b_.y_r])

        def fst(e):
            ins = None
            for g in range(2):
                ins = e.matmul(b_.stp[:, 2 * g:2 * g + 2, :], lhsT=btm[:, g * 128:(g + 1) * 128],
                               rhs=xdd[:, 2 * g:2 * g + 2, :], start=True, stop=True, skip_group_check=True)
            return ins
        P.op("pe", fst, reads=[b_.tm_r, b_.x_r], writes=[b_.st_r])
        hv = hst[:, d, :].rearrange("p (h e) -> p h e", h=4)
        P.op("dve", lambda e: e.tensor_tensor(out=hv, in0=hv, in1=sm4[:, 3, :].unsqueeze(2).broadcast_to([128, 4, 64]),
                                              op=ALU.mult), reads=[b_.w_r, h_r[d], b_.y_r], writes=[h_r[d]])
        P.op("dve", lambda e: e.tensor_tensor(out=hv, in0=b_.stp, in1=hv, op=ALU.add), reads=[b_.st_r, h_r[d]],
             writes=[h_r[d]])
        P.op("act", lambda e: e.copy(out=hbf[:, d, :], in_=hst[:, d, :]), reads=[h_r[d]], writes=[h_r[d]])
        P.op("dve", lambda e: e.tensor_tensor(out=yt[:], in0=b_.yo,
                                              in1=sm4[:, 1, :].unsqueeze(2).broadcast_to([128, 4, 64]), op=ALU.mult),
             reads=[b_.y_r, b_.w_r], writes=[b_.c_r])
        if first:
            P.op("dve", lambda e: e.tensor_tensor(
                out=yf[:, cidx, :].rearrange("p (h d) -> p h d", h=4), in0=b_.yd, in1=yt[:], op=ALU.add),
                reads=[b_.y_r, b_.c_r], writes=[yf_r[cidx]])
            return
        y2v = y2[:].rearrange("p (h d) -> p h d", h=4)
        P.op("dve", lambda e: e.tensor_tensor(out=y2v, in0=b_.yd, in1=yt[:], op=ALU.add), reads=[b_.y_r, b_.c_r],
             writes=[b_.c_r])
        P.op("dve", lambda e: e.tensor_tensor(out=y2[:], in0=y2[:], in1=yf[:, cidx, :], op=ALU.add),
             reads=[b_.c_r, yf_r[cidx]], writes=[b_.c_r])
        P.op("dve", lambda e: e.tensor_tensor(
            out=yt[:], in0=xtm.rearrange("p (h d) -> p h d", h=4),
            in1=Dv[:].unsqueeze(2).broadcast_to([128, 4, 64]), op=ALU.mult), reads=[b_.tm_r, cst_r, b_.c_r],
            writes=[b_.c_r])
        P.op("dve", lambda e: e.tensor_tensor(out=y2v, in0=y2v, in1=yt[:], op=ALU.add), reads=[b_.c_r],
             writes=[b_.c_r])
        P.op("dve", lambda e: e.tensor_tensor(out=y2[:], in0=y2[:], in1=szall[:, cidx, :], op=ALU.mult),
             reads=[b_.c_r, sz_r], writes=[b_.c_r])
        P.op("dve", lambda e: e.tensor_tensor(out=sq[:], in0=y2[:], in1=y2[:], op=ALU.mult), reads=[b_.c_r],
             writes=[b_.c_r])
        P.op("dve", lambda e: e.reduce_sum(out=s2[:], in_=sq[:].rearrange("p (g d) -> p g d", g=2), axis=AX.X),
             reads=[b_.c_r], writes=[b_.c_r])
        P.op("dve", lambda e: e.tensor_scalar(out=s2[:], in0=s2[:], scalar1=1.0 / 128, scalar2=EPS, op0=ALU.mult,
                                              op1=ALU.add), reads=[b_.c_r], writes=[b_.c_r])
        P.op("pool", lambda e: e.tensor_tensor(out=s2[:], in0=s2[:], in1=c.mhalf[:, 0:2], op=ALU.pow),
             reads=[b_.c_r, c.epst_r], writes=[b_.c_r])
        P.op("dve", lambda e: e.tensor_tensor(
            out=y2[:].rearrange("p (g d) -> p g d", g=2), in0=y2[:].rearrange("p (g d) -> p g d", g=2),
            in1=s2[:].unsqueeze(2).broadcast_to([128, 2, 128]), op=ALU.mult), reads=[b_.c_r], writes=[b_.c_r])
        bq = cidx // 4
        ob, obr = obuf[d]
        P.op("pool", lambda e: e.tensor_tensor(out=ob[:, cidx % 4, :], in0=y2[:], in1=ng[:], op=ALU.mult),
             reads=[b_.c_r, cst_r], writes=[obr])
        ocnt[bq] += 1
        if ocnt[bq] == 4:
            dst = c.Os[t0 + bq * 512:t0 + (bq + 1) * 512, 768:1024].rearrange("(j p) c -> p j c", p=128)
            P.dma("sp", dst, ob[:], reads=[obr])
    for i in range(nt):
        la = P.capture(lambda: chunk(0, i))
        lb = P.capture(lambda: chunk(1, i))
        P.replay([la, lb])


MIXERS = {"fn": mixer_fn, "na": mixer_na, "da": mixer_da, "ssm": mixer_ssm}


SEQ_LENS = (4096, 2048, 2048)
_CACHE = {}


def kernel(**inputs):
    xp = np.ascontiguousarray(inputs["x_prompt"], dtype=np.float32)
    xs = np.ascontiguousarray(inputs["x_sample"], dtype=np.float32)
    if "nc" not in _CACHE:
        _CACHE["nc"] = build(SEQ_LENS)
        _CACHE["consts"] = make_consts(SEQ_LENS)
    nc = _CACHE["nc"]
    consts = _CACHE["consts"]
    wts = {k: np.ascontiguousarray(inputs[k], dtype=np.float32) for k in WEIGHT_SHAPES}
    in_maps = []
    for i in range(8):
        xin = np.concatenate([xs[i], xp[2 * i], xp[2 * i + 1]], axis=0)
        m = {"xin": xin}
        m.update(wts)
        m.update(consts)
        in_maps.append(m)
    res = run_bass_kernel_spmd(nc, in_maps, core_ids=list(range(8)))
    yp = np.empty_like(xp)
    ys = np.empty_like(xs)
    for i in range(8):
        yy = res.results[i]["y"]
        ys[i] = yy[0:4096]
        yp[2 * i] = yy[4096:6144]
        yp[2 * i + 1] = yy[6144:8192]
    return (yp, ys)
```
